# Optimizing a Trainium2 kernel written in Bass

```python
import math
import jax
import jax.numpy as jnp
from jax import lax
import numpy as np

D_MODEL = 1024
BATCH = 4
SEQ = 8192
DEPTH = 2

BLOCK_Q = 128
EPS = 1e-6
NEG = -1e30
D_FF = 2816

MLA_HEADS = 8
MLA_Q_RANK = 256
MLA_KV_RANK = 128
MLA_NOPE = 64
MLA_ROPE = 32
MLA_V = 64
ROPE_THETA = 10000.0

SWA_HEADS = 8
SWA_KV_HEADS = 2
SWA_HD = 64
SWA_WINDOW = 128

NSA_HEADS = 8
NSA_KV_HEADS = 2
NSA_HD = 64
NSA_CMP_LEN = 32
NSA_CMP_STRIDE = 16
NSA_CMP_HIDDEN = 128
NSA_SEL_LEN = 64
NSA_TOPK = 16
NSA_WINDOW = 512
NSA_FORCE = 1e4

DIFF_HEADS = 4
DIFF_HD = 64

N_BUCKETS = 32
MAX_DIST = 128
N_BIAS_HEADS = SWA_HEADS + NSA_HEADS + DIFF_HEADS

N_BRANCH = 4
BRANCH_W = 512

COL_SIZES = (
    MLA_Q_RANK, MLA_KV_RANK, MLA_ROPE,
    SWA_HEADS * SWA_HD, SWA_KV_HEADS * SWA_HD, SWA_KV_HEADS * SWA_HD,
    NSA_HEADS * NSA_HD,
    NSA_KV_HEADS * NSA_HD, NSA_KV_HEADS * NSA_HD,
    NSA_KV_HEADS * NSA_HD, NSA_KV_HEADS * NSA_HD,
    NSA_KV_HEADS * NSA_HD, NSA_KV_HEADS * NSA_HD,
    3 * NSA_HEADS,
    DIFF_HEADS * 2 * DIFF_HD, DIFF_HEADS * 2 * DIFF_HD, DIFF_HEADS * 2 * DIFF_HD,
)
IN_COLS = sum(COL_SIZES)

kernel_name = 'hybrid_mla_swa_nsa_diff_block'


def rmsnorm(x, g):
    xf = x.astype(jnp.float32)
    y = xf * lax.rsqrt(jnp.mean(xf * xf, -1, keepdims=True) + EPS)
    return (y * g.astype(jnp.float32)).astype(x.dtype)


def swiglu(x, w_gate, w_up, w_down):
    return (jax.nn.silu(x @ w_gate) * (x @ w_up)) @ w_down


def rope(x, pos):
    half = x.shape[-1] // 2
    freqs = ROPE_THETA ** (-jnp.arange(half, dtype=jnp.float32) / half)
    ang = pos[:, None].astype(jnp.float32) * freqs
    cos, sin = jnp.cos(ang), jnp.sin(ang)
    xf = x.astype(jnp.float32)
    x1, x2 = xf[..., :half], xf[..., half:]
    return jnp.concatenate([x1 * cos - x2 * sin, x1 * sin + x2 * cos], -1).astype(x.dtype)


def t5_bucket(dist):
    n = jnp.maximum(dist, 0)
    max_exact = N_BUCKETS // 2
    nf = jnp.maximum(n, 1).astype(jnp.float32)
    large = max_exact + (jnp.log(nf / max_exact) / math.log(MAX_DIST / max_exact)
                         * (N_BUCKETS - max_exact)).astype(jnp.int32)
    large = jnp.minimum(large, N_BUCKETS - 1)
    return jnp.where(n < max_exact, n, large)


def rel_bias(table, q_pos, k_pos):
    b = t5_bucket(q_pos[:, None] - k_pos[None, :])
    return jnp.moveaxis(table[b], -1, 0).astype(jnp.float32)


def masked_softmax(s, mask):
    s = jnp.where(mask, s, NEG)
    m = jnp.max(s, -1, keepdims=True)
    e = jnp.where(mask, jnp.exp(s - m), 0.0)
    den = jnp.sum(e, -1, keepdims=True)
    return e / jnp.where(den > 0, den, 1.0)


def dense_causal_attn(q, k, v, scale):
    B, H, S, _ = q.shape
    dv = v.shape[-1]
    kpos = jnp.arange(S)

    def blk(i):
        start = i * BLOCK_Q
        qpos = start + jnp.arange(BLOCK_Q)
        qb = lax.dynamic_slice_in_dim(q, start, BLOCK_Q, axis=2)
        s = jnp.einsum('bhqd,bhkd->bhqk', qb, k).astype(jnp.float32) * scale
        s = jnp.where(kpos[None, :] <= qpos[:, None], s, NEG)
        p = jax.nn.softmax(s, -1)
        return jnp.einsum('bhqk,bhkd->bhqd', p.astype(v.dtype), v)

    o = lax.map(blk, jnp.arange(S // BLOCK_Q))
    return o.transpose(1, 0, 3, 2, 4).reshape(B, S, H * dv)


def banded_attn(q, k, v, window, table, sinks):
    B, Hkv, G, S, d = q.shape
    dv = v.shape[-1]
    nprev = -(-window // BLOCK_Q)
    pad = nprev * BLOCK_Q
    span = pad + BLOCK_Q
    kp = jnp.pad(k, ((0, 0), (0, 0), (pad, 0), (0, 0)))
    vp = jnp.pad(v, ((0, 0), (0, 0), (pad, 0), (0, 0)))
    scale = d ** -0.5

    def blk(i):
        start = i * BLOCK_Q
        qpos = start + jnp.arange(BLOCK_Q)
        kpos = start - pad + jnp.arange(span)
        qb = lax.dynamic_slice_in_dim(q, start, BLOCK_Q, axis=3)
        kb = lax.dynamic_slice_in_dim(kp, start, span, axis=2)
        vb = lax.dynamic_slice_in_dim(vp, start, span, axis=2)
        s = jnp.einsum('bhgqd,bhkd->bhgqk', qb, kb).astype(jnp.float32) * scale
        s = s + rel_bias(table, qpos, kpos).reshape(Hkv, G, BLOCK_Q, span)
        dist = qpos[:, None] - kpos[None, :]
        mask = (dist >= 0) & (dist < window) & (kpos[None, :] >= 0)
        s = jnp.where(mask, s, NEG)
        if sinks is None:
            p = jax.nn.softmax(s, -1)
        else:
            sk = sinks.astype(jnp.float32).reshape(Hkv, G, 1, 1)
            m = jnp.maximum(jnp.max(s, -1, keepdims=True), sk)
            e = jnp.exp(s - m)
            p = e / (jnp.sum(e, -1, keepdims=True) + jnp.exp(sk - m))
        return jnp.einsum('bhgqk,bhkd->bhgqd', p.astype(vb.dtype), vb)

    o = lax.map(blk, jnp.arange(S // BLOCK_Q))
    return jnp.moveaxis(o, 0, 3).reshape(B, Hkv, G, S, dv)


def mla_mixer(cq, ckv, k_rope, q_norm, kv_norm, w_uq, w_ukv, pos):
    B, S, _ = cq.shape
    H = MLA_HEADS
    q = (rmsnorm(cq, q_norm) @ w_uq).reshape(B, S, H, MLA_NOPE + MLA_ROPE).transpose(0, 2, 1, 3)
    q = jnp.concatenate([q[..., :MLA_NOPE], rope(q[..., MLA_NOPE:], pos)], -1)
    kv = (rmsnorm(ckv, kv_norm) @ w_ukv).reshape(B, S, H, MLA_NOPE + MLA_V).transpose(0, 2, 1, 3)
    k_r = jnp.broadcast_to(rope(k_rope[:, None], pos), (B, H, S, MLA_ROPE))
    k = jnp.concatenate([kv[..., :MLA_NOPE], k_r], -1)
    v = kv[..., MLA_NOPE:]
    return dense_causal_attn(q, k, v, (MLA_NOPE + MLA_ROPE) ** -0.5)


def swa_mixer(q, k, v, sinks, table):
    B, S, _ = q.shape
    G = SWA_HEADS // SWA_KV_HEADS
    q = q.reshape(B, S, SWA_KV_HEADS, G, SWA_HD).transpose(0, 2, 3, 1, 4)
    k = k.reshape(B, S, SWA_KV_HEADS, SWA_HD).transpose(0, 2, 1, 3)
    v = v.reshape(B, S, SWA_KV_HEADS, SWA_HD).transpose(0, 2, 1, 3)
    o = banded_attn(q, k, v, SWA_WINDOW, table, sinks)
    return o.transpose(0, 3, 1, 2, 4).reshape(B, S, SWA_HEADS * SWA_HD)


def nsa_compress(t, pos_emb, w1, w2):
    B, Hkv, S, d = t.shape
    nc = (S - NSA_CMP_LEN) // NSA_CMP_STRIDE + 1
    idx = np.arange(nc)[:, None] * NSA_CMP_STRIDE + np.arange(NSA_CMP_LEN)[None, :]
    blocks = jnp.take(t, jnp.asarray(idx, dtype=jnp.int32), axis=2) + pos_emb
    hdn = jax.nn.gelu(blocks.reshape(B, Hkv, nc, NSA_CMP_LEN * d) @ w1)
    return hdn @ w2


def nsa_mixer(q, kc, vc, ks, vs, kw, vw, gate_logits, cmp_pos, cmp_w1, cmp_w2, table):
    B, S, _ = q.shape
    Hkv, G, d = NSA_KV_HEADS, NSA_HEADS // NSA_KV_HEADS, NSA_HD
    SEL = NSA_SEL_LEN
    q = q.reshape(B, S, Hkv, G, d).transpose(0, 2, 3, 1, 4)

    def heads(t):
        return t.reshape(B, S, Hkv, d).transpose(0, 2, 1, 3)

    kc, vc, ks, vs, kw, vw = heads(kc), heads(vc), heads(ks), heads(vs), heads(kw), heads(vw)
    scale = d ** -0.5
    kcmp = nsa_compress(kc, cmp_pos[0], cmp_w1[0], cmp_w2[0])
    vcmp = nsa_compress(vc, cmp_pos[1], cmp_w1[1], cmp_w2[1])
    nc = kcmp.shape[2]
    c_start = np.arange(nc) * NSA_CMP_STRIDE
    c_last = jnp.asarray(c_start + NSA_CMP_LEN - 1, dtype=jnp.int32)
    nsel = S // SEL
    s_start = np.arange(nsel) * SEL
    overlap = jnp.asarray(((c_start[:, None] < s_start[None, :] + SEL)
                           & (s_start[None, :] < c_start[:, None] + NSA_CMP_LEN)).astype(np.float32))
    topk = min(NSA_TOPK, nsel)
    ksb = ks.reshape(B, Hkv, nsel, SEL, d)
    vsb = vs.reshape(B, Hkv, nsel, SEL, d)
    tab = table.reshape(N_BUCKETS, Hkv, G)
    gather = jax.vmap(jax.vmap(lambda blocks, ix: blocks[ix]))
    group_bias = jax.vmap(lambda bk, tb: tb[bk], in_axes=(1, 1), out_axes=1)
    sel_ids = jnp.arange(nsel)

    def blk(i):
        start = i * BLOCK_Q
        qpos = start + jnp.arange(BLOCK_Q)
        qb = lax.dynamic_slice_in_dim(q, start, BLOCK_Q, axis=3)
        s_c = jnp.einsum('bhgqd,bhcd->bhgqc', qb, kcmp).astype(jnp.float32) * scale
        p_c = masked_softmax(s_c, c_last[None, :] <= qpos[:, None])
        o_c = jnp.einsum('bhgqc,bhcd->bhgqd', p_c.astype(vcmp.dtype), vcmp)
        imp = jnp.einsum('bhgqc,cn->bhqn', p_c, overlap)
        cur = qpos // SEL
        causal = sel_ids[None, :] * SEL <= qpos[:, None]
        forced = ((sel_ids[None, :] == 0) | (sel_ids[None, :] == cur[:, None])
                  | (sel_ids[None, :] == cur[:, None] - 1))
        imp = jnp.where(causal, jnp.where(forced, NSA_FORCE, imp), -1.0)
        _, top = lax.top_k(imp, topk)
        k_sel = gather(ksb, top).reshape(B, Hkv, BLOCK_Q, topk * SEL, d)
        v_sel = gather(vsb, top).reshape(B, Hkv, BLOCK_Q, topk * SEL, d)
        kpos = (top[..., None] * SEL + jnp.arange(SEL)).reshape(B, Hkv, BLOCK_Q, topk * SEL)
        dist = qpos[:, None] - kpos
        bias = jnp.moveaxis(group_bias(t5_bucket(dist), tab), -1, 2).astype(jnp.float32)
        s_s = jnp.einsum('bhgqd,bhqkd->bhgqk', qb, k_sel).astype(jnp.float32) * scale + bias
        s_s = jnp.where((dist >= 0)[:, :, None], s_s, NEG)
        p_s = jax.nn.softmax(s_s, -1)
        o_s = jnp.einsum('bhgqk,bhqkd->bhgqd', p_s.astype(v_sel.dtype), v_sel)
        return o_c, o_s

    o_c, o_s = lax.map(blk, jnp.arange(S // BLOCK_Q))
    o_c = jnp.moveaxis(o_c, 0, 3).reshape(B, Hkv, G, S, d)
    o_s = jnp.moveaxis(o_s, 0, 3).reshape(B, Hkv, G, S, d)
    o_w = banded_attn(q, kw, vw, NSA_WINDOW, table, None)
    g = jax.nn.sigmoid(gate_logits.astype(jnp.float32)).reshape(B, S, Hkv, G, 3).transpose(0, 2, 3, 1, 4)
    o = g[..., 0:1] * o_c + g[..., 1:2] * o_s + g[..., 2:3] * o_w
    return o.astype(q.dtype).transpose(0, 3, 1, 2, 4).reshape(B, S, NSA_HEADS * d)


def diff_mixer(q, k, v, lam_params, subln, table, layer):
    B, S, _ = q.shape
    H, d = DIFF_HEADS, DIFF_HD
    q = q.reshape(B, S, H, 2, d).transpose(0, 2, 3, 1, 4)
    k = k.reshape(B, S, H, 2, d).transpose(0, 2, 3, 1, 4)
    v = v.reshape(B, S, H, 2 * d).transpose(0, 2, 1, 3)
    lam_init = 0.8 - 0.6 * math.exp(-0.3 * layer)
    lp = lam_params.astype(jnp.float32)
    lam = jnp.exp(jnp.sum(lp[0] * lp[1])) - jnp.exp(jnp.sum(lp[2] * lp[3])) + lam_init
    scale = d ** -0.5
    kpos = jnp.arange(S)

    def blk(i):
        start = i * BLOCK_Q
        qpos = start + jnp.arange(BLOCK_Q)
        qb = lax.dynamic_slice_in_dim(q, start, BLOCK_Q, axis=3)
        s = jnp.einsum('bhmqd,bhmkd->bhmqk', qb, k).astype(jnp.float32) * scale
        s = s + rel_bias(table, qpos, kpos)[:, None]
        s = jnp.where(kpos[None, :] <= qpos[:, None], s, NEG)
        p = jax.nn.softmax(s, -1)
        a = p[:, :, 0] - lam * p[:, :, 1]
        return jnp.einsum('bhqk,bhkd->bhqd', a.astype(v.dtype), v)

    o = lax.map(blk, jnp.arange(S // BLOCK_Q))
    o = jnp.moveaxis(o, 0, 2).reshape(B, H, S, 2 * d)
    o = rmsnorm(o, subln) * (1.0 - lam_init)
    return o.transpose(0, 2, 1, 3).reshape(B, S, H * 2 * d)


def setup_inputs(seed: int = 0) -> dict:
    key = jax.random.key(seed)
    k = jax.random.split(key, 22)
    f32 = jnp.float32

    def w(kk, shape, fan_in):
        return jax.random.normal(kk, shape, f32) * (fan_in ** -0.5)

    def gain(kk, shape):
        return 1.0 + 0.05 * jax.random.normal(kk, shape, f32)

    L, d = NSA_CMP_LEN, NSA_HD
    return {
        'x': jax.random.normal(k[0], (BATCH, SEQ, D_MODEL), f32),
        'norm_g': gain(k[1], (DEPTH, 3, D_MODEL)),
        'w_in': w(k[2], (DEPTH, D_MODEL, IN_COLS), D_MODEL),
        'mla_q_norm': gain(k[3], (DEPTH, MLA_Q_RANK)),
        'mla_kv_norm': gain(k[4], (DEPTH, MLA_KV_RANK)),
        'mla_w_uq': w(k[5], (DEPTH, MLA_Q_RANK, MLA_HEADS * (MLA_NOPE + MLA_ROPE)), MLA_Q_RANK),
        'mla_w_ukv': w(k[6], (DEPTH, MLA_KV_RANK, MLA_HEADS * (MLA_NOPE + MLA_V)), MLA_KV_RANK),
        'swa_sinks': 0.5 * jax.random.normal(k[7], (DEPTH, SWA_HEADS), f32),
        'nsa_cmp_pos': 0.1 * jax.random.normal(k[8], (DEPTH, 2, L, d), f32),
        'nsa_cmp_w1': w(k[9], (DEPTH, 2, L * d, NSA_CMP_HIDDEN), L * d),
        'nsa_cmp_w2': w(k[10], (DEPTH, 2, NSA_CMP_HIDDEN, d), NSA_CMP_HIDDEN),
        'diff_lambda': 0.1 * jax.random.normal(k[11], (DEPTH, 4, DIFF_HD), f32),
        'diff_subln': gain(k[12], (DEPTH, 2 * DIFF_HD)),
        'rel_bias_table': 0.2 * jax.random.normal(k[13], (N_BUCKETS, N_BIAS_HEADS), f32),
        'w_branch': w(k[14], (DEPTH, N_BRANCH, BRANCH_W, D_MODEL), BRANCH_W),
        'w_gate': w(k[15], (DEPTH, N_BRANCH, D_MODEL, D_MODEL), D_MODEL),
        'w_o': w(k[16], (DEPTH, D_MODEL, D_MODEL), D_MODEL),
        'ffn_w_gate': w(k[17], (DEPTH, 2, D_MODEL, D_FF), D_MODEL),
        'ffn_w_up': w(k[18], (DEPTH, 2, D_MODEL, D_FF), D_MODEL),
        'ffn_w_down': w(k[19], (DEPTH, 2, D_FF, D_MODEL), D_FF),
        'final_g': gain(k[20], (D_MODEL,)),
    }


def reference(x, norm_g, w_in, mla_q_norm, mla_kv_norm, mla_w_uq, mla_w_ukv, swa_sinks,
              nsa_cmp_pos, nsa_cmp_w1, nsa_cmp_w2, diff_lambda, diff_subln, rel_bias_table,
              w_branch, w_gate, w_o, ffn_w_gate, ffn_w_up, ffn_w_down, final_g):
    S = x.shape[1]
    pos = jnp.arange(S, dtype=jnp.int32)
    tab_swa = rel_bias_table[:, :SWA_HEADS]
    tab_nsa = rel_bias_table[:, SWA_HEADS:SWA_HEADS + NSA_HEADS]
    tab_diff = rel_bias_table[:, SWA_HEADS + NSA_HEADS:]
    offs = [int(o) for o in np.cumsum(COL_SIZES)[:-1]]
    h = x
    for l in range(DEPTH):
        h = h + 0.5 * swiglu(rmsnorm(h, norm_g[l, 0]), ffn_w_gate[l, 0], ffn_w_up[l, 0], ffn_w_down[l, 0])
        u = rmsnorm(h, norm_g[l, 1])
        cols = u @ w_in[l]
        (cq, ckv, krope, sq, sk, sv, nq, nkc, nvc, nks, nvs, nkw, nvw, ngate,
         dq, dk, dv) = jnp.split(cols, offs, axis=-1)
        y_a = mla_mixer(cq, ckv, krope, mla_q_norm[l], mla_kv_norm[l], mla_w_uq[l], mla_w_ukv[l], pos)
        y_b = swa_mixer(sq, sk, sv, swa_sinks[l], tab_swa)
        y_c = nsa_mixer(nq, nkc, nvc, nks, nvs, nkw, nvw, ngate,
                        nsa_cmp_pos[l], nsa_cmp_w1[l], nsa_cmp_w2[l], tab_nsa)
        y_d = diff_mixer(dq, dk, dv, diff_lambda[l], diff_subln[l], tab_diff, l)
        merged = jax.nn.sigmoid(u @ w_gate[l, 0]) * (y_a @ w_branch[l, 0])
        merged = merged + jax.nn.sigmoid(u @ w_gate[l, 1]) * (y_b @ w_branch[l, 1])
        merged = merged + jax.nn.sigmoid(u @ w_gate[l, 2]) * (y_c @ w_branch[l, 2])
        merged = merged + jax.nn.sigmoid(u @ w_gate[l, 3]) * (y_d @ w_branch[l, 3])
        h = h + merged @ w_o[l]
        h = h + 0.5 * swiglu(rmsnorm(h, norm_g[l, 2]), ffn_w_gate[l, 1], ffn_w_up[l, 1], ffn_w_down[l, 1])
    return rmsnorm(h, final_g)
```

```python
import contextlib
import math
import numpy as np
import ml_dtypes
import concourse.bass as bass
import concourse.mybir as mybir
from concourse.bass_utils import run_bass_kernel_spmd

F32 = mybir.dt.float32
BF16 = mybir.dt.bfloat16
AF = mybir.ActivationFunctionType
ALU = mybir.AluOpType

D = 1024
S = 8192
NB = 4
DFF = 2816
NF = 22
EPS = 1e-6
TB = 512
NEGM = -30000.0


class Buf:
    __slots__ = ("lw", "rd", "excl")

    def __init__(self, excl=False):
        self.lw = None
        self.rd = []
        self.excl = excl


class Chan:
    __slots__ = ("sem", "n")

    def __init__(self, sem):
        self.sem = sem
        self.n = 0


class Prog:
    ENGS = ("pe", "act", "dve", "pool", "sp")

    def __init__(self, nc, tag):
        self.nc = nc
        self.tag = tag
        self.stack = contextlib.ExitStack()
        self.lists = {e: [] for e in self.ENGS}
        self.nops = {e: 0 for e in self.ENGS}
        self.waited = {e: {} for e in self.ENGS}
        self.needed = {e: set() for e in self.ENGS}
        self.sems = []
        self.esem = {e: self._sem(tag + "s_" + e) for e in self.ENGS}
        self.nchan = 0
        self.out_toks = []
        self.all_chans = []
        self.nt = 0

    def _sem(self, name):
        h = self.nc.alloc_semaphore(name=name)
        self.sems.append(h)
        return h

    def chan(self):
        self.nchan += 1
        ch = Chan(self._sem("%sc%d" % (self.tag, self.nchan)))
        self.all_chans.append(ch)
        return ch

    def sbuf(self, shape, dt, name=None):
        self.nt += 1
        t = self.stack.enter_context(self.nc.sbuf_tensor("%s_%s%d" % (self.tag, name or "t", self.nt), list(shape), dt))
        return t

    def psum(self, shape, dt=F32, name=None):
        self.nt += 1
        return self.stack.enter_context(self.nc.psum_tensor("%s_%s%d" % (self.tag, name or "p", self.nt), list(shape), dt))

    def _wait(self, eng, tok, same_ok=False):
        if tok is None:
            return
        kind, key, val = tok
        if kind == "e" and key == eng and (same_ok or eng in ("pe", "sp")):
            return
        k = (kind, id(key) if kind == "d" else key)
        w = self.waited[eng]
        if w.get(k, -1) >= val:
            return
        w[k] = val
        if kind == "e":
            self.needed[key].add(val)
        self.lists[eng].append(("w", kind, key, val))

    def _deps(self, eng, reads, writes):
        for b in reads:
            self._wait(eng, b.lw)
        for b in writes:
            self._wait(eng, b.lw)
            for t in b.rd:
                self._wait(eng, t, same_ok=True)

    def _mark(self, tok, reads, writes):
        for b in reads:
            b.rd.append(tok)
            if len(b.rd) > 64:
                b.rd = b.rd[-48:]
        for b in writes:
            b.lw = tok
            b.rd = []

    def op(self, eng, fn, reads=(), writes=()):
        ex = [b for b in reads if b.excl]
        if ex:
            reads = [b for b in reads if not b.excl]
            writes = list(writes) + ex
        self._deps(eng, reads, writes)
        self.nops[eng] += 1
        o = self.nops[eng]
        self.lists[eng].append(("o", fn, o))
        tok = ("e", eng, o)
        self._mark(tok, reads, writes)
        return tok

    def dma(self, eng, out, in_, chan, reads=(), writes=(), final=False):
        self._deps(eng, reads, writes)
        if chan.n > 0:
            self._wait(eng, ("d", chan, 16 * chan.n))
        chan.n += 1
        tok = ("d", chan, 16 * chan.n)
        self.lists[eng].append(("d", out, in_, chan))
        self._mark(tok, reads, writes)
        if final:
            self.out_toks.append(tok)
        return tok

    def collective(self, kind, groups, in_ap, out_ap):
        ch = Chan(self._sem("%scc%d" % (self.tag, self.nchan)))
        self.nchan += 1
        self.lists["pool"].append(("c", lambda e: e.collective_compute(kind, ALU.bypass, replica_groups=groups, ins=[in_ap], outs=[out_ap]), ch))
        self.lists["pool"].append(("w", "d", ch, 1))

    def build(self):
        nc = self.nc
        for t in self.out_toks:
            self._wait("sp", t)
        for ch in self.all_chans:
            if ch.n > 0:
                self._wait("sp", ("d", ch, 16 * ch.n))
        last = {e: self.nops[e] for e in self.ENGS}
        for e in ("pe", "act", "dve", "pool"):
            for f in ("pe", "act", "dve", "pool"):
                if f != e and last[f] > 0:
                    self._wait(e, ("e", f, last[f]))
        rankd = {e: {o: i + 1 for i, o in enumerate(sorted(self.needed[e]))} for e in self.ENGS}

        def run(ename, e):
            need = self.needed[ename]
            for it in self.lists[ename]:
                if it[0] == "w":
                    _, kind, key, val = it
                    if kind == "e":
                        e.wait_ge(self.esem[key], rankd[key][val])
                    else:
                        e.wait_ge(key.sem, val)
                elif it[0] == "o":
                    ins = it[1](e)
                    if it[2] in need:
                        ins.then_inc(self.esem[ename], 1)
                elif it[0] == "c":
                    it[1](e).then_inc(it[2].sem, 1)
                else:
                    _, out, in_, chan = it
                    e.dma_start(out=out, in_=in_).then_inc(chan.sem, 16)

        with nc.Block() as block:
            block.tensor(lambda e: run("pe", e))
            block.scalar(lambda e: run("act", e))
            block.vector(lambda e: run("dve", e))
            block.gpsimd(lambda e: run("pool", e))
            block.sync(lambda e: run("sp", e))
        self.stack.close()
        nc.all_engine_barrier()
        nc.clear_and_free_semaphores(self.sems)
        nc.all_engine_barrier()


class Ring:
    def __init__(self, p, n, shape, dt, name, psum=False, chan=True):
        self.items = []
        for i in range(n):
            t = p.psum(shape, dt, name) if psum else p.sbuf(shape, dt, name)
            self.items.append((t, Buf(excl=psum), p.chan() if chan else None))
        self.i = 0

    def next(self):
        it = self.items[self.i % len(self.items)]
        self.i += 1
        return it


def dram(nc, name, shape, dt, kind="Internal"):
    return nc.dram_tensor(name, list(shape), dt, kind=kind)


def precast(p, src, K, N, dst, chans, nsplit=None):
    C = K // 128
    b = Buf()
    for c in range(C):
        ch = chans[c % len(chans)]
        p.dma("pool", dst.ap()[:, c, :], src[c * 128:(c + 1) * 128, :], ch, writes=[])
    return b


def precast_chunked(p, src, K, chunks, dst, chans):
    for i, (c0, w) in enumerate(chunks):
        ch = chans[i % len(chans)]
        p.dma("pool", dst.ap()[i, :, :, 0:w], src[:, c0:c0 + w].rearrange("(c p) j -> p c j", p=128), ch, writes=[])


class TokCtx:
    def __init__(self, p, nc):
        self.p = p
        self.nc = nc
        self.identb = p.sbuf([128, 128], BF16, "identb")
        self.b_id = Buf()
        p.op("pool", lambda e: e.memset(self.identb[:], 1.0), writes=[self.b_id])
        p.op("pool", lambda e: e.affine_select(out=self.identb[:], in_=self.identb[:], pattern=[[-1, 128]],
                                                compare_op=ALU.is_equal, fill=0.0, base=0, channel_multiplier=1),
             reads=[self.b_id], writes=[self.b_id])
        self.eps = p.sbuf([128, 1], F32, "eps")
        self.b_eps = Buf()
        p.op("pool", lambda e: e.memset(self.eps[:], EPS), writes=[self.b_eps])
        self.h = [(p.sbuf([128, D], F32, "h"), Buf(), p.chan(), p.chan()) for _ in range(4)]
        self.junk = p.sbuf([128, D], BF16, "junk")
        self.b_junk = Buf()
        self.ss = p.sbuf([128, 4], F32, "ss")
        self.b_ss = Buf()
        self.rs = p.sbuf([128, 4], F32, "rs")
        self.b_rs = Buf()
        self.xnb = Ring(p, 2, [128, D], BF16, "xnb", chan=False)
        self.pst = Ring(p, 2, [128, D], BF16, "pst", psum=True, chan=False)
        self.psf = Ring(p, 6, [128, 512], F32, "psf", psum=True, chan=False)
        self.wst = Ring(p, 8, [128, 8, 128], BF16, "wst")
        self.sg = Ring(p, 2, [128, 512], F32, "sg", chan=False)
        self.hidT = p.sbuf([128, NF, TB], BF16, "hidT")
        self.b_hid = Buf()
        self.wd = p.sbuf([128, NF, D], BF16, "wd")
        self.b_wd = Buf()
        self.c_wd = [p.chan() for _ in range(2)]

    def load_h(self, src, row0):
        p = self.p
        for t in range(4):
            ht, hb, hc, _ = self.h[t]
            p.dma("sp", ht[:], src[row0 + t * 128: row0 + (t + 1) * 128, :], hc, writes=[hb])

    def store_h(self, dst, row0, final=False):
        p = self.p
        for t in range(4):
            ht, hb, _, hc2 = self.h[t]
            p.dma("sp", dst[row0 + t * 128: row0 + (t + 1) * 128, :], ht[:], hc2, reads=[hb], final=final)

    def norm_T(self, gT, b_g, outT, b_out):
        p = self.p
        for t in range(4):
            ht, hb, _, _ = self.h[t]
            ss, rs = self.ss, self.rs
            p.op("act", lambda e, ht=ht, t=t: e.activation(out=self.junk[:], in_=ht[:], func=AF.Square,
                                                           accum_out=ss[:, t:t + 1]),
                 reads=[hb], writes=[self.b_junk, self.b_ss])
            p.op("act", lambda e, t=t: e.activation(out=rs[:, t:t + 1], in_=ss[:, t:t + 1], func=AF.Sqrt,
                                                    bias=self.eps[:], scale=1.0 / D),
                 reads=[self.b_ss, self.b_eps], writes=[self.b_rs])
            p.op("dve", lambda e, t=t: e.reciprocal(out=rs[:, t:t + 1], in_=rs[:, t:t + 1]),
                 reads=[self.b_rs], writes=[self.b_rs])
            xn, bxn, _ = self.xnb.next()
            p.op("dve", lambda e, xn=xn, ht=ht, t=t: e.tensor_scalar(out=xn[:], in0=ht[:], scalar1=rs[:, t:t + 1],
                                                                     scalar2=None, op0=ALU.mult),
                 reads=[hb, self.b_rs], writes=[bxn])
            ps, bps, _ = self.pst.next()
            for c in range(8):
                p.op("pe", lambda e, ps=ps, xn=xn, c=c: e.transpose(ps[:, c * 128:(c + 1) * 128],
                                                                    xn[:, c * 128:(c + 1) * 128], self.identb[:]),
                     reads=[bxn, self.b_id], writes=[bps])
            p.op("dve", lambda e, ps=ps, t=t: e.tensor_tensor(
                out=outT[:, :, t * 128:(t + 1) * 128], in0=ps[:].rearrange("p (c j) -> p c j", c=8),
                in1=gT[:].unsqueeze(2).to_broadcast([128, 8, 128]), op=ALU.mult),
                 reads=[bps, b_g], writes=[b_out])

    def load_wd(self, wd_c):
        p = self.p
        half = NF // 2
        p.dma("sp", self.wd[:, 0:half, :], wd_c.ap()[:, 0:half, :], self.c_wd[0], writes=[self.b_wd])
        p.dma("sp", self.wd[:, half:NF, :], wd_c.ap()[:, half:NF, :], self.c_wd[1], writes=[])
        self.b_wd.lw = None
        self.wd_toks = [("d", self.c_wd[0], 16 * self.c_wd[0].n), ("d", self.c_wd[1], 16 * self.c_wd[1].n)]

    def ffn(self, xnT, b_xn, wg_c, wu_c):
        p = self.p
        for f in range(NF):
            wg, bwg, cwg = self.wst.next()
            p.dma("sp", wg[:], wg_c.ap()[f], cwg, writes=[bwg])
            wu, bwu, cwu = self.wst.next()
            p.dma("sp", wu[:], wu_c.ap()[f], cwu, writes=[bwu])
            gps, bg, _ = self.psf.next()
            for c in range(8):
                p.op("pe", lambda e, gps=gps, wg=wg, c=c: e.matmul(gps[:], lhsT=wg[:, c, :], rhs=xnT[:, c, :],
                                                                   start=(c == 0), stop=(c == 7)),
                     reads=[bwg, b_xn], writes=[bg])
            ups, bu, _ = self.psf.next()
            for c in range(8):
                p.op("pe", lambda e, ups=ups, wu=wu, c=c: e.matmul(ups[:], lhsT=wu[:, c, :], rhs=xnT[:, c, :],
                                                                   start=(c == 0), stop=(c == 7)),
                     reads=[bwu, b_xn], writes=[bu])
            sg, bsg, _ = self.sg.next()
            p.op("act", lambda e, sg=sg, gps=gps: e.activation(out=sg[:], in_=gps[:], func=AF.Silu),
                 reads=[bg], writes=[bsg])
            p.op("dve", lambda e, sg=sg, ups=ups, f=f: e.tensor_tensor(out=self.hidT[:, f, :], in0=sg[:], in1=ups[:],
                                                                       op=ALU.mult),
                 reads=[bsg, bu], writes=[self.b_hid])
        for tk in self.wd_toks:
            p._wait("pe", tk)
        for t in range(4):
            ht, hb, _, _ = self.h[t]
            for hf in range(2):
                ops, bo, _ = self.psf.next()
                for f in range(NF):
                    p.op("pe", lambda e, ops=ops, t=t, hf=hf, f=f: e.matmul(
                        ops[:], lhsT=self.hidT[:, f, t * 128:(t + 1) * 128], rhs=self.wd[:, f, hf * 512:(hf + 1) * 512],
                        start=(f == 0), stop=(f == NF - 1)),
                         reads=[self.b_hid], writes=[bo])
                p.op("dve", lambda e, ops=ops, ht=ht, hf=hf: e.scalar_tensor_tensor(
                    out=ht[:, hf * 512:(hf + 1) * 512], in0=ops[:], scalar=0.5, in1=ht[:, hf * 512:(hf + 1) * 512],
                    op0=ALU.mult, op1=ALU.add),
                     reads=[bo, hb], writes=[hb])


def phase_A(nc, tag, h_in, h_out, uT_out, gT0_d, gT1_d, wg_d, wu_d, wd_d, nblk):
    wg_c = dram(nc, tag + "wg_c", [NF, 128, 8, 128], BF16)
    wu_c = dram(nc, tag + "wu_c", [NF, 128, 8, 128], BF16)
    wd_c = dram(nc, tag + "wd_c", [128, NF, D], BF16)
    p = Prog(nc, tag + "pc")
    chans = [p.chan() for _ in range(8)]
    precast_chunked(p, wg_d, D, [(f * 128, 128) for f in range(NF)], wg_c, chans)
    precast_chunked(p, wu_d, D, [(f * 128, 128) for f in range(NF)], wu_c, chans)
    precast(p, wd_d, DFF, D, wd_c, chans)
    for ch in chans:
        p._wait("pool", ("d", ch, 16 * ch.n))
    p.build()
    p = Prog(nc, tag)
    cx = TokCtx(p, nc)
    g0 = p.sbuf([128, 8], F32, "g0"); b_g0 = Buf(); c_g0 = p.chan()
    g1 = p.sbuf([128, 8], F32, "g1"); b_g1 = Buf(); c_g1 = p.chan()
    p.dma("sp", g0[:], gT0_d, c_g0, writes=[b_g0])
    p.dma("sp", g1[:], gT1_d, c_g1, writes=[b_g1])
    cx.load_wd(wd_c)
    xnT = p.sbuf([128, 8, TB], BF16, "xnT"); b_xn = Buf()
    uT = p.sbuf([128, 8, TB], BF16, "uT"); b_u = Buf(); c_u = p.chan()
    for lb in range(nblk):
        cx.load_h(h_in, lb * TB)
        cx.norm_T(g0, b_g0, xnT, b_xn)
        cx.ffn(xnT, b_xn, wg_c, wu_c)
        cx.store_h(h_out, lb * TB, final=True)
        cx.norm_T(g1, b_g1, uT, b_u)
        p.dma("sp", uT_out.ap()[lb], uT[:], c_u, reads=[b_u], final=True)
    p.build()


def build_launch_A():
    nc = bass.Bass("TRN2", target_bir_lowering=False)
    h_in = nc.dram_tensor("h_in", [4096, D], F32, kind="ExternalInput").ap()
    gT0 = nc.dram_tensor("gT0", [128, 8], F32, kind="ExternalInput").ap()
    gT1 = nc.dram_tensor("gT1", [128, 8], F32, kind="ExternalInput").ap()
    wg = nc.dram_tensor("wg", [D, DFF], F32, kind="ExternalInput").ap()
    wu = nc.dram_tensor("wu", [D, DFF], F32, kind="ExternalInput").ap()
    wd = nc.dram_tensor("wd", [DFF, D], F32, kind="ExternalInput").ap()
    h_out = nc.dram_tensor("h_out", [4096, D], F32, kind="ExternalOutput").ap()
    uT_out = nc.dram_tensor("uT_out", [8, 128, 8, TB], BF16, kind="ExternalOutput")
    phase_A(nc, "A", h_in, h_out, uT_out, gT0, gT1, wg, wu, wd, 8)
    return nc


def gT_layout(g):
    return np.ascontiguousarray(np.asarray(g, np.float32).reshape(8, 128).T)


def phase_C(nc, tag, h_in, h_out, uT_in, yT_in, gT2_d, wgate_d, wbr_d, wo_d, wg_d, wu_d, wd_d, nblk, gfin_d=None, ygath=None, par_d=None):
    wg_c = dram(nc, tag + "wg_c", [NF, 128, 8, 128], BF16)
    wu_c = dram(nc, tag + "wu_c", [NF, 128, 8, 128], BF16)
    wd_c = dram(nc, tag + "wd_c", [128, NF, D], BF16)
    wgate_c = [dram(nc, tag + "wgate_c%d" % i, [8, 128, 8, 128], BF16) for i in range(4)]
    wbr_c = [dram(nc, tag + "wbr_c%d" % i, [8, 128, 4, 128], BF16) for i in range(4)]
    wo_c = dram(nc, tag + "wo_c", [128, 8, D], BF16)
    p = Prog(nc, tag + "pc")
    chans = [p.chan() for _ in range(8)]
    precast_chunked(p, wg_d, D, [(f * 128, 128) for f in range(NF)], wg_c, chans)
    precast_chunked(p, wu_d, D, [(f * 128, 128) for f in range(NF)], wu_c, chans)
    precast(p, wd_d, DFF, D, wd_c, chans)
    for i in range(4):
        precast_chunked(p, wgate_d[i], D, [(j * 128, 128) for j in range(8)], wgate_c[i], chans)
        precast_chunked(p, wbr_d[i], 512, [(j * 128, 128) for j in range(8)], wbr_c[i], chans)
    precast(p, wo_d, D, D, wo_c, chans)
    for ch in chans:
        p._wait("pool", ("d", ch, 16 * ch.n))
    p.build()

    p = Prog(nc, tag)
    cx = TokCtx(p, nc)
    g2 = p.sbuf([128, 8], F32, "g2"); b_g2 = Buf(); c_g2 = p.chan()
    p.dma("sp", g2[:], gT2_d, c_g2, writes=[b_g2])
    cx.load_wd(wd_c)
    wo = p.sbuf([128, 8, D], BF16, "wo"); b_wo = Buf(); c_wo = p.chan()
    p.dma("sp", wo[:], wo_c.ap(), c_wo, writes=[b_wo])
    if gfin_d is not None:
        gfin = p.sbuf([128, D], F32, "gfin"); b_gf = Buf(); c_gf = p.chan()
        p.dma("sp", gfin[:], bass.AP(gfin_d.tensor, 0, [[0, 128], [1, D]]), c_gf, writes=[b_gf])
    xnT = p.sbuf([128, 8, TB], BF16, "xnT"); b_xn = Buf()
    uT = p.sbuf([128, 8, TB], BF16, "uT"); b_u = Buf(); c_u = p.chan()
    yT = p.sbuf([128, 16, TB], BF16, "yT"); b_y = Buf(); c_y = p.chan()
    mT = p.sbuf([128, 8, TB], BF16, "mT"); b_m = Buf()
    macc = Ring(p, 2, [128, TB], F32, "macc", chan=False)
    tmpr = Ring(p, 2, [128, TB], F32, "tmpr", chan=False)
    wbst = Ring(p, 6, [128, 4, 128], BF16, "wbst")
    if ygath is not None:
        yT2 = p.sbuf([128, 16, TB], BF16, "yT2"); b_y2 = Buf(); c_y2 = [p.chan() for _ in range(8)]
        c_y1 = [p.chan() for _ in range(8)]
        parf = p.sbuf([128, TB], F32, "parf"); b_pf = Buf(); c_pf = p.chan()
        parm = p.sbuf([128, TB], mybir.dt.uint32, "parm"); b_pm = Buf()
        p.dma("sp", parf[:], par_d, c_pf, writes=[b_pf])
        p.op("dve", lambda e: e.tensor_scalar(out=parm[:], in0=parf[:], scalar1=0.5, scalar2=None, op0=ALU.is_gt), reads=[b_pf], writes=[b_pm])
    for lb in range(nblk):
        cx.load_h(h_in, lb * TB)
        p.dma("sp", uT[:], uT_in.ap()[lb], c_u, writes=[b_u])
        if ygath is None:
            p.dma("sp", yT[:], yT_in.ap()[lb], c_y, writes=[b_y])
        else:
            n_ = 0
            for rr in range(2):
                for i in range(4):
                    p.dma("sp", yT[:, i * 4 + 2 * rr:i * 4 + 2 * rr + 2, :],
                          ygath.ap()[lb // 2, rr, lb % 2, i].rearrange("(k p) t -> p k t", p=128), c_y1[n_], writes=[b_y] if n_ == 0 else [])
                    p.dma("sp", yT2[:, i * 4 + 2 * rr:i * 4 + 2 * rr + 2, :],
                          ygath.ap()[4 + lb // 2, rr, lb % 2, i].rearrange("(k p) t -> p k t", p=128), c_y2[n_], writes=[b_y2] if n_ == 0 else [])
                    n_ += 1
            for ch in c_y1 + c_y2:
                p._wait("dve", ("d", ch, 16 * ch.n))
            for k in range(16):
                p.op("dve", lambda e, k=k: e.copy_predicated(yT[:, k, :], parm[:], yT2[:, k, :]), reads=[b_pm, b_y2], writes=[b_y])
        for j in range(8):
            ma, bma, _ = macc.next()
            for i in range(4):
                wgt, bwg, cwg = cx.wst.next()
                p.dma("sp", wgt[:], wgate_c[i].ap()[j], cwg, writes=[bwg])
                wb, bwb, cwb = wbst.next()
                p.dma("sp", wb[:], wbr_c[i].ap()[j], cwb, writes=[bwb])
                gps, bg, _ = cx.psf.next()
                for c in range(8):
                    p.op("pe", lambda e, gps=gps, wgt=wgt, c=c: e.matmul(gps[:], lhsT=wgt[:, c, :], rhs=uT[:, c, :],
                                                                         start=(c == 0), stop=(c == 7)),
                         reads=[bwg, b_u], writes=[bg])
                bps, bb, _ = cx.psf.next()
                for k in range(4):
                    p.op("pe", lambda e, bps=bps, wb=wb, k=k, i=i: e.matmul(bps[:], lhsT=wb[:, k, :], rhs=yT[:, i * 4 + k, :],
                                                                             start=(k == 0), stop=(k == 3)),
                         reads=[bwb, b_y], writes=[bb])
                sg, bsg, _ = cx.sg.next()
                p.op("act", lambda e, sg=sg, gps=gps: e.activation(out=sg[:], in_=gps[:], func=AF.Sigmoid),
                     reads=[bg], writes=[bsg])
                if i == 0:
                    p.op("dve", lambda e, ma=ma, sg=sg, bps=bps: e.tensor_tensor(out=ma[:], in0=sg[:], in1=bps[:], op=ALU.mult),
                         reads=[bsg, bb], writes=[bma])
                else:
                    tm, btm, _ = tmpr.next()
                    p.op("dve", lambda e, tm=tm, sg=sg, bps=bps: e.tensor_tensor(out=tm[:], in0=sg[:], in1=bps[:], op=ALU.mult),
                         reads=[bsg, bb], writes=[btm])
                    p.op("pool", lambda e, ma=ma, tm=tm: e.tensor_tensor(out=ma[:], in0=ma[:], in1=tm[:], op=ALU.add),
                         reads=[bma, btm], writes=[bma])
            p.op("act", lambda e, ma=ma, j=j: e.copy(out=mT[:, j, :], in_=ma[:]), reads=[bma], writes=[b_m])
        for t in range(4):
            ht, hb, _, _ = cx.h[t]
            for hf in range(2):
                ops, bo, _ = cx.psf.next()
                for j in range(8):
                    p.op("pe", lambda e, ops=ops, t=t, hf=hf, j=j: e.matmul(
                        ops[:], lhsT=mT[:, j, t * 128:(t + 1) * 128], rhs=wo[:, j, hf * 512:(hf + 1) * 512],
                        start=(j == 0), stop=(j == 7)), reads=[b_m, b_wo], writes=[bo])
                p.op("dve", lambda e, ops=ops, ht=ht, hf=hf: e.tensor_tensor(
                    out=ht[:, hf * 512:(hf + 1) * 512], in0=ht[:, hf * 512:(hf + 1) * 512], in1=ops[:], op=ALU.add),
                     reads=[bo, hb], writes=[hb])
        cx.norm_T(g2, b_g2, xnT, b_xn)
        cx.ffn(xnT, b_xn, wg_c, wu_c)
        if gfin_d is not None:
            for t in range(4):
                ht, hb, _, _ = cx.h[t]
                ss, rs = cx.ss, cx.rs
                p.op("act", lambda e, ht=ht, t=t: e.activation(out=cx.junk[:], in_=ht[:], func=AF.Square,
                                                               accum_out=ss[:, t:t + 1]),
                     reads=[hb], writes=[cx.b_junk, cx.b_ss])
                p.op("act", lambda e, t=t: e.activation(out=rs[:, t:t + 1], in_=ss[:, t:t + 1], func=AF.Sqrt,
                                                        bias=cx.eps[:], scale=1.0 / D),
                     reads=[cx.b_ss, cx.b_eps], writes=[cx.b_rs])
                p.op("dve", lambda e, t=t: e.reciprocal(out=rs[:, t:t + 1], in_=rs[:, t:t + 1]),
                     reads=[cx.b_rs], writes=[cx.b_rs])
                p.op("dve", lambda e, ht=ht, t=t: e.scalar_tensor_tensor(out=ht[:], in0=ht[:], scalar=rs[:, t:t + 1],
                                                                         in1=gfin[:], op0=ALU.mult, op1=ALU.mult),
                     reads=[hb, cx.b_rs, b_gf], writes=[hb])
        cx.store_h(h_out, lb * TB, final=True)
    p.build()


def build_launch_C(final):
    nc = bass.Bass("TRN2", target_bir_lowering=False)
    h_in = nc.dram_tensor("h_in", [4096, D], F32, kind="ExternalInput").ap()
    uT_in = nc.dram_tensor("uT_in", [8, 128, 8, TB], BF16, kind="ExternalInput")
    yT_in = nc.dram_tensor("yT_in", [8, 128, 16, TB], BF16, kind="ExternalInput")
    gT2 = nc.dram_tensor("gT2", [128, 8], F32, kind="ExternalInput").ap()
    wgate = nc.dram_tensor("wgate", [4, D, D], F32, kind="ExternalInput").ap()
    wbr = nc.dram_tensor("wbr", [4, 512, D], F32, kind="ExternalInput").ap()
    wo = nc.dram_tensor("wo", [D, D], F32, kind="ExternalInput").ap()
    wg = nc.dram_tensor("wg", [D, DFF], F32, kind="ExternalInput").ap()
    wu = nc.dram_tensor("wu", [D, DFF], F32, kind="ExternalInput").ap()
    wd = nc.dram_tensor("wd", [DFF, D], F32, kind="ExternalInput").ap()
    gfin = nc.dram_tensor("gfin", [D], F32, kind="ExternalInput").ap() if final else None
    h_out = nc.dram_tensor("h_out", [4096, D], F32, kind="ExternalOutput").ap()
    phase_C(nc, "C", h_in, h_out, uT_in, yT_in, gT2, wgate, wbr, wo, wg, wu, wd, 8, gfin)
    return nc


FM_COLS = 1804
TM_COLS = 448
OFF = dict(cq=0, ckv=256, kr=384, krp=416, sq=448, sk=704, nq=768, kcvc=1024, kskw=1152, ng=1280, dq=1292, dk=1548)
NKIND = 15
FM_CHUNKS = [(0, 128), (128, 128), (256, 128), (384, 32), (416, 32), (448, 128), (576, 128), (704, 64), (768, 128), (896, 128),
             (1024, 128), (1152, 128), (1280, 12), (1292, 128), (1420, 128), (1548, 128), (1676, 128)]
SW = 1024
TVL = 1152


class Scratch:
    def __init__(self, nc, tag):
        d = lambda n, s, dt=BF16: dram(nc, tag + n, s, dt)
        self.mla_qT = d("mla_qT", [16, 4, 96, TB])
        self.mla_kT = d("mla_kT", [16, 4, 96, TB])
        self.mla_v = d("mla_v", [16, 128, 4, 256])
        self.swa_qT = d("swa_qT", [16, 256, TB])
        self.swa_kT = d("swa_kT", [16, 64, TB])
        self.swa_v = d("swa_v", [16, 128, 4, 64])
        self.nsa_qT = d("nsa_qT", [16, 256, TB])
        self.nsa_kcT = d("nsa_kcT", [64, S + 64])
        self.nsa_vcT = d("nsa_vcT", [64, S + 64])
        self.nsa_ksT = d("nsa_ksT", [16, 64, TB])
        self.nsa_kwT = d("nsa_kwT", [16, 64, TB])
        self.nsa_vs = d("nsa_vs", [16, 128, 4, 64])
        self.nsa_vw = d("nsa_vw", [16, 128, 4, 64])
        self.nsa_gsig = d("nsa_gsig", [16, 12, TB], F32)
        self.diff_qT = d("diff_qT", [16, 256, TB])
        self.diff_kT = d("diff_kT", [16, 256, TB])
        self.diff_v = d("diff_v", [16, 128, 4, 256])
        self.tvec = d("tvec", [NKIND, TVL])


def phase_B1(nc, tag, sc, uT_in, win_fm_d, win_tm_d, wuq_d, wuqp_d, wukvk_d, wukvv_d, qnT_d, kvn_d, ropeC_d, ropeS_d, nblk=16, stage=99):
    fm_c = dram(nc, tag + "fm_c", [len(FM_CHUNKS), 128, 8, 128], BF16)
    tm_c = dram(nc, tag + "tm_c", [128, 8, TM_COLS], BF16)
    p = Prog(nc, tag + "pc")
    chans = [p.chan() for _ in range(8)]
    precast_chunked(p, win_fm_d, D, FM_CHUNKS, fm_c, chans)
    precast(p, win_tm_d, D, TM_COLS, tm_c, chans)
    for ch in chans:
        p._wait("pool", ("d", ch, 16 * ch.n))
    p.build()

    p = Prog(nc, tag)
    psf = Ring(p, 7, [128, 512], F32, "psf", psum=True, chan=False)
    wst = Ring(p, 8, [128, 8, 128], BF16, "wst")
    uTr = Ring(p, 2, [128, 8, TB], BF16, "uT")
    ev = Ring(p, 6, [128, TB], BF16, "ev")
    eps = p.sbuf([128, 1], F32, "eps"); b_eps = Buf()
    p.op("pool", lambda e: e.memset(eps[:], EPS), writes=[b_eps])
    ones_f = p.sbuf([128, 128], F32, "ones"); b_ones = Buf()
    p.op("pool", lambda e: e.memset(ones_f[:], 1.0), writes=[b_ones])
    def res(shape, src):
        t = p.sbuf(shape, BF16, "res"); b = Buf(); c = p.chan()
        p.dma("pool", t[:], src, c, writes=[b])
        return t, b
    wuq, b_wuq = res([128, 2, 384], wuq_d.rearrange("(c p) n -> p c n", p=128))
    wuqp, b_wuqp = res([128, 2, 384], wuqp_d.rearrange("(c p) n -> p c n", p=128))
    wkk, b_wkk = res([128, 256], wukvk_d)
    wkv, b_wkv = res([128, 256], wukvv_d)
    wtm = p.sbuf([128, 8, TM_COLS], BF16, "wtm"); b_wtm = Buf(); c_wtm = p.chan()
    p.dma("sp", wtm[:], tm_c.ap(), c_wtm, writes=[b_wtm])
    qn = p.sbuf([128, 2], F32, "qn"); b_qn = Buf(); c_qn = p.chan()
    p.dma("sp", qn[:], qnT_d, c_qn, writes=[b_qn])
    kvn = p.sbuf([128, 1], F32, "kvn"); b_kvn = Buf(); c_kvn = p.chan()
    p.dma("sp", kvn[:], kvn_d, c_kvn, writes=[b_kvn])
    rC = p.sbuf([96, TB], F32, "rC"); b_rC = Buf(); c_rC = p.chan()
    rS = p.sbuf([96, TB], F32, "rS"); b_rS = Buf(); c_rS = p.chan()
    rC32 = p.sbuf([32, TB], F32, "rC32"); b_rC32 = Buf(); c_rC32 = p.chan()
    rS32 = p.sbuf([32, TB], F32, "rS32"); b_rS32 = Buf(); c_rS32 = p.chan()
    sq = [(p.sbuf([128, TB], F32, "sq"), Buf()) for _ in range(2)]
    cqf = [(p.sbuf([128, TB], F32, "cqf"), Buf()) for _ in range(2)]
    rs = p.sbuf([128, TB], F32, "rs"); b_rs = Buf()
    cqn = p.sbuf([128, 2, TB], BF16, "cqn"); b_cqn = Buf()
    ckvn = p.sbuf([128, TB], BF16, "ckvn"); b_ckvn = Buf()
    ta = Ring(p, 2, [96, TB], F32, "ta", chan=False)
    tb = Ring(p, 2, [96, TB], F32, "tb", chan=False)
    vt = p.sbuf([128, 4, 256], BF16, "vt"); b_vt = Buf(); c_vt = p.chan()
    vtm = p.sbuf([128, 4, TM_COLS], BF16, "vtm"); b_vtm = Buf(); c_vtm = [p.chan() for _ in range(4)]
    gs = p.sbuf([12, TB], F32, "gs"); b_gs = Buf(); c_gs = p.chan()
    flip = [0]

    def fm(uT, b_u, col0, w):
        wt, bw, cw = wst.next()
        p.dma("sp", wt[:], fm_c.ap()[FM_CHUNKS.index((col0, w))], cw, writes=[bw])
        ps, bp, _ = psf.next()
        for c in range(8):
            p.op("pe", lambda e, ps=ps, wt=wt, c=c, w=w: e.matmul(ps[0:w, :], lhsT=wt[:, c, 0:w], rhs=uT[:, c, :],
                                                                   start=(c == 0), stop=(c == 7)),
                 reads=[bw, b_u], writes=[bp])
        return ps, bp

    def evac(ps, bp, w, scale, dsts):
        t, bt, ct = ev.next()
        flip[0] ^= 1
        if flip[0]:
            p.op("act", lambda e: e.activation(out=t[0:w, :], in_=ps[0:w, :], func=AF.Copy, scale=scale),
                 reads=[bp], writes=[bt])
        else:
            p.op("dve", lambda e: e.tensor_scalar(out=t[0:w, :], in0=ps[0:w, :], scalar1=scale, scalar2=None, op0=ALU.mult),
                 reads=[bp], writes=[bt])
        for i, (dst, r0, nr) in enumerate(dsts):
            ch = ct if i == 0 else extra_ch[i - 1]
            p.dma("sp", dst, t[r0:r0 + nr, :], ch, reads=[bt])

    extra_ch = [p.chan() for _ in range(3)]

    def rms_fm(chunks, nfeat, gn, b_gn, outs, b_out):
        for i, (ps, bp) in enumerate(chunks):
            p.op("act", lambda e, i=i, ps=ps: e.activation(out=sq[i][0][:], in_=ps[:], func=AF.Square),
                 reads=[bp], writes=[sq[i][1]])
            p.op("dve", lambda e, i=i, ps=ps: e.tensor_copy(out=cqf[i][0][:], in_=ps[:]), reads=[bp], writes=[cqf[i][1]])
        ss, bss, _ = psf.next()
        n = len(chunks)
        for i in range(n):
            p.op("pe", lambda e, i=i, ss=ss: e.matmul(ss[:], lhsT=ones_f[:], rhs=sq[i][0][:], start=(i == 0), stop=(i == n - 1)),
                 reads=[b_ones, sq[i][1]], writes=[bss])
        p.op("act", lambda e, ss=ss: e.activation(out=rs[:], in_=ss[:], func=AF.Sqrt, bias=eps[:], scale=1.0 / nfeat),
             reads=[bss, b_eps], writes=[b_rs])
        p.op("dve", lambda e: e.reciprocal(out=rs[:], in_=rs[:]), reads=[b_rs], writes=[b_rs])
        for i in range(n):
            p.op("dve", lambda e, i=i: e.scalar_tensor_tensor(out=outs[i], in0=cqf[i][0][:], scalar=gn[:, i:i + 1], in1=rs[:],
                                                              op0=ALU.mult, op1=ALU.mult),
                 reads=[cqf[i][1], b_gn, b_rs], writes=[b_out])

    for blk in range(nblk):
        uT, b_u, c_u = uTr.next()
        p.dma("sp", uT[:], (uT_in(blk) if callable(uT_in) else uT_in.ap()[blk]), c_u, writes=[b_u])
        p.dma("sp", rC[:], ropeC_d[blk], c_rC, writes=[b_rC])
        p.dma("sp", rS[:], ropeS_d[blk], c_rS, writes=[b_rS])
        p.dma("sp", rC32[:], ropeC_d[blk, 64:96, :], c_rC32, writes=[b_rC32])
        p.dma("sp", rS32[:], ropeS_d[blk, 64:96, :], c_rS32, writes=[b_rS32])
        ch = [fm(uT, b_u, OFF["cq"], 128), fm(uT, b_u, OFF["cq"] + 128, 128)]
        rms_fm(ch, 256, qn, b_qn, [cqn[:, 0, :], cqn[:, 1, :]], b_cqn)
        ch = [fm(uT, b_u, OFF["ckv"], 128)]
        rms_fm(ch, 128, kvn, b_kvn, [ckvn[:]], b_ckvn)
        if stage < 1:
            continue
        pa, bpa = fm(uT, b_u, OFF["kr"], 32)
        pb, bpb = fm(uT, b_u, OFF["krp"], 32)
        t1, bt1, _ = ta.next()
        t2, bt2, _ = tb.next()
        p.op("dve", lambda e, t1=t1, pa=pa: e.tensor_tensor(out=t1[0:32, :], in0=pa[0:32, :], in1=rC32[:], op=ALU.mult),
             reads=[bpa, b_rC32], writes=[bt1])
        p.op("dve", lambda e, t2=t2, pb=pb: e.tensor_tensor(out=t2[0:32, :], in0=pb[0:32, :], in1=rS32[:], op=ALU.mult),
             reads=[bpb, b_rS32], writes=[bt2])
        kr, bkr, ckr = ev.next()
        p.op("pool", lambda e, kr=kr, t1=t1, t2=t2: e.tensor_tensor(out=kr[0:32, :], in0=t1[0:32, :], in1=t2[0:32, :], op=ALU.add),
             reads=[bt1, bt2], writes=[bkr])
        for h in range(4):
            p.dma("sp", sc.mla_kT.ap()[blk, h, 64:96, :], kr[0:32, :], ckr if h == 0 else extra_ch[h - 1], reads=[bkr])
        if stage < 2:
            continue
        for h in range(4):
            pq, bpq, _ = psf.next()
            pp, bpp, _ = psf.next()
            for c in range(2):
                p.op("pe", lambda e, pq=pq, c=c, h=h: e.matmul(pq[0:96, :], lhsT=wuq[:, c, h * 96:(h + 1) * 96], rhs=cqn[:, c, :],
                                                               start=(c == 0), stop=(c == 1)), reads=[b_wuq, b_cqn], writes=[bpq])
            for c in range(2):
                p.op("pe", lambda e, pp=pp, c=c, h=h: e.matmul(pp[0:96, :], lhsT=wuqp[:, c, h * 96:(h + 1) * 96], rhs=cqn[:, c, :],
                                                               start=(c == 0), stop=(c == 1)), reads=[b_wuqp, b_cqn], writes=[bpp])
            t1, bt1, _ = ta.next()
            t2, bt2, _ = tb.next()
            p.op("dve", lambda e, t1=t1, pq=pq: e.tensor_tensor(out=t1[:], in0=pq[0:96, :], in1=rC[:], op=ALU.mult),
                 reads=[bpq, b_rC], writes=[bt1])
            p.op("dve", lambda e, t2=t2, pp=pp: e.tensor_tensor(out=t2[:], in0=pp[0:96, :], in1=rS[:], op=ALU.mult),
                 reads=[bpp, b_rS], writes=[bt2])
            qq, bqq, cqq = ev.next()
            p.op("pool", lambda e, qq=qq, t1=t1, t2=t2: e.tensor_tensor(out=qq[0:96, :], in0=t1[:], in1=t2[:], op=ALU.add),
                 reads=[bt1, bt2], writes=[bqq])
            p.dma("sp", sc.mla_qT.ap()[blk, h], qq[0:96, :], cqq, reads=[bqq])
        if stage < 3:
            continue
        for hp in range(2):
            pk, bpk, _ = psf.next()
            p.op("pe", lambda e, pk=pk, hp=hp: e.matmul(pk[:], lhsT=wkk[:, hp * 128:(hp + 1) * 128], rhs=ckvn[:], start=True, stop=True),
                 reads=[b_wkk, b_ckvn], writes=[bpk])
            evac(pk, bpk, 128, 1.0, [(sc.mla_kT.ap()[blk, 2 * hp, 0:64, :], 0, 64), (sc.mla_kT.ap()[blk, 2 * hp + 1, 0:64, :], 64, 64)])
        for t in range(4):
            pv, bpv, _ = psf.next()
            p.op("pe", lambda e, pv=pv, t=t: e.matmul(pv[:, 0:256], lhsT=ckvn[:, t * 128:(t + 1) * 128], rhs=wkv[:], start=True, stop=True),
                 reads=[b_ckvn, b_wkv], writes=[bpv])
            p.op("act", lambda e, pv=pv, t=t: e.copy(out=vt[:, t, :], in_=pv[:, 0:256]), reads=[bpv], writes=[b_vt])
        p.dma("sp", sc.mla_v.ap()[blk], vt[:], c_vt, reads=[b_vt])
        if stage < 4:
            continue
        for i in range(2):
            ps, bp = fm(uT, b_u, OFF["sq"] + i * 128, 128)
            evac(ps, bp, 128, 0.125, [(sc.swa_qT.ap()[blk, i * 128:(i + 1) * 128, :], 0, 128)])
        ps, bp = fm(uT, b_u, OFF["sk"], 64)
        evac(ps, bp, 64, 1.0, [(sc.swa_kT.ap()[blk], 0, 64)])
        for i in range(2):
            ps, bp = fm(uT, b_u, OFF["nq"] + i * 128, 128)
            evac(ps, bp, 128, 0.125, [(sc.nsa_qT.ap()[blk, i * 128:(i + 1) * 128, :], 0, 128)])
        ps, bp = fm(uT, b_u, OFF["kcvc"], 128)
        evac(ps, bp, 128, 1.0, [(sc.nsa_kcT.ap()[:, blk * TB:(blk + 1) * TB], 0, 64), (sc.nsa_vcT.ap()[:, blk * TB:(blk + 1) * TB], 64, 64)])
        ps, bp = fm(uT, b_u, OFF["kskw"], 128)
        evac(ps, bp, 128, 1.0, [(sc.nsa_ksT.ap()[blk], 0, 64), (sc.nsa_kwT.ap()[blk], 64, 64)])
        ps, bp = fm(uT, b_u, OFF["ng"], 12)
        p.op("act", lambda e, ps=ps: e.activation(out=gs[:], in_=ps[0:12, :], func=AF.Sigmoid), reads=[bp], writes=[b_gs])
        p.dma("sp", sc.nsa_gsig.ap()[blk], gs[:], c_gs, reads=[b_gs])
        for i in range(2):
            ps, bp = fm(uT, b_u, OFF["dq"] + i * 128, 128)
            evac(ps, bp, 128, 0.125, [(sc.diff_qT.ap()[blk, i * 128:(i + 1) * 128, :], 0, 128)])
        for i in range(2):
            ps, bp = fm(uT, b_u, OFF["dk"] + i * 128, 128)
            evac(ps, bp, 128, 1.0, [(sc.diff_kT.ap()[blk, i * 128:(i + 1) * 128, :], 0, 128)])
        if stage < 5:
            continue
        for t in range(4):
            pv, bpv, _ = psf.next()
            for c in range(8):
                p.op("pe", lambda e, pv=pv, t=t, c=c, uT=uT: e.matmul(pv[:, 0:TM_COLS], lhsT=uT[:, c, t * 128:(t + 1) * 128], rhs=wtm[:, c, :],
                                                               start=(c == 0), stop=(c == 7)), reads=[b_u, b_wtm], writes=[bpv])
            p.op("act", lambda e, pv=pv, t=t: e.copy(out=vtm[:, t, :], in_=pv[:, 0:TM_COLS]), reads=[bpv], writes=[b_vtm])
        p.dma("sp", sc.swa_v.ap()[blk], vtm[:, :, 0:64], c_vtm[0], reads=[b_vtm])
        p.dma("sp", sc.nsa_vs.ap()[blk], vtm[:, :, 64:128], c_vtm[1], reads=[b_vtm])
        p.dma("sp", sc.nsa_vw.ap()[blk], vtm[:, :, 128:192], c_vtm[2], reads=[b_vtm])
        p.dma("sp", sc.diff_v.ap()[blk], vtm[:, :, 192:448], c_vtm[3], reads=[b_vtm])
    for it in p.lists["sp"]:
        pass
    p.final_all_dma = True
    p.build()


CO = np.cumsum([0, 256, 128, 32, 512, 128, 128, 512, 128, 128, 128, 128, 128, 128, 24, 512, 512, 512])
(O_CQ, O_CKV, O_KR, O_SQ, O_SK, O_SV, O_NQ, O_NKC, O_NVC, O_NKS, O_NVS, O_NKW, O_NVW, O_NG, O_DQ, O_DK, O_DV) = [int(v) for v in CO[:-1]]
PERM32 = np.concatenate([np.arange(16, 32), np.arange(0, 16)])


def rope_tables():
    half = 16
    freqs = (10000.0 ** (-np.arange(half, dtype=np.float32) / half)).astype(np.float32)
    pos = np.arange(S, dtype=np.float32)
    ang = pos[:, None] * freqs[None, :]
    cos = np.cos(ang).astype(np.float32).T
    sin = np.sin(ang).astype(np.float32).T
    C = np.ones((96, S), np.float32)
    Sg = np.zeros((96, S), np.float32)
    C[64:80] = cos
    C[80:96] = cos
    Sg[64:80] = -sin
    Sg[80:96] = sin
    C = np.ascontiguousarray(C.reshape(96, 16, TB).transpose(1, 0, 2))
    Sg = np.ascontiguousarray(Sg.reshape(96, 16, TB).transpose(1, 0, 2))
    return C, Sg


def prep_B_weights(inp, l, r):
    w = np.asarray(inp["w_in"][l], np.float32)
    sl = lambda o, n: w[:, o:o + n]
    kr = sl(O_KR, 32)
    fmc = [sl(O_CQ, 256), sl(O_CKV, 128), kr, kr[:, PERM32],
           sl(O_SQ + 256 * r, 256), sl(O_SK + 64 * r, 64),
           sl(O_NQ + 256 * r, 256), sl(O_NKC + 64 * r, 64), sl(O_NVC + 64 * r, 64), sl(O_NKS + 64 * r, 64), sl(O_NKW + 64 * r, 64),
           sl(O_NG + 12 * r, 12), sl(O_DQ + 256 * r, 256), sl(O_DK + 256 * r, 256)]
    win_fm = np.ascontiguousarray(np.concatenate(fmc, 1))
    assert win_fm.shape[1] == FM_COLS
    win_tm = np.ascontiguousarray(np.concatenate([sl(O_SV + 64 * r, 64), sl(O_NVS + 64 * r, 64), sl(O_NVW + 64 * r, 64), sl(O_DV + 256 * r, 256)], 1))
    wuq = np.asarray(inp["mla_w_uq"][l], np.float32).reshape(256, 8, 96)[:, 4 * r:4 * r + 4]
    wuqp = wuq.copy()
    wuqp[:, :, 64:96] = wuq[:, :, 64:96][:, :, PERM32]
    wukv = np.asarray(inp["mla_w_ukv"][l], np.float32).reshape(128, 8, 128)[:, 4 * r:4 * r + 4]
    return dict(win_fm=win_fm, win_tm=win_tm, wuq=np.ascontiguousarray(wuq.reshape(256, 384)),
                wuqp=np.ascontiguousarray(wuqp.reshape(256, 384)),
                wukvk=np.ascontiguousarray(wukv[:, :, 0:64].reshape(128, 256)),
                wukvv=np.ascontiguousarray(wukv[:, :, 64:128].reshape(128, 256)),
                qnT=np.ascontiguousarray(np.asarray(inp["mla_q_norm"][l], np.float32).reshape(2, 128).T),
                kvn=np.ascontiguousarray(np.asarray(inp["mla_kv_norm"][l], np.float32).reshape(128, 1)))


def t5_bucket_np(n):
    n = np.maximum(n, 0)
    nf = np.maximum(n, 1).astype(np.float32)
    large = 16 + (np.log(nf / 16) / math.log(128 / 16) * 16).astype(np.int32)
    large = np.minimum(large, 31)
    return np.where(n < 16, n, large)


def b2_consts():
    c = {}
    c["identb"] = np.eye(128, dtype=np.float32).astype(ml_dtypes.bfloat16)
    c["identf"] = np.eye(128, dtype=np.float32)
    c["Jb"] = np.eye(128, dtype=np.float32)[::-1].copy().astype(ml_dtypes.bfloat16)
    k = np.arange(S)
    c["E"] = (k[None, :] // 64 == np.arange(128)[:, None]).astype(np.float32).astype(ml_dtypes.bfloat16)
    cl = np.arange(128)[:, None]
    ql = np.arange(512)[None, :]
    c["maskC"] = np.stack([np.where(16 * cl + 31 - ql <= 512 * dl, 0.0, NEGM) for dl in range(5)]).astype(np.float32).astype(ml_dtypes.bfloat16)
    d = np.arange(TVL) - 127
    bk = t5_bucket_np(d)
    oh = np.zeros((3, 33, TVL), np.float32)
    for v, hi in enumerate((10 ** 9, 128, 512)):
        ok = (d >= 0) & (d < hi)
        for b in range(32):
            oh[v, b] = ((bk == b) & ok)
        oh[v, 32] = ~ok
    c["OH"] = oh
    qq = np.arange(128)[:, None]
    m = np.arange(256)[None, :] - 126
    cc = (qq >= 64).astype(np.int32)
    c["SA"] = (m < cc - 1).astype(np.float32)
    c["SB"] = np.where((m == cc) | (m == cc - 1), 1e4, np.where(m > cc, -1.0, 0.0)).astype(np.float32)
    cs = np.arange(512) * 16
    ss = np.arange(128) * 64
    ov = ((cs[:, None] < ss[None, :] + 64) & (ss[None, :] < cs[:, None] + 32)).astype(np.float32)
    ov[511] = 0
    ovx = np.concatenate([ov, np.ones((512, 1), np.float32)], 1).reshape(4, 128, 129).transpose(1, 0, 2)
    c["ovx"] = np.ascontiguousarray(ovx).astype(ml_dtypes.bfloat16)
    sel = np.zeros((12, 12 * 64), np.float32)
    for r_ in range(12):
        sel[r_, r_ * 64:(r_ + 1) * 64] = 1
    c["Sel"] = sel
    return c


def prep_B2_inputs(inp, l, r):
    tab = np.asarray(inp["rel_bias_table"], np.float32)
    tabs = np.zeros((33, NKIND), np.float32)
    tabs[:32, 0:4] = tab[:, 8 + 4 * r:8 + 4 * r + 4]
    tabs[:32, 4:6] = tab[:, 16 + 2 * r:16 + 2 * r + 2]
    tabs[:32, 7:11] = tab[:, 4 * r:4 * r + 4]
    tabs[:32, 11:15] = tab[:, 8 + 4 * r:8 + 4 * r + 4]
    tabs[32, :] = NEGM
    d = dict(tabs=tabs,
             sinks=np.ascontiguousarray(np.asarray(inp["swa_sinks"][l], np.float32)[4 * r:4 * r + 4]),
             lamp=np.ascontiguousarray(np.asarray(inp["diff_lambda"][l], np.float32).reshape(1, 256)),
             subln=np.ascontiguousarray(np.asarray(inp["diff_subln"][l], np.float32).reshape(128, 1)),
             w1=np.ascontiguousarray(np.asarray(inp["nsa_cmp_w1"][l], np.float32)),
             w2=np.ascontiguousarray(np.asarray(inp["nsa_cmp_w2"][l], np.float32)),
             pos=np.ascontiguousarray(np.asarray(inp["nsa_cmp_pos"][l], np.float32).reshape(2, 16, 2, 64).transpose(0, 2, 3, 1).reshape(2, 128, 16)))
    return d


def phase_B2(nc, tag, sc, yT_out, cst, bi, lam_init):
    p = Prog(nc, tag)
    S_ = Ring(p, 3, [128, 512], F32, "S", psum=True, chan=False)
    O_ = Ring(p, 2, [128, 512], F32, "O", psum=True, chan=False)
    X_ = Ring(p, 3, [128, 512], F32, "X", psum=True, chan=False)
    P_ = Ring(p, 4, [128, 512], BF16, "P", chan=False)
    KT_ = Ring(p, 4, [96, TB], BF16, "KT")
    V65 = Ring(p, 4, [128, 4, 65], BF16, "V65")
    V128 = Ring(p, 4, [128, 4, 128], BF16, "V128")

    def const(shape, dt, src, eng="sp"):
        t = p.sbuf(shape, dt, "k"); b = Buf(); c = p.chan()
        p.dma(eng, t[:], src, c, writes=[b])
        return t, b
    identb, b_idb = const([128, 128], BF16, cst["identb"])
    identf, b_idf = const([128, 128], F32, cst["identf"])
    Jb, b_J = const([128, 128], BF16, cst["Jb"])
    E, b_E = const([128, S], BF16, cst["E"])
    maskC, b_mC = const([128, 5, 512], BF16, cst["maskC"].rearrange("v p q -> p v q"))
    SA, b_SA = const([128, 256], F32, cst["SA"])
    SB, b_SB = const([128, 256], F32, cst["SB"])
    ovx, b_ov = const([128, 4, 129], BF16, cst["ovx"])
    Sel, b_Sel = const([12, 768], F32, cst["Sel"])
    tabs, b_tabs = const([33, NKIND], F32, bi["tabs"])
    cb, b_cb = const([128, NKIND], F32, bass.AP(bi["tabs"].tensor, 31 * NKIND, [[0, 128], [1, NKIND]]))
    sinke, b_sk = const([128, 4], F32, bass.AP(bi["sinks"].tensor, 0, [[0, 128], [1, 4]]))
    p.op("act", lambda e: e.activation(out=sinke[:], in_=sinke[:], func=AF.Exp), reads=[b_sk], writes=[b_sk])
    subln, b_sub = const([128, 1], F32, bi["subln"])
    p.op("dve", lambda e: e.tensor_scalar(out=subln[:], in0=subln[:], scalar1=1.0 - lam_init, scalar2=None, op0=ALU.mult),
         reads=[b_sub], writes=[b_sub])
    ones_f = p.sbuf([128, 128], F32, "ones"); b_ones = Buf()
    p.op("pool", lambda e: e.memset(ones_f[:], 1.0), writes=[b_ones])
    onesb = p.sbuf([128, 1], BF16, "onesb"); b_onesb = Buf()
    p.op("pool", lambda e: e.memset(onesb[:], 1.0), writes=[b_onesb])
    zb = p.sbuf([128, 1], F32, "zb"); b_zb = Buf()
    p.op("pool", lambda e: e.memset(zb[:], 0.0), writes=[b_zb])
    eps = p.sbuf([128, 1], F32, "eps"); b_eps = Buf()
    p.op("pool", lambda e: e.memset(eps[:], EPS), writes=[b_eps])
    for (t, b, _) in V65.items:
        p.op("pool", lambda e, t=t: e.memset(t[:], 1.0), writes=[b])
    lamp, b_lp = const([1, 256], F32, bi["lamp"])
    lt = p.sbuf([1, 128], F32, "lt"); b_lt = Buf()
    l2 = p.sbuf([1, 4], F32, "l2"); b_l2 = Buf()
    for j in range(2):
        p.op("dve", lambda e, j=j: e.tensor_tensor(out=lt[:, j * 64:(j + 1) * 64], in0=lamp[:, j * 128:j * 128 + 64],
                                                    in1=lamp[:, j * 128 + 64:j * 128 + 128], op=ALU.mult), reads=[b_lp], writes=[b_lt])
        p.op("act", lambda e, j=j: e.activation(out=lt[:, j * 64:(j + 1) * 64], in_=lt[:, j * 64:(j + 1) * 64], func=AF.Identity,
                                                accum_out=l2[:, j:j + 1]), reads=[b_lt], writes=[b_lt, b_l2])
    p.op("act", lambda e: e.activation(out=l2[:, 0:2], in_=l2[:, 0:2], func=AF.Exp), reads=[b_l2], writes=[b_l2])
    p.op("dve", lambda e: e.tensor_tensor(out=l2[:, 2:3], in0=l2[:, 1:2], in1=l2[:, 0:1], op=ALU.subtract), reads=[b_l2], writes=[b_l2])
    p.op("dve", lambda e: e.tensor_scalar(out=l2[:, 3:4], in0=l2[:, 2:3], scalar1=-lam_init, scalar2=None, op0=ALU.add),
         reads=[b_l2], writes=[b_l2])
    neglam = l2[0:1, 3:4]
    OHs = p.sbuf([33, TVL], F32, "OH"); b_OH = Buf(); c_OH = p.chan()
    tvs = p.sbuf([8, TVL], BF16, "tvs"); b_tvs = Buf(); c_tvs = p.chan()
    strips = p.sbuf([128, NKIND, SW], BF16, "strips"); b_st = Buf(); c_st = [p.chan() for _ in range(NKIND)]
    for v, (k0, k1) in enumerate(((0, 7), (7, 11), (11, 15))):
        p.dma("sp", OHs[:], cst["OH"][v], c_OH, writes=[b_OH])
        for cg in range(3):
            ps, bp, _ = X_.next()
            p.op("pe", lambda e, ps=ps, k0=k0, k1=k1, cg=cg: e.matmul(ps[0:k1 - k0, 0:384], lhsT=tabs[:, k0:k1], rhs=OHs[:, cg * 384:(cg + 1) * 384],
                                                                     start=True, stop=True), reads=[b_tabs, b_OH], writes=[bp])
            p.op("dve", lambda e, ps=ps, k0=k0, k1=k1, cg=cg: e.tensor_copy(out=tvs[0:k1 - k0, cg * 384:(cg + 1) * 384], in_=ps[0:k1 - k0, 0:384]),
                 reads=[bp], writes=[b_tvs])
        tk = p.dma("sp", sc.tvec.ap()[k0:k1, :], tvs[0:k1 - k0, :], c_tvs, reads=[b_tvs])
        for kd in range(k0, k1):
            p._wait("sp", tk)
            p.dma("sp", strips[:, kd, :], bass.AP(sc.tvec, kd * TVL, [[1, 128], [1, SW]]), c_st[kd], writes=[])
    st_toks = [("d", ch, 16) for ch in c_st]
    for tk in st_toks:
        p._wait("pe", tk)
    K_SEL, K_DIFF, K_CAUS, K_SWA, K_WIN = 0, 4, 6, 7, 11
    import os as _os
    if _os.environ.get('B2CUT') == '1':
        p.build()
        return

    kc2 = p.sbuf([128, S + 32], BF16, "kc2"); b_kc2 = Buf(); c_kc2 = [p.chan(), p.chan()]
    w1s = p.sbuf([128, 16, 128], BF16, "w1s"); b_w1 = Buf(); c_w1 = p.chan()
    w2s = p.sbuf([128, 64], BF16, "w2s"); b_w2 = Buf(); c_w2 = p.chan()
    posf = p.sbuf([128, 16], F32, "posf"); b_pf = Buf(); c_pf = p.chan()
    posb = p.sbuf([128, 16], BF16, "posb"); b_pb = Buf()
    b1 = p.sbuf([128, 1], F32, "b1"); b_b1 = Buf()
    xs = p.sbuf([128, 512], F32, "xs"); b_xs = Buf()
    x2 = p.sbuf([128, 512], F32, "x2"); b_x2 = Buf()
    hdn = p.sbuf([128, 512], BF16, "hdn"); b_hdn = Buf()
    kcmpT = p.sbuf([64, 512], BF16, "kcmpT"); b_kcm = Buf()
    vcx = p.sbuf([128, 4, 65], BF16, "vcx"); b_vcx = Buf()
    p.op("pool", lambda e: e.memset(vcx[:], 1.0), writes=[b_vcx])
    p.op("pool", lambda e: e.memset(hdn[:], 0.0), writes=[b_hdn])
    p.op("pool", lambda e: e.memset(kc2[:, S - 8:S + 32], 0.0), writes=[b_kc2])
    for kv, src in ((0, sc.nsa_kcT), (1, sc.nsa_vcT)):
        p.dma("sp", kc2[0:64, 0:S], src.ap()[:, 0:S], c_kc2[0], writes=[b_kc2])
        p.dma("sp", kc2[64:128, 0:S - 1], src.ap()[:, 1:S], c_kc2[1], writes=[b_kc2])
        p.dma("pool", w1s[:], bi["w1"][kv].rearrange("(j p) n -> p j n", p=128), c_w1, writes=[b_w1])
        p.dma("pool", w2s[:], bi["w2"][kv], c_w2, writes=[b_w2])
        p.dma("sp", posf[:], bi["pos"][kv], c_pf, writes=[b_pf])
        p.op("dve", lambda e: e.tensor_copy(out=posb[:], in_=posf[:]), reads=[b_pf], writes=[b_pb])
        ps, bp, _ = X_.next()
        for j in range(16):
            p.op("pe", lambda e, ps=ps, j=j: e.matmul(ps[:, 0:1], lhsT=w1s[:, j, :], rhs=posb[:, j:j + 1], start=(j == 0), stop=(j == 15)),
                 reads=[b_w1, b_pb], writes=[bp])
        p.op("dve", lambda e, ps=ps: e.tensor_copy(out=b1[:], in_=ps[:, 0:1]), reads=[bp], writes=[b_b1])
        ph, bph, _ = X_.next()
        for j in range(16):
            p.op("pe", lambda e, ph=ph, j=j: e.matmul(ph[:, 0:511], lhsT=w1s[:, j, :], rhs=kc2[:, 2 * j:2 * j + 16 * 511:16],
                                                      start=(j == 0), stop=(j == 15)), reads=[b_w1, b_kc2], writes=[bph])
        p.op("act", lambda e, ph=ph: e.activation(out=xs[:, 0:511], in_=ph[:, 0:511], func=AF.Identity, bias=b1[:], scale=1.0),
             reads=[bph, b_b1], writes=[b_xs])
        p.op("dve", lambda e: e.tensor_tensor(out=x2[:, 0:511], in0=xs[:, 0:511], in1=xs[:, 0:511], op=ALU.mult), reads=[b_xs], writes=[b_x2])
        p.op("dve", lambda e: e.tensor_scalar(out=x2[:, 0:511], in0=x2[:, 0:511], scalar1=0.044715, scalar2=1.0, op0=ALU.mult, op1=ALU.add),
             reads=[b_x2], writes=[b_x2])
        p.op("dve", lambda e: e.tensor_tensor(out=x2[:, 0:511], in0=x2[:, 0:511], in1=xs[:, 0:511], op=ALU.mult), reads=[b_x2, b_xs], writes=[b_x2])
        p.op("act", lambda e: e.activation(out=x2[:, 0:511], in_=x2[:, 0:511], func=AF.Sigmoid, scale=1.5957691216057308),
             reads=[b_x2], writes=[b_x2])
        p.op("dve", lambda e: e.tensor_tensor(out=hdn[:, 0:511], in0=x2[:, 0:511], in1=xs[:, 0:511], op=ALU.mult), reads=[b_x2, b_xs], writes=[b_hdn])
        if kv == 0:
            pk, bpk, _ = X_.next()
            p.op("pe", lambda e, pk=pk: e.matmul(pk[0:64, :], lhsT=w2s[:], rhs=hdn[:], start=True, stop=True), reads=[b_w2, b_hdn], writes=[bpk])
            p.op("act", lambda e, pk=pk: e.copy(out=kcmpT[:], in_=pk[0:64, :]), reads=[bpk], writes=[b_kcm])
        else:
            for ct in range(4):
                pk, bpk, _ = X_.next()
                p.op("pe", lambda e, pk=pk, ct=ct: e.matmul(pk[:, 0:64], lhsT=hdn[:, ct * 128:(ct + 1) * 128], rhs=w2s[:], start=True, stop=True),
                     reads=[b_w2, b_hdn], writes=[bpk])
                p.op("act", lambda e, pk=pk, ct=ct: e.copy(out=vcx[:, ct, 0:64], in_=pk[:, 0:64]), reads=[bpk], writes=[b_vcx])

    if _os.environ.get('B2CUT') == '2':
        p.build()
        return
    qm = p.sbuf([96, 4, TB], BF16, "qm"); b_qm = Buf(); c_qm = p.chan()
    qs = p.sbuf([64, 4, TB], BF16, "qs"); b_qs = Buf(); c_qs = p.chan()
    qn_ = p.sbuf([64, 4, TB], BF16, "qn"); b_qn = Buf(); c_qn = p.chan()
    qd = p.sbuf([64, 4, TB], BF16, "qd"); b_qd = Buf(); c_qd = p.chan()
    gsg = p.sbuf([12, TB], F32, "gsg"); b_gsg = Buf(); c_gsg = p.chan()
    rd = p.sbuf([65, TB], F32, "rd"); b_rd = Buf()
    osb = Ring(p, 2, [128, TB], F32, "osb", chan=False)
    ost = Ring(p, 3, [128, TB], BF16, "ost")
    p_acc = [(p.sbuf([64, TB], F32, "acc"), Buf()) for _ in range(4)]
    tmpn = p.sbuf([64, TB], F32, "tmpn"); b_tmpn = Buf()
    d1 = p.sbuf([128, TB], F32, "d1"); b_d1 = Buf()
    d2 = p.sbuf([128, TB], F32, "d2"); b_d2 = Buf()
    dsq = p.sbuf([128, TB], F32, "dsq"); b_dsq = Buf()
    drs = p.sbuf([128, TB], F32, "drs"); b_drs = Buf()
    impa = p.sbuf([128, 4, 128], F32, "impa"); b_impa = Buf()
    rdi = p.sbuf([128, 8], F32, "rdi"); b_rdi = Buf()
    top = p.sbuf([128, 16], F32, "top"); b_top = Buf()
    wk = p.sbuf([128, 128], F32, "wk"); b_wk = Buf()
    mq = p.sbuf([128, 128], F32, "mq"); b_mq = Buf()
    MT = p.sbuf([128, TB], BF16, "MT"); b_MT = Buf()
    SGC = True

    def mm(out, lhsT, rhs, start, stop):
        return lambda e: e.matmul(out, lhsT=lhsT, rhs=rhs, start=start, stop=stop, skip_group_check=SGC)

    def attn(qT, b_q, tiles, scale, ops, bo, M, den=None, after=None, n=None):
        if n is None:
            tiles = list(tiles)
            n = len(tiles)

        def stage1(tl):
            q0 = tl["q0"]
            sp_, bs, _ = S_.next()
            ex = tl.get("ex", [])
            p.op("pe", mm(sp_[:, q0:], tl["kT"], qT[:, q0:], True, not ex), reads=[b_q] + tl["kb"], writes=[bs])
            for j, (l_, r_, bufs) in enumerate(ex):
                p.op("pe", mm(sp_[:, q0:], l_, r_, False, j == len(ex) - 1), reads=bufs, writes=[bs])
            P, bP, _ = P_.next()
            bias = tl.get("bias", zb[:, 0:1])
            p.op("act", lambda e, P=P, sp_=sp_, q0=q0, bias=bias: e.activation(out=P[:, q0:], in_=sp_[:, q0:], func=AF.Exp, bias=bias, scale=scale),
                 reads=[bs, b_zb, b_cb], writes=[bP])
            return P, bP

        def stage2(i, tl, P, bP):
            q0 = tl["q0"]
            p.op("pe", mm(ops[0:M, q0:], tl["v"], P[:, q0:], i == 0, i == n - 1), reads=[bP] + tl["vb"], writes=[bo])
            if den is not None:
                p.op("pe", mm(den[0][0:1, q0:], onesb[:, 0:1], P[:, q0:], i == 0, i == n - 1), reads=[bP, b_onesb], writes=[den[1]])
            if after is not None:
                after(P, bP, tl, i, n)

        prev = None
        for i, tl in enumerate(tiles):
            P, bP = stage1(tl)
            if prev is not None:
                stage2(*prev)
            prev = (i, tl, P, bP)
        if prev is not None:
            stage2(*prev)

    def load_kv(kT_src, dk, v_src, ring, dv):
        kt, bk, ck = KT_.next()
        p.dma("sp", kt[0:dk, :], kT_src, ck, writes=[bk])
        vt, bv, cv = ring.next()
        p.dma("sp", vt[:, :, 0:dv], v_src, cv, writes=[bv])
        return kt, bk, vt, bv

    def normalize(ops, bo, dv, den_ps, bden, dp, sink=None, mulneg=False, dest=None, bdest=None):
        p.op("dve", lambda e: e.tensor_scalar(out=rd[dp:dp + 1, :], in0=den_ps[dp:dp + 1, :], scalar1=(sink if sink is not None else 0.0),
                                              scalar2=1e-30, op0=ALU.add, op1=ALU.max), reads=[bden, b_sk], writes=[b_rd])
        p.op("dve", lambda e: e.reciprocal(out=rd[dp:dp + 1, :], in_=rd[dp:dp + 1, :]), reads=[b_rd], writes=[b_rd])
        if mulneg:
            p.op("dve", lambda e: e.tensor_scalar(out=rd[dp:dp + 1, :], in0=rd[dp:dp + 1, :], scalar1=neglam, scalar2=None, op0=ALU.mult),
                 reads=[b_rd, b_l2], writes=[b_rd])
        bc, bbc, _ = X_.next()
        p.op("pe", lambda e: e.matmul(bc[0:dv, :], lhsT=ones_f[dp:dp + 1, 0:dv], rhs=rd[dp:dp + 1, :], start=True, stop=True),
             reads=[b_ones, b_rd], writes=[bbc])
        o, bo_, _ = osb.next()
        p.op("act", lambda e: e.copy(out=o[0:dv, :], in_=ops[0:dv, :]), reads=[bo], writes=[bo_])
        p.op("dve", lambda e: e.tensor_tensor(out=dest[0:dv, :], in0=o[0:dv, :], in1=bc[0:dv, :], op=ALU.mult), reads=[bo_, bbc], writes=[bdest])

    import os as _os
    for qb in range(int(_os.environ.get('NQB', '16'))):
        p.dma("sp", qm[:], sc.mla_qT.ap()[qb].rearrange("h d t -> d h t"), c_qm, writes=[b_qm])
        p.dma("sp", qs[:], sc.swa_qT.ap()[qb].rearrange("(h d) t -> d h t", d=64), c_qs, writes=[b_qs])
        p.dma("sp", qn_[:], sc.nsa_qT.ap()[qb].rearrange("(h d) t -> d h t", d=64), c_qn, writes=[b_qn])
        p.dma("sp", qd[:], sc.diff_qT.ap()[qb].rearrange("(h d) t -> d h t", d=64), c_qd, writes=[b_qd])
        p.dma("sp", gsg[:], sc.nsa_gsig.ap()[qb], c_gsg, writes=[b_gsg])

        def causal_tiles(kT_of, dk, v_of, ring, dv, strip_kind, far_bias, near_prev):
            nxt = load_kv(kT_of(0), dk, v_of(0), ring, dv)
            for kb in range(qb + 1):
                kt, bk, vt, bv = nxt
                if kb < qb:
                    nxt = load_kv(kT_of(kb + 1), dk, v_of(kb + 1), ring, dv)
                for t4 in range(4):
                    tl = dict(kT=kt[0:dk, t4 * 128:(t4 + 1) * 128], kb=[bk], v=vt[:, t4, :], vb=[bv], q0=0)
                    if kb == qb:
                        tl["q0"] = 128 * t4
                        tl["ex"] = [(Jb[:], strips[:, strip_kind, 0:512 - 128 * t4], [b_J])]
                    elif near_prev and kb == qb - 1 and t4 == 3:
                        tl["ex"] = [(Jb[:], strips[:, strip_kind, 128:640], [b_J])]
                    elif far_bias is not None:
                        tl["bias"] = far_bias
                    yield tl

        for h in range(4):
            ops, bo, _ = O_.next()
            tiles = causal_tiles(lambda kb: sc.mla_kT.ap()[kb, h], 96, lambda kb: sc.mla_v.ap()[kb][:, :, h * 64:(h + 1) * 64], V65, 64, K_CAUS, None, False)
            attn(qm[:, h, :], b_qm, tiles, 96 ** -0.5, ops, bo, 65, n=4 * (qb + 1))
            yt, byt, cyt = ost.next()
            normalize(ops, bo, 64, ops, bo, 64, dest=yt, bdest=byt)
            p.dma("sp", yT_out.ap()[qb, 0, h * 64:(h + 1) * 64, :], yt[0:64, :], cyt, reads=[byt], final=True)
        kbs = ([qb - 1] if qb > 0 else []) + [qb]
        kvl = {kb: load_kv(sc.swa_kT.ap()[kb], 64, sc.swa_v.ap()[kb], V65, 64) for kb in kbs}
        for g in range(4):
            tiles = []
            if qb > 0:
                kt, bk, vt, bv = kvl[qb - 1]
                tiles.append(dict(kT=kt[0:64, 384:512], kb=[bk], v=vt[:, 3, :], vb=[bv], q0=0, ex=[(Jb[:], strips[:, K_SWA + g, 128:640], [b_J])]))
            kt, bk, vt, bv = kvl[qb]
            for t4 in range(4):
                tiles.append(dict(kT=kt[0:64, t4 * 128:(t4 + 1) * 128], kb=[bk], v=vt[:, t4, :], vb=[bv], q0=128 * t4,
                                  ex=[(Jb[:], strips[:, K_SWA + g, 0:512 - 128 * t4], [b_J])]))
            if qb == 0:
                tiles[0]["q0"] = 0
            ops, bo, _ = O_.next()
            attn(qs[:, g, :], b_qs, tiles, 1.0, ops, bo, 65)
            yt, byt, cyt = ost.next()
            normalize(ops, bo, 64, ops, bo, 64, sink=sinke[64:65, g:g + 1], dest=yt, bdest=byt)
            p.dma("sp", yT_out.ap()[qb, 1, g * 64:(g + 1) * 64, :], yt[0:64, :], cyt, reads=[byt], final=True)
        for h in range(2):
            for m_ in range(2):
                ops, bo, _ = O_.next()
                dps, bd, _ = X_.next()
                tiles = causal_tiles(lambda kb: sc.diff_kT.ap()[kb, (2 * h + m_) * 64:(2 * h + m_ + 1) * 64, :], 64,
                                     lambda kb: sc.diff_v.ap()[kb][:, :, h * 128:(h + 1) * 128], V128, 128, K_DIFF + h, cb[:, K_DIFF + h:K_DIFF + h + 1], True)
                attn(qd[:, 2 * h + m_, :], b_qd, tiles, 1.0, ops, bo, 128, den=(dps, bd), n=4 * (qb + 1))
                normalize(ops, bo, 128, dps, bd, 0, mulneg=(m_ == 1), dest=(d1 if m_ == 0 else d2), bdest=(b_d1 if m_ == 0 else b_d2))
            p.op("pool", lambda e: e.tensor_tensor(out=d1[:], in0=d1[:], in1=d2[:], op=ALU.add), reads=[b_d1, b_d2], writes=[b_d1])
            p.op("act", lambda e: e.activation(out=dsq[:], in_=d1[:], func=AF.Square), reads=[b_d1], writes=[b_dsq])
            ss, bss, _ = X_.next()
            p.op("pe", lambda e, ss=ss: e.matmul(ss[:], lhsT=ones_f[:], rhs=dsq[:], start=True, stop=True), reads=[b_ones, b_dsq], writes=[bss])
            p.op("act", lambda e, ss=ss: e.activation(out=drs[:], in_=ss[:], func=AF.Sqrt, bias=eps[:], scale=1.0 / 128), reads=[bss, b_eps], writes=[b_drs])
            p.op("dve", lambda e: e.reciprocal(out=drs[:], in_=drs[:]), reads=[b_drs], writes=[b_drs])
            yt, byt, cyt = ost.next()
            p.op("dve", lambda e, yt=yt: e.scalar_tensor_tensor(out=yt[:], in0=d1[:], scalar=subln[:, 0:1], in1=drs[:], op0=ALU.mult, op1=ALU.mult),
                 reads=[b_d1, b_sub, b_drs], writes=[byt])
            p.dma("sp", yT_out.ap()[qb, 3, h * 128:(h + 1) * 128, :], yt[:], cyt, reads=[byt], final=True)
        p.op("pool", lambda e: e.memset(impa[:], 0.0), writes=[b_impa])
        cts = [ct for ct in range(4) if qb - 4 * ct >= 0]
        oc_keep = []
        for g in range(4):
            tiles = []
            for ct in cts:
                dl = qb - 4 * ct
                tl = dict(kT=kcmpT[:, ct * 128:(ct + 1) * 128], kb=[b_kcm], v=vcx[:, ct, :], vb=[b_vcx], q0=0, ct=ct)
                if dl <= 4:
                    tl["ex"] = [(identb[:], maskC[:, dl, :], [b_idb, b_mC])]
                tiles.append(tl)
            ops, bo, _ = O_.next()
            ips = [X_.next(), X_.next()]
            for ip, bip, _ in ips:
                p.op("dve", lambda e, ip=ip: e.memset(ip[:], 0.0), writes=[bip])

            def after(P, bP, tl, i, n, ips=ips):
                for t4 in range(4):
                    ip, bip, _ = ips[t4 // 2]
                    p.op("pe", mm(ip[:, (t4 % 2) * 129:(t4 % 2) * 129 + 129], P[:, t4 * 128:(t4 + 1) * 128], ovx[:, tl["ct"], :], False, i == n - 1),
                         reads=[bP, b_ov], writes=[bip])
            attn(qn_[:, g, :], b_qn, tiles, 1.0, ops, bo, 65, after=after)
            for t4 in range(4):
                ip, bip, _ = ips[t4 // 2]
                o_ = (t4 % 2) * 129
                p.op("dve", lambda e, ip=ip, o_=o_, t4=t4: e.tensor_scalar(out=rdi[:, t4:t4 + 1], in0=ip[:, o_ + 128:o_ + 129], scalar1=1e-30, scalar2=None, op0=ALU.max),
                     reads=[bip], writes=[b_rdi])
                p.op("dve", lambda e, t4=t4: e.reciprocal(out=rdi[:, t4:t4 + 1], in_=rdi[:, t4:t4 + 1]), reads=[b_rdi], writes=[b_rdi])
                p.op("dve", lambda e, ip=ip, o_=o_, t4=t4: e.scalar_tensor_tensor(out=impa[:, t4, :], in0=ip[:, o_:o_ + 128], scalar=rdi[:, t4:t4 + 1],
                                                                                  in1=impa[:, t4, :], op0=ALU.mult, op1=ALU.add),
                     reads=[bip, b_rdi, b_impa], writes=[b_impa])
            oc_keep.append((ops, bo))
            gb_, bgb, _ = X_.next()
            p.op("pe", lambda e, gb_=gb_, g=g: e.matmul(gb_[0:64, :], lhsT=Sel[:, (g * 3) * 64:(g * 3 + 1) * 64], rhs=gsg[:], start=True, stop=True),
                 reads=[b_Sel, b_gsg], writes=[bgb])
            normalize(ops, bo, 64, ops, bo, 64, dest=tmpn, bdest=b_tmpn)
            accg = p_acc[g]
            p.op("dve", lambda e, gb_=gb_, accg=accg: e.tensor_tensor(out=accg[0][:], in0=tmpn[:], in1=gb_[0:64, :], op=ALU.mult),
                 reads=[b_tmpn, bgb], writes=[accg[1]])
        for t4 in range(4):
            T = 4 * qb + t4
            c0 = 126 - 2 * T
            p.op("dve", lambda e, t4=t4, c0=c0: e.tensor_tensor(out=impa[:, t4, :], in0=impa[:, t4, :], in1=SA[:, c0:c0 + 128], op=ALU.mult),
                 reads=[b_impa, b_SA], writes=[b_impa])
            p.op("dve", lambda e, t4=t4, c0=c0: e.tensor_tensor(out=impa[:, t4, :], in0=impa[:, t4, :], in1=SB[:, c0:c0 + 128], op=ALU.add),
                 reads=[b_impa, b_SB], writes=[b_impa])
            p.op("dve", lambda e, t4=t4: e.memset(impa[:, t4, 0:1], 1e4), reads=[], writes=[b_impa])
            p.op("dve", lambda e, t4=t4: e.max(out=top[:, 0:8], in_=impa[:, t4, :]), reads=[b_impa], writes=[b_top])
            p.op("dve", lambda e, t4=t4: e.match_replace(out=wk[:], in_to_replace=top[:, 0:8], in_values=impa[:, t4, :], imm_value=-1e30),
                 reads=[b_impa, b_top], writes=[b_wk])
            p.op("dve", lambda e: e.max(out=top[:, 8:16], in_=wk[:]), reads=[b_wk], writes=[b_top])
            p.op("dve", lambda e, t4=t4: e.tensor_scalar(out=mq[:], in0=impa[:, t4, :], scalar1=top[:, 15:16], scalar2=1.0, op0=ALU.is_ge, op1=ALU.subtract),
                 reads=[b_impa, b_top], writes=[b_mq])
            p.op("dve", lambda e: e.tensor_scalar(out=mq[:], in0=mq[:], scalar1=-NEGM, scalar2=None, op0=ALU.mult), reads=[b_mq], writes=[b_mq])
            tp, btp, _ = X_.next()
            p.op("pe", lambda e, tp=tp: e.transpose(tp[:, 0:128], mq[:], identf[:]), reads=[b_mq, b_idf], writes=[btp])
            p.op("act", lambda e, tp=tp, t4=t4: e.copy(out=MT[:, t4 * 128:(t4 + 1) * 128], in_=tp[:, 0:128]), reads=[btp], writes=[b_MT])
        kvs = [load_kv(sc.nsa_ksT.ap()[kb], 64, sc.nsa_vs.ap()[kb], V65, 64) for kb in range(0)]
        for g in range(4):
            accg = p_acc[g]
            for br in (1, 2):
                tiles = []
                nt = None
                if br == 1:
                    def sel_tiles(g=g):
                        nxt = load_kv(sc.nsa_ksT.ap()[0], 64, sc.nsa_vs.ap()[0], V65, 64)
                        for kb in range(qb + 1):
                            kt, bk, vt, bv = nxt
                            if kb < qb:
                                nxt = load_kv(sc.nsa_ksT.ap()[kb + 1], 64, sc.nsa_vs.ap()[kb + 1], V65, 64)
                            for t4 in range(4):
                                KT = kb * 4 + t4
                                tl = dict(kT=kt[0:64, t4 * 128:(t4 + 1) * 128], kb=[bk], v=vt[:, t4, :], vb=[bv], q0=0)
                                q0 = 128 * t4 if kb == qb else 0
                                tl["q0"] = q0
                                tl["ex"] = [(E[:, KT * 128:(KT + 1) * 128], MT[:, q0:], [b_E, b_MT])]
                                if kb == qb:
                                    tl["ex"].append((Jb[:], strips[:, K_SEL + g, 0:512 - q0], [b_J]))
                                elif kb == qb - 1 and t4 == 3:
                                    tl["ex"].append((Jb[:], strips[:, K_SEL + g, 128:640], [b_J]))
                                else:
                                    tl["bias"] = cb[:, K_SEL + g:K_SEL + g + 1]
                                yield tl
                    tiles = sel_tiles()
                    nt = 4 * (qb + 1)
                else:
                    if qb > 0:
                        kt, bk, vt, bv = load_kv(sc.nsa_kwT.ap()[qb - 1], 64, sc.nsa_vw.ap()[qb - 1], V65, 64)
                        for t4 in range(4):
                            rel = 512 - 128 * t4
                            tiles.append(dict(kT=kt[0:64, t4 * 128:(t4 + 1) * 128], kb=[bk], v=vt[:, t4, :], vb=[bv], q0=0,
                                              ex=[(Jb[:], strips[:, K_WIN + g, rel:rel + 512], [b_J])]))
                    kt, bk, vt, bv = load_kv(sc.nsa_kwT.ap()[qb], 64, sc.nsa_vw.ap()[qb], V65, 64)
                    for t4 in range(4):
                        tiles.append(dict(kT=kt[0:64, t4 * 128:(t4 + 1) * 128], kb=[bk], v=vt[:, t4, :], vb=[bv], q0=128 * t4,
                                          ex=[(Jb[:], strips[:, K_WIN + g, 0:512 - 128 * t4], [b_J])]))
                ops, bo, _ = O_.next()
                attn(qn_[:, g, :], b_qn, tiles, 1.0, ops, bo, 65, n=nt)
                gb_, bgb, _ = X_.next()
                p.op("pe", lambda e, gb_=gb_, g=g, br=br: e.matmul(gb_[0:64, :], lhsT=Sel[:, (g * 3 + br) * 64:(g * 3 + br + 1) * 64], rhs=gsg[:], start=True, stop=True),
                     reads=[b_Sel, b_gsg], writes=[bgb])
                normalize(ops, bo, 64, ops, bo, 64, dest=tmpn, bdest=b_tmpn)
                p.op("dve", lambda e, gb_=gb_: e.tensor_tensor(out=tmpn[:], in0=tmpn[:], in1=gb_[0:64, :], op=ALU.mult), reads=[b_tmpn, bgb], writes=[b_tmpn])
                p.op("pool", lambda e, accg=accg: e.tensor_tensor(out=accg[0][:], in0=accg[0][:], in1=tmpn[:], op=ALU.add), reads=[accg[1], b_tmpn], writes=[accg[1]])
            yt, byt, cyt = ost.next()
            p.op("act", lambda e, yt=yt, accg=accg: e.copy(out=yt[0:64, :], in_=accg[0][:]), reads=[accg[1]], writes=[byt])
            p.dma("sp", yT_out.ap()[qb, 2, g * 64:(g + 1) * 64, :], yt[0:64, :], cyt, reads=[byt], final=True)
    p.build()


def build_launch_B(l):
    nc = bass.Bass("TRN2", target_bir_lowering=False)
    E_ = lambda n, s, dt=F32: nc.dram_tensor(n, list(s), dt, kind="ExternalInput")
    uT_in = E_("uT_in", [16, 128, 8, TB], BF16)
    shp = dict(win_fm=[D, FM_COLS], win_tm=[D, TM_COLS], wuq=[256, 384], wuqp=[256, 384], wukvk=[128, 256], wukvv=[128, 256],
               qnT=[128, 2], kvn=[128, 1])
    a = {k: E_(k, v).ap() for k, v in shp.items()}
    ropeC = E_("ropeC", [16, 96, TB]).ap()
    ropeS = E_("ropeS", [16, 96, TB]).ap()
    cshp = dict(identb=([128, 128], BF16), identf=([128, 128], F32), Jb=([128, 128], BF16), E=([128, S], BF16), maskC=([5, 128, 512], BF16),
                OH=([3, 33, TVL], F32), SA=([128, 256], F32), SB=([128, 256], F32), ovx=([128, 4, 129], BF16), Sel=([12, 768], F32))
    cst = {k: E_("k_" + k, v[0], v[1]).ap() for k, v in cshp.items()}
    bshp = dict(tabs=[33, NKIND], sinks=[4], lamp=[1, 256], subln=[128, 1], w1=[2, 2048, 128], w2=[2, 128, 64], pos=[2, 128, 16])
    bi = {k: E_("b_" + k, v).ap() for k, v in bshp.items()}
    yT_out = nc.dram_tensor("yT_out", [16, 4, 256, TB], BF16, kind="ExternalOutput")
    sc = Scratch(nc, "sc_")
    import os as _os
    if _os.environ.get('SKIPB1') != '1':
      phase_B1(nc, "B1", sc, uT_in, a["win_fm"], a["win_tm"], a["wuq"], a["wuqp"], a["wukvk"], a["wukvv"], a["qnT"], a["kvn"], ropeC, ropeS)
    lam_init = 0.8 - 0.6 * math.exp(-0.3 * l)
    phase_B2(nc, "B2", sc, yT_out, cst, bi, lam_init)
    return nc


_CACHE = {}


def _get(kind, arg=None):
    key = (kind, arg)
    if key not in _CACHE:
        if kind == "A":
            _CACHE[key] = build_launch_A()
        elif kind == "B":
            _CACHE[key] = build_launch_B(arg)
        else:
            _CACHE[key] = build_launch_C(arg)
    return _CACHE[key]


def _bf(x):
    return np.ascontiguousarray(x)


def kernel_unfused(**inp):
    x = np.asarray(inp["x"], np.float32)
    cores = list(range(8))
    h = [np.ascontiguousarray(x[c // 2, (c % 2) * 4096:(c % 2 + 1) * 4096]) for c in cores]
    C_, S_g = rope_tables()
    consts = b2_consts()
    for l in range(2):
        maps = []
        for c in cores:
            maps.append(dict(h_in=h[c], gT0=gT_layout(inp["norm_g"][l, 0]), gT1=gT_layout(inp["norm_g"][l, 1]),
                             wg=np.asarray(inp["ffn_w_gate"][l, 0], np.float32), wu=np.asarray(inp["ffn_w_up"][l, 0], np.float32),
                             wd=np.asarray(inp["ffn_w_down"][l, 0], np.float32)))
        res = run_bass_kernel_spmd(_get("A"), maps, core_ids=cores).results
        h = [np.asarray(r["h_out"]) for r in res]
        uT = [np.asarray(r["uT_out"]) for r in res]
        maps = []
        for c in cores:
            s_, r_ = c // 2, c % 2
            m = dict(uT_in=np.ascontiguousarray(np.concatenate([uT[2 * s_], uT[2 * s_ + 1]], 0)), ropeC=C_, ropeS=S_g)
            m.update(prep_B_weights(inp, l, r_))
            m.update({'k_' + k: v for k, v in consts.items()})
            m.update({'b_' + k: v for k, v in prep_B2_inputs(inp, l, r_).items()})
            maps.append(m)
        res = run_bass_kernel_spmd(_get("B", l), maps, core_ids=cores).results
        yT = [np.asarray(r["yT_out"]) for r in res]
        maps = []
        for c in cores:
            s_, r_ = c // 2, c % 2
            y_all = np.concatenate([yT[2 * s_][8 * r_:8 * r_ + 8], yT[2 * s_ + 1][8 * r_:8 * r_ + 8]], 2)
            y_all = y_all.reshape(8, 4, 4, 128, TB).transpose(0, 3, 1, 2, 4).reshape(8, 128, 16, TB)
            m = dict(h_in=h[c], uT_in=uT[c], yT_in=np.ascontiguousarray(y_all), gT2=gT_layout(inp["norm_g"][l, 2]),
                     wgate=np.asarray(inp["w_gate"][l], np.float32), wbr=np.asarray(inp["w_branch"][l], np.float32),
                     wo=np.asarray(inp["w_o"][l], np.float32), wg=np.asarray(inp["ffn_w_gate"][l, 1], np.float32),
                     wu=np.asarray(inp["ffn_w_up"][l, 1], np.float32), wd=np.asarray(inp["ffn_w_down"][l, 1], np.float32))
            if l == 1:
                m["gfin"] = np.asarray(inp["final_g"], np.float32)
            maps.append(m)
        res = run_bass_kernel_spmd(_get("C", l == 1), maps, core_ids=cores).results
        h = [np.asarray(r["h_out"]) for r in res]
    out = np.zeros((NB, S, D), np.float32)
    for c in cores:
        out[c // 2, (c % 2) * 4096:(c % 2 + 1) * 4096] = h[c]
    return out


PAIRS = [[0, 1], [2, 3], [4, 5], [6, 7]]
A_SHP = dict(gT0=[128, 8], gT1=[128, 8], gT2=[128, 8], wg1=[D, DFF], wu1=[D, DFF], wd1=[DFF, D], wg2=[D, DFF], wu2=[D, DFF], wd2=[DFF, D],
             wgate=[4, D, D], wbr=[4, 512, D], wo=[D, D])
B_SHP = dict(win_fm=[D, FM_COLS], win_tm=[D, TM_COLS], wuq=[256, 384], wuqp=[256, 384], wukvk=[128, 256], wukvv=[128, 256],
             qnT=[128, 2], kvn=[128, 1])
B2_SHP = dict(tabs=[33, NKIND], sinks=[4], lamp=[1, 256], subln=[128, 1], w1=[2, 2048, 128], w2=[2, 128, 64], pos=[2, 128, 16])
C_SHP = dict(identb=([128, 128], BF16), identf=([128, 128], F32), Jb=([128, 128], BF16), E=([128, S], BF16), maskC=([5, 128, 512], BF16),
             OH=([3, 33, TVL], F32), SA=([128, 256], F32), SB=([128, 256], F32), ovx=([128, 4, 129], BF16), Sel=([12, 768], F32))


def build_fused():
    nc = bass.Bass("TRN2", target_bir_lowering=False)
    E_ = lambda n, s, dt=F32: nc.dram_tensor(n, list(s), dt, kind="ExternalInput")
    x_in = E_("x_in", [4096, D]).ap()
    par = E_("par", [128, TB]).ap()
    gfin = E_("gfin", [D]).ap()
    ropeC = E_("ropeC", [16, 96, TB]).ap()
    ropeS = E_("ropeS", [16, 96, TB]).ap()
    cst = {k: E_("k_" + k, v[0], v[1]).ap() for k, v in C_SHP.items()}
    out = nc.dram_tensor("out", [4096, D], F32, kind="ExternalOutput").ap()
    hA = dram(nc, "hA", [4096, D], F32).ap()
    hC = dram(nc, "hC", [4096, D], F32).ap()
    uT_mine = dram(nc, "uT_mine", [8, 128, 8, TB], BF16)
    uT_full = dram(nc, "uT_full", [4, 2, 2, 128, 8, TB], BF16)
    yT_mine = dram(nc, "yT_mine", [16, 4, 256, TB], BF16)
    yT_full = dram(nc, "yT_full", [8, 2, 2, 4, 256, TB], BF16)
    sc = Scratch(nc, "sc_")
    h_src = x_in
    import os as _os
    fcut = int(_os.environ.get("FCUT", "99"))
    nph = [0]

    def go():
        nph[0] += 1
        return nph[0] <= fcut
    for l in range(2):
        a = {k: E_("l%d_%s" % (l, k), v).ap() for k, v in A_SHP.items()}
        b = {k: E_("l%d_%s" % (l, k), v).ap() for k, v in B_SHP.items()}
        bi = {k: E_("l%d_b_%s" % (l, k), v).ap() for k, v in B2_SHP.items()}
        t = "L%d" % l
        if go():
            phase_A(nc, t + "A", h_src, hA, uT_mine, a["gT0"], a["gT1"], a["wg1"], a["wu1"], a["wd1"], 8)
        p = Prog(nc, t + "X1")
        if _os.environ.get("NOCC") != "1" and go():
          for j in range(4):
            p.collective("AllGather", PAIRS, uT_mine.ap()[2 * j:2 * j + 2].rearrange("b p c t -> (b p) (c t)"),
                         uT_full.ap()[j].rearrange("r b p c t -> (r b p) (c t)"))
        p.build()
        if go():
          phase_B1(nc, t + "B1", sc, (lambda bb: uT_full.ap()[(bb % 8) // 2, bb // 8, (bb % 8) % 2]), b["win_fm"], b["win_tm"], b["wuq"], b["wuqp"], b["wukvk"], b["wukvv"], b["qnT"], b["kvn"], ropeC, ropeS)
        if go():
            phase_B2(nc, t + "B2", sc, yT_mine, cst, bi, 0.8 - 0.6 * math.exp(-0.3 * l))
        p = Prog(nc, t + "X2")
        if _os.environ.get("NOCC") != "1" and go():
          for j in range(8):
            p.collective("AllGather", PAIRS, yT_mine.ap()[2 * j:2 * j + 2].rearrange("b i f t -> (b i f) t"),
                         yT_full.ap()[j].rearrange("r b i f t -> (r b i f) t"))
        p.build()
        if go():
          phase_C(nc, t + "C", hA, out if l == 1 else hC, uT_mine, None, a["gT2"], a["wgate"], a["wbr"], a["wo"], a["wg2"], a["wu2"], a["wd2"], 8,
                gfin_d=gfin if l == 1 else None, ygath=yT_full, par_d=par)
        h_src = hC
    return nc


def kernel(**inp):
    x = np.asarray(inp["x"], np.float32)
    cores = list(range(8))
    C_, S_g = rope_tables()
    consts = b2_consts()
    f32 = lambda a: np.ascontiguousarray(np.asarray(a, np.float32))
    shared = dict(gfin=f32(inp["final_g"]), ropeC=C_, ropeS=S_g)
    shared.update({"k_" + k: v for k, v in consts.items()})
    for l in range(2):
        shared.update({"l%d_gT%d" % (l, i): gT_layout(inp["norm_g"][l, i]) for i in range(3)})
        for j, nm in ((0, "1"), (1, "2")):
            shared["l%d_wg%s" % (l, nm)] = f32(inp["ffn_w_gate"][l, j])
            shared["l%d_wu%s" % (l, nm)] = f32(inp["ffn_w_up"][l, j])
            shared["l%d_wd%s" % (l, nm)] = f32(inp["ffn_w_down"][l, j])
        shared["l%d_wgate" % l] = f32(inp["w_gate"][l])
        shared["l%d_wbr" % l] = f32(inp["w_branch"][l])
        shared["l%d_wo" % l] = f32(inp["w_o"][l])
    perpar = []
    for r_ in range(2):
        d = {}
        for l in range(2):
            d.update({"l%d_%s" % (l, k): v for k, v in prep_B_weights(inp, l, r_).items()})
            d.update({"l%d_b_%s" % (l, k): v for k, v in prep_B2_inputs(inp, l, r_).items()})
        d["par"] = np.full((128, TB), float(r_), np.float32)
        perpar.append(d)
    maps = []
    for c in cores:
        m = dict(shared)
        m.update(perpar[c % 2])
        m["x_in"] = np.ascontiguousarray(x[c // 2, (c % 2) * 4096:(c % 2 + 1) * 4096])
        maps.append(m)
    if "F" not in _CACHE:
        _CACHE["F"] = build_fused()
    res = run_bass_kernel_spmd(_CACHE["F"], maps, core_ids=cores).results
    out = np.zeros((NB, S, D), np.float32)
    for c in cores:
        out[c // 2, (c % 2) * 4096:(c % 2 + 1) * 4096] = np.asarray(res[c]["out"])
    return out
```

```python
import contextlib
import math
import numpy as np
import ml_dtypes
import concourse.bass as bass
import concourse.mybir as mybir
from concourse.bass_utils import run_bass_kernel_spmd

F32 = mybir.dt.float32
BF16 = mybir.dt.bfloat16
AF = mybir.ActivationFunctionType
ALU = mybir.AluOpType

D = 1024
S = 8192
NB = 4
DFF = 2816
NF = 22
EPS = 1e-6
TB = 512
NEGM = -30000.0


class Buf:
    __slots__ = ("lw", "rd", "excl")

    def __init__(self, excl=False):
        self.lw = None
        self.rd = []
        self.excl = excl


class Chan:
    __slots__ = ("sem", "n")

    def __init__(self, sem):
        self.sem = sem
        self.n = 0


class Prog:
    ENGS = ("pe", "act", "dve", "pool", "sp")

    def __init__(self, nc, tag):
        self.nc = nc
        self.tag = tag
        self.stack = contextlib.ExitStack()
        self.lists = {e: [] for e in self.ENGS}
        self.nops = {e: 0 for e in self.ENGS}
        self.waited = {e: {} for e in self.ENGS}
        self.needed = {e: set() for e in self.ENGS}
        self.sems = []
        self.esem = {e: self._sem(tag + "s_" + e) for e in self.ENGS}
        self.nchan = 0
        self.out_toks = []
        self.all_chans = []
        self.nt = 0

    def _sem(self, name):
        h = self.nc.alloc_semaphore(name=name)
        self.sems.append(h)
        return h

    def chan(self):
        self.nchan += 1
        ch = Chan(self._sem("%sc%d" % (self.tag, self.nchan)))
        self.all_chans.append(ch)
        return ch

    def sbuf(self, shape, dt, name=None):
        self.nt += 1
        t = self.stack.enter_context(self.nc.sbuf_tensor("%s_%s%d" % (self.tag, name or "t", self.nt), list(shape), dt))
        return t

    def psum(self, shape, dt=F32, name=None):
        self.nt += 1
        return self.stack.enter_context(self.nc.psum_tensor("%s_%s%d" % (self.tag, name or "p", self.nt), list(shape), dt))

    def _wait(self, eng, tok, same_ok=False):
        if tok is None:
            return
        kind, key, val = tok
        if kind == "e" and key == eng and (same_ok or eng in ("pe", "sp")):
            return
        k = (kind, id(key) if kind == "d" else key)
        w = self.waited[eng]
        if w.get(k, -1) >= val:
            return
        w[k] = val
        if kind == "e":
            self.needed[key].add(val)
        self.lists[eng].append(("w", kind, key, val))

    def _deps(self, eng, reads, writes):
        for b in reads:
            self._wait(eng, b.lw)
        for b in writes:
            self._wait(eng, b.lw)
            for t in b.rd:
                self._wait(eng, t, same_ok=True)

    def _mark(self, tok, reads, writes):
        for b in reads:
            b.rd.append(tok)
            if len(b.rd) > 64:
                b.rd = b.rd[-48:]
        for b in writes:
            b.lw = tok
            b.rd = []

    def op(self, eng, fn, reads=(), writes=()):
        ex = [b for b in reads if b.excl]
        if ex:
            reads = [b for b in reads if not b.excl]
            writes = list(writes) + ex
        self._deps(eng, reads, writes)
        self.nops[eng] += 1
        o = self.nops[eng]
        self.lists[eng].append(("o", fn, o))
        tok = ("e", eng, o)
        self._mark(tok, reads, writes)
        return tok

    def dma(self, eng, out, in_, chan, reads=(), writes=(), final=False):
        self._deps(eng, reads, writes)
        if chan.n > 0:
            self._wait(eng, ("d", chan, 16 * chan.n))
        chan.n += 1
        tok = ("d", chan, 16 * chan.n)
        self.lists[eng].append(("d", out, in_, chan))
        self._mark(tok, reads, writes)
        if final:
            self.out_toks.append(tok)
        return tok

    def collective(self, kind, groups, in_ap, out_ap):
        ch = Chan(self._sem("%scc%d" % (self.tag, self.nchan)))
        self.nchan += 1
        self.lists["pool"].append(("c", lambda e: e.collective_compute(kind, ALU.bypass, replica_groups=groups, ins=[in_ap], outs=[out_ap]), ch))
        self.lists["pool"].append(("w", "d", ch, 1))

    def build(self):
        nc = self.nc
        for t in self.out_toks:
            self._wait("sp", t)
        for ch in self.all_chans:
            if ch.n > 0:
                self._wait("sp", ("d", ch, 16 * ch.n))
        last = {e: self.nops[e] for e in self.ENGS}
        for e in ("pe", "act", "dve", "pool"):
            for f in ("pe", "act", "dve", "pool"):
                if f != e and last[f] > 0:
                    self._wait(e, ("e", f, last[f]))
        rankd = {e: {o: i + 1 for i, o in enumerate(sorted(self.needed[e]))} for e in self.ENGS}

        def run(ename, e):
            need = self.needed[ename]
            for it in self.lists[ename]:
                if it[0] == "w":
                    _, kind, key, val = it
                    if kind == "e":
                        e.wait_ge(self.esem[key], rankd[key][val])
                    else:
                        e.wait_ge(key.sem, val)
                elif it[0] == "o":
                    ins = it[1](e)
                    if it[2] in need:
                        ins.then_inc(self.esem[ename], 1)
                elif it[0] == "c":
                    it[1](e).then_inc(it[2].sem, 1)
                else:
                    _, out, in_, chan = it
                    e.dma_start(out=out, in_=in_).then_inc(chan.sem, 16)

        with nc.Block() as block:
            block.tensor(lambda e: run("pe", e))
            block.scalar(lambda e: run("act", e))
            block.vector(lambda e: run("dve", e))
            block.gpsimd(lambda e: run("pool", e))
            block.sync(lambda e: run("sp", e))
        self.stack.close()
        nc.all_engine_barrier()
        nc.clear_and_free_semaphores(self.sems)
        nc.all_engine_barrier()


class Ring:
    def __init__(self, p, n, shape, dt, name, psum=False, chan=True):
        self.items = []
        for i in range(n):
            t = p.psum(shape, dt, name) if psum else p.sbuf(shape, dt, name)
            self.items.append((t, Buf(excl=psum), p.chan() if chan else None))
        self.i = 0

    def next(self):
        it = self.items[self.i % len(self.items)]
        self.i += 1
        return it


def dram(nc, name, shape, dt, kind="Internal"):
    return nc.dram_tensor(name, list(shape), dt, kind=kind)


def precast(p, src, K, N, dst, chans, nsplit=None):
    C = K // 128
    b = Buf()
    for c in range(C):
        ch = chans[c % len(chans)]
        p.dma("pool", dst.ap()[:, c, :], src[c * 128:(c + 1) * 128, :], ch, writes=[])
    return b


def precast_chunked(p, src, K, chunks, dst, chans):
    for i, (c0, w) in enumerate(chunks):
        ch = chans[i % len(chans)]
        p.dma("pool", dst.ap()[i, :, :, 0:w], src[:, c0:c0 + w].rearrange("(c p) j -> p c j", p=128), ch, writes=[])


class TokCtx:
    def __init__(self, p, nc):
        self.p = p
        self.nc = nc
        self.identb = p.sbuf([128, 128], BF16, "identb")
        self.b_id = Buf()
        p.op("pool", lambda e: e.memset(self.identb[:], 1.0), writes=[self.b_id])
        p.op("pool", lambda e: e.affine_select(out=self.identb[:], in_=self.identb[:], pattern=[[-1, 128]],
                                                compare_op=ALU.is_equal, fill=0.0, base=0, channel_multiplier=1),
             reads=[self.b_id], writes=[self.b_id])
        self.eps = p.sbuf([128, 1], F32, "eps")
        self.b_eps = Buf()
        p.op("pool", lambda e: e.memset(self.eps[:], EPS), writes=[self.b_eps])
        self.h = [(p.sbuf([128, D], F32, "h"), Buf(), p.chan(), p.chan()) for _ in range(4)]
        self.junk = p.sbuf([128, D], BF16, "junk")
        self.b_junk = Buf()
        self.ss = p.sbuf([128, 4], F32, "ss")
        self.b_ss = Buf()
        self.rs = p.sbuf([128, 4], F32, "rs")
        self.b_rs = Buf()
        self.xnb = Ring(p, 2, [128, D], BF16, "xnb", chan=False)
        self.pst = Ring(p, 2, [128, D], BF16, "pst", psum=True, chan=False)
        self.psf = Ring(p, 6, [128, 512], F32, "psf", psum=True, chan=False)
        self.wst = Ring(p, 8, [128, 8, 128], BF16, "wst")
        self.sg = Ring(p, 2, [128, 512], F32, "sg", chan=False)
        self.hidT = p.sbuf([128, NF, TB], BF16, "hidT")
        self.b_hid = Buf()
        self.wd = p.sbuf([128, NF, D], BF16, "wd")
        self.b_wd = Buf()
        self.c_wd = [p.chan() for _ in range(2)]

    def load_h(self, src, row0):
        p = self.p
        for t in range(4):
            ht, hb, hc, _ = self.h[t]
            p.dma("sp", ht[:], src[row0 + t * 128: row0 + (t + 1) * 128, :], hc, writes=[hb])

    def store_h(self, dst, row0, final=False):
        p = self.p
        for t in range(4):
            ht, hb, _, hc2 = self.h[t]
            p.dma("sp", dst[row0 + t * 128: row0 + (t + 1) * 128, :], ht[:], hc2, reads=[hb], final=final)

    def norm_T(self, gT, b_g, outT, b_out):
        p = self.p
        for t in range(4):
            ht, hb, _, _ = self.h[t]
            ss, rs = self.ss, self.rs
            p.op("act", lambda e, ht=ht, t=t: e.activation(out=self.junk[:], in_=ht[:], func=AF.Square,
                                                           accum_out=ss[:, t:t + 1]),
                 reads=[hb], writes=[self.b_junk, self.b_ss])
            p.op("act", lambda e, t=t: e.activation(out=rs[:, t:t + 1], in_=ss[:, t:t + 1], func=AF.Sqrt,
                                                    bias=self.eps[:], scale=1.0 / D),
                 reads=[self.b_ss, self.b_eps], writes=[self.b_rs])
            p.op("dve", lambda e, t=t: e.reciprocal(out=rs[:, t:t + 1], in_=rs[:, t:t + 1]),
                 reads=[self.b_rs], writes=[self.b_rs])
            xn, bxn, _ = self.xnb.next()
            p.op("dve", lambda e, xn=xn, ht=ht, t=t: e.tensor_scalar(out=xn[:], in0=ht[:], scalar1=rs[:, t:t + 1],
                                                                     scalar2=None, op0=ALU.mult),
                 reads=[hb, self.b_rs], writes=[bxn])
            ps, bps, _ = self.pst.next()
            for c in range(8):
                p.op("pe", lambda e, ps=ps, xn=xn, c=c: e.transpose(ps[:, c * 128:(c + 1) * 128],
                                                                    xn[:, c * 128:(c + 1) * 128], self.identb[:]),
                     reads=[bxn, self.b_id], writes=[bps])
            p.op("dve", lambda e, ps=ps, t=t: e.tensor_tensor(
                out=outT[:, :, t * 128:(t + 1) * 128], in0=ps[:].rearrange("p (c j) -> p c j", c=8),
                in1=gT[:].unsqueeze(2).to_broadcast([128, 8, 128]), op=ALU.mult),
                 reads=[bps, b_g], writes=[b_out])

    def load_wd(self, wd_c):
        p = self.p
        half = NF // 2
        p.dma("sp", self.wd[:, 0:half, :], wd_c.ap()[:, 0:half, :], self.c_wd[0], writes=[self.b_wd])
        p.dma("sp", self.wd[:, half:NF, :], wd_c.ap()[:, half:NF, :], self.c_wd[1], writes=[])
        self.b_wd.lw = None
        self.wd_toks = [("d", self.c_wd[0], 16 * self.c_wd[0].n), ("d", self.c_wd[1], 16 * self.c_wd[1].n)]

    def ffn(self, xnT, b_xn, wg_c, wu_c):
        p = self.p
        for f in range(NF):
            wg, bwg, cwg = self.wst.next()
            p.dma("sp", wg[:], wg_c.ap()[f], cwg, writes=[bwg])
            wu, bwu, cwu = self.wst.next()
            p.dma("sp", wu[:], wu_c.ap()[f], cwu, writes=[bwu])
            gps, bg, _ = self.psf.next()
            for c in range(8):
                p.op("pe", lambda e, gps=gps, wg=wg, c=c: e.matmul(gps[:], lhsT=wg[:, c, :], rhs=xnT[:, c, :],
                                                                   start=(c == 0), stop=(c == 7)),
                     reads=[bwg, b_xn], writes=[bg])
            ups, bu, _ = self.psf.next()
            for c in range(8):
                p.op("pe", lambda e, ups=ups, wu=wu, c=c: e.matmul(ups[:], lhsT=wu[:, c, :], rhs=xnT[:, c, :],
                                                                   start=(c == 0), stop=(c == 7)),
                     reads=[bwu, b_xn], writes=[bu])
            sg, bsg, _ = self.sg.next()
            p.op("act", lambda e, sg=sg, gps=gps: e.activation(out=sg[:], in_=gps[:], func=AF.Silu),
                 reads=[bg], writes=[bsg])
            p.op("dve", lambda e, sg=sg, ups=ups, f=f: e.tensor_tensor(out=self.hidT[:, f, :], in0=sg[:], in1=ups[:],
                                                                       op=ALU.mult),
                 reads=[bsg, bu], writes=[self.b_hid])
        for tk in self.wd_toks:
            p._wait("pe", tk)
        for t in range(4):
            ht, hb, _, _ = self.h[t]
            for hf in range(2):
                ops, bo, _ = self.psf.next()
                for f in range(NF):
                    p.op("pe", lambda e, ops=ops, t=t, hf=hf, f=f: e.matmul(
                        ops[:], lhsT=self.hidT[:, f, t * 128:(t + 1) * 128], rhs=self.wd[:, f, hf * 512:(hf + 1) * 512],
                        start=(f == 0), stop=(f == NF - 1)),
                         reads=[self.b_hid], writes=[bo])
                p.op("dve", lambda e, ops=ops, ht=ht, hf=hf: e.scalar_tensor_tensor(
                    out=ht[:, hf * 512:(hf + 1) * 512], in0=ops[:], scalar=0.5, in1=ht[:, hf * 512:(hf + 1) * 512],
                    op0=ALU.mult, op1=ALU.add),
                     reads=[bo, hb], writes=[hb])


def phase_A(nc, tag, h_in, h_out, uT_out, gT0_d, gT1_d, wg_d, wu_d, wd_d, nblk):
    wg_c = dram(nc, tag + "wg_c", [NF, 128, 8, 128], BF16)
    wu_c = dram(nc, tag + "wu_c", [NF, 128, 8, 128], BF16)
    wd_c = dram(nc, tag + "wd_c", [128, NF, D], BF16)
    p = Prog(nc, tag + "pc")
    chans = [p.chan() for _ in range(8)]
    precast_chunked(p, wg_d, D, [(f * 128, 128) for f in range(NF)], wg_c, chans)
    precast_chunked(p, wu_d, D, [(f * 128, 128) for f in range(NF)], wu_c, chans)
    precast(p, wd_d, DFF, D, wd_c, chans)
    for ch in chans:
        p._wait("pool", ("d", ch, 16 * ch.n))
    p.build()
    p = Prog(nc, tag)
    cx = TokCtx(p, nc)
    g0 = p.sbuf([128, 8], F32, "g0"); b_g0 = Buf(); c_g0 = p.chan()
    g1 = p.sbuf([128, 8], F32, "g1"); b_g1 = Buf(); c_g1 = p.chan()
    p.dma("sp", g0[:], gT0_d, c_g0, writes=[b_g0])
    p.dma("sp", g1[:], gT1_d, c_g1, writes=[b_g1])
    cx.load_wd(wd_c)
    xnT = p.sbuf([128, 8, TB], BF16, "xnT"); b_xn = Buf()
    uT = p.sbuf([128, 8, TB], BF16, "uT"); b_u = Buf(); c_u = p.chan()
    for lb in range(nblk):
        cx.load_h(h_in, lb * TB)
        cx.norm_T(g0, b_g0, xnT, b_xn)
        cx.ffn(xnT, b_xn, wg_c, wu_c)
        cx.store_h(h_out, lb * TB, final=True)
        cx.norm_T(g1, b_g1, uT, b_u)
        p.dma("sp", uT_out.ap()[lb], uT[:], c_u, reads=[b_u], final=True)
    p.build()


def build_launch_A():
    nc = bass.Bass("TRN2", target_bir_lowering=False)
    h_in = nc.dram_tensor("h_in", [4096, D], F32, kind="ExternalInput").ap()
    gT0 = nc.dram_tensor("gT0", [128, 8], F32, kind="ExternalInput").ap()
    gT1 = nc.dram_tensor("gT1", [128, 8], F32, kind="ExternalInput").ap()
    wg = nc.dram_tensor("wg", [D, DFF], F32, kind="ExternalInput").ap()
    wu = nc.dram_tensor("wu", [D, DFF], F32, kind="ExternalInput").ap()
    wd = nc.dram_tensor("wd", [DFF, D], F32, kind="ExternalInput").ap()
    h_out = nc.dram_tensor("h_out", [4096, D], F32, kind="ExternalOutput").ap()
    uT_out = nc.dram_tensor("uT_out", [8, 128, 8, TB], BF16, kind="ExternalOutput")
    phase_A(nc, "A", h_in, h_out, uT_out, gT0, gT1, wg, wu, wd, 8)
    return nc


def gT_layout(g):
    return np.ascontiguousarray(np.asarray(g, np.float32).reshape(8, 128).T)


def phase_C(nc, tag, h_in, h_out, uT_in, yT_in, gT2_d, wgate_d, wbr_d, wo_d, wg_d, wu_d, wd_d, nblk, gfin_d=None, ygath=None, par_d=None):
    wg_c = dram(nc, tag + "wg_c", [NF, 128, 8, 128], BF16)
    wu_c = dram(nc, tag + "wu_c", [NF, 128, 8, 128], BF16)
    wd_c = dram(nc, tag + "wd_c", [128, NF, D], BF16)
    wgate_c = [dram(nc, tag + "wgate_c%d" % i, [8, 128, 8, 128], BF16) for i in range(4)]
    wbr_c = [dram(nc, tag + "wbr_c%d" % i, [8, 128, 4, 128], BF16) for i in range(4)]
    wo_c = dram(nc, tag + "wo_c", [128, 8, D], BF16)
    p = Prog(nc, tag + "pc")
    chans = [p.chan() for _ in range(8)]
    precast_chunked(p, wg_d, D, [(f * 128, 128) for f in range(NF)], wg_c, chans)
    precast_chunked(p, wu_d, D, [(f * 128, 128) for f in range(NF)], wu_c, chans)
    precast(p, wd_d, DFF, D, wd_c, chans)
    for i in range(4):
        precast_chunked(p, wgate_d[i], D, [(j * 128, 128) for j in range(8)], wgate_c[i], chans)
        precast_chunked(p, wbr_d[i], 512, [(j * 128, 128) for j in range(8)], wbr_c[i], chans)
    precast(p, wo_d, D, D, wo_c, chans)
    for ch in chans:
        p._wait("pool", ("d", ch, 16 * ch.n))
    p.build()

    p = Prog(nc, tag)
    cx = TokCtx(p, nc)
    g2 = p.sbuf([128, 8], F32, "g2"); b_g2 = Buf(); c_g2 = p.chan()
    p.dma("sp", g2[:], gT2_d, c_g2, writes=[b_g2])
    cx.load_wd(wd_c)
    wo = p.sbuf([128, 8, D], BF16, "wo"); b_wo = Buf(); c_wo = p.chan()
    p.dma("sp", wo[:], wo_c.ap(), c_wo, writes=[b_wo])
    if gfin_d is not None:
        gfin = p.sbuf([128, D], F32, "gfin"); b_gf = Buf(); c_gf = p.chan()
        p.dma("sp", gfin[:], bass.AP(gfin_d.tensor, 0, [[0, 128], [1, D]]), c_gf, writes=[b_gf])
    xnT = p.sbuf([128, 8, TB], BF16, "xnT"); b_xn = Buf()
    uT = p.sbuf([128, 8, TB], BF16, "uT"); b_u = Buf(); c_u = p.chan()
    yT = p.sbuf([128, 16, TB], BF16, "yT"); b_y = Buf(); c_y = p.chan()
    mT = p.sbuf([128, 8, TB], BF16, "mT"); b_m = Buf()
    macc = Ring(p, 2, [128, TB], F32, "macc", chan=False)
    tmpr = Ring(p, 2, [128, TB], F32, "tmpr", chan=False)
    wbst = Ring(p, 6, [128, 4, 128], BF16, "wbst")
    if ygath is not None:
        yT2 = p.sbuf([128, 16, TB], BF16, "yT2"); b_y2 = Buf(); c_y2 = [p.chan() for _ in range(8)]
        c_y1 = [p.chan() for _ in range(8)]
        parf = p.sbuf([128, TB], F32, "parf"); b_pf = Buf(); c_pf = p.chan()
        parm = p.sbuf([128, TB], mybir.dt.uint32, "parm"); b_pm = Buf()
        p.dma("sp", parf[:], par_d, c_pf, writes=[b_pf])
        p.op("dve", lambda e: e.tensor_scalar(out=parm[:], in0=parf[:], scalar1=0.5, scalar2=None, op0=ALU.is_gt), reads=[b_pf], writes=[b_pm])
    for lb in range(nblk):
        cx.load_h(h_in, lb * TB)
        p.dma("sp", uT[:], uT_in.ap()[lb], c_u, writes=[b_u])
        if ygath is None:
            p.dma("sp", yT[:], yT_in.ap()[lb], c_y, writes=[b_y])
        else:
            n_ = 0
            for rr in range(2):
                for i in range(4):
                    p.dma("sp", yT[:, i * 4 + 2 * rr:i * 4 + 2 * rr + 2, :],
                          ygath.ap()[lb // 2, rr, lb % 2, i].rearrange("(k p) t -> p k t", p=128), c_y1[n_], writes=[b_y] if n_ == 0 else [])
                    p.dma("sp", yT2[:, i * 4 + 2 * rr:i * 4 + 2 * rr + 2, :],
                          ygath.ap()[4 + lb // 2, rr, lb % 2, i].rearrange("(k p) t -> p k t", p=128), c_y2[n_], writes=[b_y2] if n_ == 0 else [])
                    n_ += 1
            for ch in c_y1 + c_y2:
                p._wait("dve", ("d", ch, 16 * ch.n))
            for k in range(16):
                p.op("dve", lambda e, k=k: e.copy_predicated(yT[:, k, :], parm[:], yT2[:, k, :]), reads=[b_pm, b_y2], writes=[b_y])
        for j in range(8):
            ma, bma, _ = macc.next()
            for i in range(4):
                wgt, bwg, cwg = cx.wst.next()
                p.dma("sp", wgt[:], wgate_c[i].ap()[j], cwg, writes=[bwg])
                wb, bwb, cwb = wbst.next()
                p.dma("sp", wb[:], wbr_c[i].ap()[j], cwb, writes=[bwb])
                gps, bg, _ = cx.psf.next()
                for c in range(8):
                    p.op("pe", lambda e, gps=gps, wgt=wgt, c=c: e.matmul(gps[:], lhsT=wgt[:, c, :], rhs=uT[:, c, :],
                                                                         start=(c == 0), stop=(c == 7)),
                         reads=[bwg, b_u], writes=[bg])
                bps, bb, _ = cx.psf.next()
                for k in range(4):
                    p.op("pe", lambda e, bps=bps, wb=wb, k=k, i=i: e.matmul(bps[:], lhsT=wb[:, k, :], rhs=yT[:, i * 4 + k, :],
                                                                             start=(k == 0), stop=(k == 3)),
                         reads=[bwb, b_y], writes=[bb])
                sg, bsg, _ = cx.sg.next()
                p.op("act", lambda e, sg=sg, gps=gps: e.activation(out=sg[:], in_=gps[:], func=AF.Sigmoid),
                     reads=[bg], writes=[bsg])
                if i == 0:
                    p.op("dve", lambda e, ma=ma, sg=sg, bps=bps: e.tensor_tensor(out=ma[:], in0=sg[:], in1=bps[:], op=ALU.mult),
                         reads=[bsg, bb], writes=[bma])
                else:
                    tm, btm, _ = tmpr.next()
                    p.op("dve", lambda e, tm=tm, sg=sg, bps=bps: e.tensor_tensor(out=tm[:], in0=sg[:], in1=bps[:], op=ALU.mult),
                         reads=[bsg, bb], writes=[btm])
                    p.op("pool", lambda e, ma=ma, tm=tm: e.tensor_tensor(out=ma[:], in0=ma[:], in1=tm[:], op=ALU.add),
                         reads=[bma, btm], writes=[bma])
            p.op("act", lambda e, ma=ma, j=j: e.copy(out=mT[:, j, :], in_=ma[:]), reads=[bma], writes=[b_m])
        for t in range(4):
            ht, hb, _, _ = cx.h[t]
            for hf in range(2):
                ops, bo, _ = cx.psf.next()
                for j in range(8):
                    p.op("pe", lambda e, ops=ops, t=t, hf=hf, j=j: e.matmul(
                        ops[:], lhsT=mT[:, j, t * 128:(t + 1) * 128], rhs=wo[:, j, hf * 512:(hf + 1) * 512],
                        start=(j == 0), stop=(j == 7)), reads=[b_m, b_wo], writes=[bo])
                p.op("dve", lambda e, ops=ops, ht=ht, hf=hf: e.tensor_tensor(
                    out=ht[:, hf * 512:(hf + 1) * 512], in0=ht[:, hf * 512:(hf + 1) * 512], in1=ops[:], op=ALU.add),
                     reads=[bo, hb], writes=[hb])
        cx.norm_T(g2, b_g2, xnT, b_xn)
        cx.ffn(xnT, b_xn, wg_c, wu_c)
        if gfin_d is not None:
            for t in range(4):
                ht, hb, _, _ = cx.h[t]
                ss, rs = cx.ss, cx.rs
                p.op("act", lambda e, ht=ht, t=t: e.activation(out=cx.junk[:], in_=ht[:], func=AF.Square,
                                                               accum_out=ss[:, t:t + 1]),
                     reads=[hb], writes=[cx.b_junk, cx.b_ss])
                p.op("act", lambda e, t=t: e.activation(out=rs[:, t:t + 1], in_=ss[:, t:t + 1], func=AF.Sqrt,
                                                        bias=cx.eps[:], scale=1.0 / D),
                     reads=[cx.b_ss, cx.b_eps], writes=[cx.b_rs])
                p.op("dve", lambda e, t=t: e.reciprocal(out=rs[:, t:t + 1], in_=rs[:, t:t + 1]),
                     reads=[cx.b_rs], writes=[cx.b_rs])
                p.op("dve", lambda e, ht=ht, t=t: e.scalar_tensor_tensor(out=ht[:], in0=ht[:], scalar=rs[:, t:t + 1],
                                                                         in1=gfin[:], op0=ALU.mult, op1=ALU.mult),
                     reads=[hb, cx.b_rs, b_gf], writes=[hb])
        cx.store_h(h_out, lb * TB, final=True)
    p.build()


def build_launch_C(final):
    nc = bass.Bass("TRN2", target_bir_lowering=False)
    h_in = nc.dram_tensor("h_in", [4096, D], F32, kind="ExternalInput").ap()
    uT_in = nc.dram_tensor("uT_in", [8, 128, 8, TB], BF16, kind="ExternalInput")
    yT_in = nc.dram_tensor("yT_in", [8, 128, 16, TB], BF16, kind="ExternalInput")
    gT2 = nc.dram_tensor("gT2", [128, 8], F32, kind="ExternalInput").ap()
    wgate = nc.dram_tensor("wgate", [4, D, D], F32, kind="ExternalInput").ap()
    wbr = nc.dram_tensor("wbr", [4, 512, D], F32, kind="ExternalInput").ap()
    wo = nc.dram_tensor("wo", [D, D], F32, kind="ExternalInput").ap()
    wg = nc.dram_tensor("wg", [D, DFF], F32, kind="ExternalInput").ap()
    wu = nc.dram_tensor("wu", [D, DFF], F32, kind="ExternalInput").ap()
    wd = nc.dram_tensor("wd", [DFF, D], F32, kind="ExternalInput").ap()
    gfin = nc.dram_tensor("gfin", [D], F32, kind="ExternalInput").ap() if final else None
    h_out = nc.dram_tensor("h_out", [4096, D], F32, kind="ExternalOutput").ap()
    phase_C(nc, "C", h_in, h_out, uT_in, yT_in, gT2, wgate, wbr, wo, wg, wu, wd, 8, gfin)
    return nc


FM_COLS = 1804
TM_COLS = 448
OFF = dict(cq=0, ckv=256, kr=384, krp=416, sq=448, sk=704, nq=768, kcvc=1024, kskw=1152, ng=1280, dq=1292, dk=1548)
NKIND = 15
FM_CHUNKS = [(0, 128), (128, 128), (256, 128), (384, 32), (416, 32), (448, 128), (576, 128), (704, 64), (768, 128), (896, 128),
             (1024, 128), (1152, 128), (1280, 12), (1292, 128), (1420, 128), (1548, 128), (1676, 128)]
SW = 1024
TVL = 1152


class Scratch:
    def __init__(self, nc, tag):
        d = lambda n, s, dt=BF16: dram(nc, tag + n, s, dt)
        self.mla_qT = d("mla_qT", [16, 4, 96, TB])
        self.mla_kT = d("mla_kT", [16, 4, 96, TB])
        self.mla_v = d("mla_v", [16, 128, 4, 256])
        self.swa_qT = d("swa_qT", [16, 256, TB])
        self.swa_kT = d("swa_kT", [16, 64, TB])
        self.swa_v = d("swa_v", [16, 128, 4, 64])
        self.nsa_qT = d("nsa_qT", [16, 256, TB])
        self.nsa_kcT = d("nsa_kcT", [64, S + 64])
        self.nsa_vcT = d("nsa_vcT", [64, S + 64])
        self.nsa_ksT = d("nsa_ksT", [16, 64, TB])
        self.nsa_kwT = d("nsa_kwT", [16, 64, TB])
        self.nsa_vs = d("nsa_vs", [16, 128, 4, 64])
        self.nsa_vw = d("nsa_vw", [16, 128, 4, 64])
        self.nsa_gsig = d("nsa_gsig", [16, 12, TB], F32)
        self.diff_qT = d("diff_qT", [16, 256, TB])
        self.diff_kT = d("diff_kT", [16, 256, TB])
        self.diff_v = d("diff_v", [16, 128, 4, 256])
        self.tvec = d("tvec", [NKIND, TVL])


def phase_B1(nc, tag, sc, uT_in, win_fm_d, win_tm_d, wuq_d, wuqp_d, wukvk_d, wukvv_d, qnT_d, kvn_d, ropeC_d, ropeS_d, nblk=16, stage=99):
    fm_c = dram(nc, tag + "fm_c", [len(FM_CHUNKS), 128, 8, 128], BF16)
    tm_c = dram(nc, tag + "tm_c", [128, 8, TM_COLS], BF16)
    p = Prog(nc, tag + "pc")
    chans = [p.chan() for _ in range(8)]
    precast_chunked(p, win_fm_d, D, FM_CHUNKS, fm_c, chans)
    precast(p, win_tm_d, D, TM_COLS, tm_c, chans)
    for ch in chans:
        p._wait("pool", ("d", ch, 16 * ch.n))
    p.build()

    p = Prog(nc, tag)
    psf = Ring(p, 7, [128, 512], F32, "psf", psum=True, chan=False)
    wst = Ring(p, 8, [128, 8, 128], BF16, "wst")
    uTr = Ring(p, 2, [128, 8, TB], BF16, "uT")
    ev = Ring(p, 6, [128, TB], BF16, "ev")
    eps = p.sbuf([128, 1], F32, "eps"); b_eps = Buf()
    p.op("pool", lambda e: e.memset(eps[:], EPS), writes=[b_eps])
    ones_f = p.sbuf([128, 128], F32, "ones"); b_ones = Buf()
    p.op("pool", lambda e: e.memset(ones_f[:], 1.0), writes=[b_ones])
    def res(shape, src):
        t = p.sbuf(shape, BF16, "res"); b = Buf(); c = p.chan()
        p.dma("pool", t[:], src, c, writes=[b])
        return t, b
    wuq, b_wuq = res([128, 2, 384], wuq_d.rearrange("(c p) n -> p c n", p=128))
    wuqp, b_wuqp = res([128, 2, 384], wuqp_d.rearrange("(c p) n -> p c n", p=128))
    wkk, b_wkk = res([128, 256], wukvk_d)
    wkv, b_wkv = res([128, 256], wukvv_d)
    wtm = p.sbuf([128, 8, TM_COLS], BF16, "wtm"); b_wtm = Buf(); c_wtm = p.chan()
    p.dma("sp", wtm[:], tm_c.ap(), c_wtm, writes=[b_wtm])
    qn = p.sbuf([128, 2], F32, "qn"); b_qn = Buf(); c_qn = p.chan()
    p.dma("sp", qn[:], qnT_d, c_qn, writes=[b_qn])
    kvn = p.sbuf([128, 1], F32, "kvn"); b_kvn = Buf(); c_kvn = p.chan()
    p.dma("sp", kvn[:], kvn_d, c_kvn, writes=[b_kvn])
    rC = p.sbuf([96, TB], F32, "rC"); b_rC = Buf(); c_rC = p.chan()
    rS = p.sbuf([96, TB], F32, "rS"); b_rS = Buf(); c_rS = p.chan()
    rC32 = p.sbuf([32, TB], F32, "rC32"); b_rC32 = Buf(); c_rC32 = p.chan()
    rS32 = p.sbuf([32, TB], F32, "rS32"); b_rS32 = Buf(); c_rS32 = p.chan()
    sq = [(p.sbuf([128, TB], F32, "sq"), Buf()) for _ in range(2)]
    cqf = [(p.sbuf([128, TB], F32, "cqf"), Buf()) for _ in range(2)]
    rs = p.sbuf([128, TB], F32, "rs"); b_rs = Buf()
    cqn = p.sbuf([128, 2, TB], BF16, "cqn"); b_cqn = Buf()
    ckvn = p.sbuf([128, TB], BF16, "ckvn"); b_ckvn = Buf()
    ta = Ring(p, 2, [96, TB], F32, "ta", chan=False)
    tb = Ring(p, 2, [96, TB], F32, "tb", chan=False)
    vt = p.sbuf([128, 4, 256], BF16, "vt"); b_vt = Buf(); c_vt = p.chan()
    vtm = p.sbuf([128, 4, TM_COLS], BF16, "vtm"); b_vtm = Buf(); c_vtm = [p.chan() for _ in range(4)]
    gs = p.sbuf([12, TB], F32, "gs"); b_gs = Buf(); c_gs = p.chan()
    flip = [0]

    def fm(uT, b_u, col0, w):
        wt, bw, cw = wst.next()
        p.dma("sp", wt[:], fm_c.ap()[FM_CHUNKS.index((col0, w))], cw, writes=[bw])
        ps, bp, _ = psf.next()
        for c in range(8):
            p.op("pe", lambda e, ps=ps, wt=wt, c=c, w=w: e.matmul(ps[0:w, :], lhsT=wt[:, c, 0:w], rhs=uT[:, c, :],
                                                                   start=(c == 0), stop=(c == 7)),
                 reads=[bw, b_u], writes=[bp])
        return ps, bp

    def evac(ps, bp, w, scale, dsts):
        t, bt, ct = ev.next()
        flip[0] ^= 1
        if flip[0]:
            p.op("act", lambda e: e.activation(out=t[0:w, :], in_=ps[0:w, :], func=AF.Copy, scale=scale),
                 reads=[bp], writes=[bt])
        else:
            p.op("dve", lambda e: e.tensor_scalar(out=t[0:w, :], in0=ps[0:w, :], scalar1=scale, scalar2=None, op0=ALU.mult),
                 reads=[bp], writes=[bt])
        for i, (dst, r0, nr) in enumerate(dsts):
            ch = ct if i == 0 else extra_ch[i - 1]
            p.dma("sp", dst, t[r0:r0 + nr, :], ch, reads=[bt])

    extra_ch = [p.chan() for _ in range(3)]

    def rms_fm(chunks, nfeat, gn, b_gn, outs, b_out):
        for i, (ps, bp) in enumerate(chunks):
            p.op("act", lambda e, i=i, ps=ps: e.activation(out=sq[i][0][:], in_=ps[:], func=AF.Square),
                 reads=[bp], writes=[sq[i][1]])
            p.op("dve", lambda e, i=i, ps=ps: e.tensor_copy(out=cqf[i][0][:], in_=ps[:]), reads=[bp], writes=[cqf[i][1]])
        ss, bss, _ = psf.next()
        n = len(chunks)
        for i in range(n):
            p.op("pe", lambda e, i=i, ss=ss: e.matmul(ss[:], lhsT=ones_f[:], rhs=sq[i][0][:], start=(i == 0), stop=(i == n - 1)),
                 reads=[b_ones, sq[i][1]], writes=[bss])
        p.op("act", lambda e, ss=ss: e.activation(out=rs[:], in_=ss[:], func=AF.Sqrt, bias=eps[:], scale=1.0 / nfeat),
             reads=[bss, b_eps], writes=[b_rs])
        p.op("dve", lambda e: e.reciprocal(out=rs[:], in_=rs[:]), reads=[b_rs], writes=[b_rs])
        for i in range(n):
            p.op("dve", lambda e, i=i: e.scalar_tensor_tensor(out=outs[i], in0=cqf[i][0][:], scalar=gn[:, i:i + 1], in1=rs[:],
                                                              op0=ALU.mult, op1=ALU.mult),
                 reads=[cqf[i][1], b_gn, b_rs], writes=[b_out])

    for blk in range(nblk):
        uT, b_u, c_u = uTr.next()
        p.dma("sp", uT[:], (uT_in(blk) if callable(uT_in) else uT_in.ap()[blk]), c_u, writes=[b_u])
        p.dma("sp", rC[:], ropeC_d[blk], c_rC, writes=[b_rC])
        p.dma("sp", rS[:], ropeS_d[blk], c_rS, writes=[b_rS])
        p.dma("sp", rC32[:], ropeC_d[blk, 64:96, :], c_rC32, writes=[b_rC32])
        p.dma("sp", rS32[:], ropeS_d[blk, 64:96, :], c_rS32, writes=[b_rS32])
        ch = [fm(uT, b_u, OFF["cq"], 128), fm(uT, b_u, OFF["cq"] + 128, 128)]
        rms_fm(ch, 256, qn, b_qn, [cqn[:, 0, :], cqn[:, 1, :]], b_cqn)
        ch = [fm(uT, b_u, OFF["ckv"], 128)]
        rms_fm(ch, 128, kvn, b_kvn, [ckvn[:]], b_ckvn)
        if stage < 1:
            continue
        pa, bpa = fm(uT, b_u, OFF["kr"], 32)
        pb, bpb = fm(uT, b_u, OFF["krp"], 32)
        t1, bt1, _ = ta.next()
        t2, bt2, _ = tb.next()
        p.op("dve", lambda e, t1=t1, pa=pa: e.tensor_tensor(out=t1[0:32, :], in0=pa[0:32, :], in1=rC32[:], op=ALU.mult),
             reads=[bpa, b_rC32], writes=[bt1])
        p.op("dve", lambda e, t2=t2, pb=pb: e.tensor_tensor(out=t2[0:32, :], in0=pb[0:32, :], in1=rS32[:], op=ALU.mult),
             reads=[bpb, b_rS32], writes=[bt2])
        kr, bkr, ckr = ev.next()
        p.op("pool", lambda e, kr=kr, t1=t1, t2=t2: e.tensor_tensor(out=kr[0:32, :], in0=t1[0:32, :], in1=t2[0:32, :], op=ALU.add),
             reads=[bt1, bt2], writes=[bkr])
        for h in range(4):
            p.dma("sp", sc.mla_kT.ap()[blk, h, 64:96, :], kr[0:32, :], ckr if h == 0 else extra_ch[h - 1], reads=[bkr])
        if stage < 2:
            continue
        for h in range(4):
            pq, bpq, _ = psf.next()
            pp, bpp, _ = psf.next()
            for c in range(2):
                p.op("pe", lambda e, pq=pq, c=c, h=h: e.matmul(pq[0:96, :], lhsT=wuq[:, c, h * 96:(h + 1) * 96], rhs=cqn[:, c, :],
                                                               start=(c == 0), stop=(c == 1)), reads=[b_wuq, b_cqn], writes=[bpq])
            for c in range(2):
                p.op("pe", lambda e, pp=pp, c=c, h=h: e.matmul(pp[0:96, :], lhsT=wuqp[:, c, h * 96:(h + 1) * 96], rhs=cqn[:, c, :],
                                                               start=(c == 0), stop=(c == 1)), reads=[b_wuqp, b_cqn], writes=[bpp])
            t1, bt1, _ = ta.next()
            t2, bt2, _ = tb.next()
            p.op("dve", lambda e, t1=t1, pq=pq: e.tensor_tensor(out=t1[:], in0=pq[0:96, :], in1=rC[:], op=ALU.mult),
                 reads=[bpq, b_rC], writes=[bt1])
            p.op("dve", lambda e, t2=t2, pp=pp: e.tensor_tensor(out=t2[:], in0=pp[0:96, :], in1=rS[:], op=ALU.mult),
                 reads=[bpp, b_rS], writes=[bt2])
            qq, bqq, cqq = ev.next()
            p.op("pool", lambda e, qq=qq, t1=t1, t2=t2: e.tensor_tensor(out=qq[0:96, :], in0=t1[:], in1=t2[:], op=ALU.add),
                 reads=[bt1, bt2], writes=[bqq])
            p.dma("sp", sc.mla_qT.ap()[blk, h], qq[0:96, :], cqq, reads=[bqq])
        if stage < 3:
            continue
        for hp in range(2):
            pk, bpk, _ = psf.next()
            p.op("pe", lambda e, pk=pk, hp=hp: e.matmul(pk[:], lhsT=wkk[:, hp * 128:(hp + 1) * 128], rhs=ckvn[:], start=True, stop=True),
                 reads=[b_wkk, b_ckvn], writes=[bpk])
            evac(pk, bpk, 128, 1.0, [(sc.mla_kT.ap()[blk, 2 * hp, 0:64, :], 0, 64), (sc.mla_kT.ap()[blk, 2 * hp + 1, 0:64, :], 64, 64)])
        for t in range(4):
            pv, bpv, _ = psf.next()
            p.op("pe", lambda e, pv=pv, t=t: e.matmul(pv[:, 0:256], lhsT=ckvn[:, t * 128:(t + 1) * 128], rhs=wkv[:], start=True, stop=True),
                 reads=[b_ckvn, b_wkv], writes=[bpv])
            p.op("act", lambda e, pv=pv, t=t: e.copy(out=vt[:, t, :], in_=pv[:, 0:256]), reads=[bpv], writes=[b_vt])
        p.dma("sp", sc.mla_v.ap()[blk], vt[:], c_vt, reads=[b_vt])
        if stage < 4:
            continue
        for i in range(2):
            ps, bp = fm(uT, b_u, OFF["sq"] + i * 128, 128)
            evac(ps, bp, 128, 0.125, [(sc.swa_qT.ap()[blk, i * 128:(i + 1) * 128, :], 0, 128)])
        ps, bp = fm(uT, b_u, OFF["sk"], 64)
        evac(ps, bp, 64, 1.0, [(sc.swa_kT.ap()[blk], 0, 64)])
        for i in range(2):
            ps, bp = fm(uT, b_u, OFF["nq"] + i * 128, 128)
            evac(ps, bp, 128, 0.125, [(sc.nsa_qT.ap()[blk, i * 128:(i + 1) * 128, :], 0, 128)])
        ps, bp = fm(uT, b_u, OFF["kcvc"], 128)
        evac(ps, bp, 128, 1.0, [(sc.nsa_kcT.ap()[:, blk * TB:(blk + 1) * TB], 0, 64), (sc.nsa_vcT.ap()[:, blk * TB:(blk + 1) * TB], 64, 64)])
        ps, bp = fm(uT, b_u, OFF["kskw"], 128)
        evac(ps, bp, 128, 1.0, [(sc.nsa_ksT.ap()[blk], 0, 64), (sc.nsa_kwT.ap()[blk], 64, 64)])
        ps, bp = fm(uT, b_u, OFF["ng"], 12)
        p.op("act", lambda e, ps=ps: e.activation(out=gs[:], in_=ps[0:12, :], func=AF.Sigmoid), reads=[bp], writes=[b_gs])
        p.dma("sp", sc.nsa_gsig.ap()[blk], gs[:], c_gs, reads=[b_gs])
        for i in range(2):
            ps, bp = fm(uT, b_u, OFF["dq"] + i * 128, 128)
            evac(ps, bp, 128, 0.125, [(sc.diff_qT.ap()[blk, i * 128:(i + 1) * 128, :], 0, 128)])
        for i in range(2):
            ps, bp = fm(uT, b_u, OFF["dk"] + i * 128, 128)
            evac(ps, bp, 128, 1.0, [(sc.diff_kT.ap()[blk, i * 128:(i + 1) * 128, :], 0, 128)])
        if stage < 5:
            continue
        for t in range(4):
            pv, bpv, _ = psf.next()
            for c in range(8):
                p.op("pe", lambda e, pv=pv, t=t, c=c, uT=uT: e.matmul(pv[:, 0:TM_COLS], lhsT=uT[:, c, t * 128:(t + 1) * 128], rhs=wtm[:, c, :],
                                                               start=(c == 0), stop=(c == 7)), reads=[b_u, b_wtm], writes=[bpv])
            p.op("act", lambda e, pv=pv, t=t: e.copy(out=vtm[:, t, :], in_=pv[:, 0:TM_COLS]), reads=[bpv], writes=[b_vtm])
        p.dma("sp", sc.swa_v.ap()[blk], vtm[:, :, 0:64], c_vtm[0], reads=[b_vtm])
        p.dma("sp", sc.nsa_vs.ap()[blk], vtm[:, :, 64:128], c_vtm[1], reads=[b_vtm])
        p.dma("sp", sc.nsa_vw.ap()[blk], vtm[:, :, 128:192], c_vtm[2], reads=[b_vtm])
        p.dma("sp", sc.diff_v.ap()[blk], vtm[:, :, 192:448], c_vtm[3], reads=[b_vtm])
    for it in p.lists["sp"]:
        pass
    p.final_all_dma = True
    p.build()


CO = np.cumsum([0, 256, 128, 32, 512, 128, 128, 512, 128, 128, 128, 128, 128, 128, 24, 512, 512, 512])
(O_CQ, O_CKV, O_KR, O_SQ, O_SK, O_SV, O_NQ, O_NKC, O_NVC, O_NKS, O_NVS, O_NKW, O_NVW, O_NG, O_DQ, O_DK, O_DV) = [int(v) for v in CO[:-1]]
PERM32 = np.concatenate([np.arange(16, 32), np.arange(0, 16)])


def rope_tables():
    half = 16
    freqs = (10000.0 ** (-np.arange(half, dtype=np.float32) / half)).astype(np.float32)
    pos = np.arange(S, dtype=np.float32)
    ang = pos[:, None] * freqs[None, :]
    cos = np.cos(ang).astype(np.float32).T
    sin = np.sin(ang).astype(np.float32).T
    C = np.ones((96, S), np.float32)
    Sg = np.zeros((96, S), np.float32)
    C[64:80] = cos
    C[80:96] = cos
    Sg[64:80] = -sin
    Sg[80:96] = sin
    C = np.ascontiguousarray(C.reshape(96, 16, TB).transpose(1, 0, 2))
    Sg = np.ascontiguousarray(Sg.reshape(96, 16, TB).transpose(1, 0, 2))
    return C, Sg


def prep_B_weights(inp, l, r):
    w = np.asarray(inp["w_in"][l], np.float32)
    sl = lambda o, n: w[:, o:o + n]
    kr = sl(O_KR, 32)
    fmc = [sl(O_CQ, 256), sl(O_CKV, 128), kr, kr[:, PERM32],
           sl(O_SQ + 256 * r, 256), sl(O_SK + 64 * r, 64),
           sl(O_NQ + 256 * r, 256), sl(O_NKC + 64 * r, 64), sl(O_NVC + 64 * r, 64), sl(O_NKS + 64 * r, 64), sl(O_NKW + 64 * r, 64),
           sl(O_NG + 12 * r, 12), sl(O_DQ + 256 * r, 256), sl(O_DK + 256 * r, 256)]
    win_fm = np.ascontiguousarray(np.concatenate(fmc, 1))
    assert win_fm.shape[1] == FM_COLS
    win_tm = np.ascontiguousarray(np.concatenate([sl(O_SV + 64 * r, 64), sl(O_NVS + 64 * r, 64), sl(O_NVW + 64 * r, 64), sl(O_DV + 256 * r, 256)], 1))
    wuq = np.asarray(inp["mla_w_uq"][l], np.float32).reshape(256, 8, 96)[:, 4 * r:4 * r + 4]
    wuqp = wuq.copy()
    wuqp[:, :, 64:96] = wuq[:, :, 64:96][:, :, PERM32]
    wukv = np.asarray(inp["mla_w_ukv"][l], np.float32).reshape(128, 8, 128)[:, 4 * r:4 * r + 4]
    return dict(win_fm=win_fm, win_tm=win_tm, wuq=np.ascontiguousarray(wuq.reshape(256, 384)),
                wuqp=np.ascontiguousarray(wuqp.reshape(256, 384)),
                wukvk=np.ascontiguousarray(wukv[:, :, 0:64].reshape(128, 256)),
                wukvv=np.ascontiguousarray(wukv[:, :, 64:128].reshape(128, 256)),
                qnT=np.ascontiguousarray(np.asarray(inp["mla_q_norm"][l], np.float32).reshape(2, 128).T),
                kvn=np.ascontiguousarray(np.asarray(inp["mla_kv_norm"][l], np.float32).reshape(128, 1)))


def t5_bucket_np(n):
    n = np.maximum(n, 0)
    nf = np.maximum(n, 1).astype(np.float32)
    large = 16 + (np.log(nf / 16) / math.log(128 / 16) * 16).astype(np.int32)
    large = np.minimum(large, 31)
    return np.where(n < 16, n, large)


def b2_consts():
    c = {}
    c["identb"] = np.eye(128, dtype=np.float32).astype(ml_dtypes.bfloat16)
    c["identf"] = np.eye(128, dtype=np.float32)
    c["Jb"] = np.eye(128, dtype=np.float32)[::-1].copy().astype(ml_dtypes.bfloat16)
    k = np.arange(S)
    c["E"] = (k[None, :] // 64 == np.arange(128)[:, None]).astype(np.float32).astype(ml_dtypes.bfloat16)
    cl = np.arange(128)[:, None]
    ql = np.arange(512)[None, :]
    c["maskC"] = np.stack([np.where(16 * cl + 31 - ql <= 512 * dl, 0.0, NEGM) for dl in range(5)]).astype(np.float32).astype(ml_dtypes.bfloat16)
    d = np.arange(TVL) - 127
    bk = t5_bucket_np(d)
    oh = np.zeros((3, 33, TVL), np.float32)
    for v, hi in enumerate((10 ** 9, 128, 512)):
        ok = (d >= 0) & (d < hi)
        for b in range(32):
            oh[v, b] = ((bk == b) & ok)
        oh[v, 32] = ~ok
    c["OH"] = oh
    qq = np.arange(128)[:, None]
    m = np.arange(256)[None, :] - 126
    cc = (qq >= 64).astype(np.int32)
    c["SA"] = (m < cc - 1).astype(np.float32)
    c["SB"] = np.where((m == cc) | (m == cc - 1), 1e4, np.where(m > cc, -1.0, 0.0)).astype(np.float32)
    cs = np.arange(512) * 16
    ss = np.arange(128) * 64
    ov = ((cs[:, None] < ss[None, :] + 64) & (ss[None, :] < cs[:, None] + 32)).astype(np.float32)
    ov[511] = 0
    ovx = np.concatenate([ov, np.ones((512, 1), np.float32)], 1).reshape(4, 128, 129).transpose(1, 0, 2)
    c["ovx"] = np.ascontiguousarray(ovx).astype(ml_dtypes.bfloat16)
    sel = np.zeros((12, 12 * 64), np.float32)
    for r_ in range(12):
        sel[r_, r_ * 64:(r_ + 1) * 64] = 1
    c["Sel"] = sel
    return c


def prep_B2_inputs(inp, l, r):
    tab = np.asarray(inp["rel_bias_table"], np.float32)
    tabs = np.zeros((33, NKIND), np.float32)
    tabs[:32, 0:4] = tab[:, 8 + 4 * r:8 + 4 * r + 4]
    tabs[:32, 4:6] = tab[:, 16 + 2 * r:16 + 2 * r + 2]
    tabs[:32, 7:11] = tab[:, 4 * r:4 * r + 4]
    tabs[:32, 11:15] = tab[:, 8 + 4 * r:8 + 4 * r + 4]
    tabs[32, :] = NEGM
    d = dict(tabs=tabs,
             sinks=np.ascontiguousarray(np.asarray(inp["swa_sinks"][l], np.float32)[4 * r:4 * r + 4]),
             lamp=np.ascontiguousarray(np.asarray(inp["diff_lambda"][l], np.float32).reshape(1, 256)),
             subln=np.ascontiguousarray(np.asarray(inp["diff_subln"][l], np.float32).reshape(128, 1)),
             w1=np.ascontiguousarray(np.asarray(inp["nsa_cmp_w1"][l], np.float32)),
             w2=np.ascontiguousarray(np.asarray(inp["nsa_cmp_w2"][l], np.float32)),
             pos=np.ascontiguousarray(np.asarray(inp["nsa_cmp_pos"][l], np.float32).reshape(2, 16, 2, 64).transpose(0, 2, 3, 1).reshape(2, 128, 16)))
    return d


def phase_B2(nc, tag, sc, yT_out, cst, bi, lam_init):
    p = Prog(nc, tag)
    S_ = Ring(p, 3, [128, 512], F32, "S", psum=True, chan=False)
    O_ = Ring(p, 2, [128, 512], F32, "O", psum=True, chan=False)
    X_ = Ring(p, 3, [128, 512], F32, "X", psum=True, chan=False)
    P_ = Ring(p, 4, [128, 512], BF16, "P", chan=False)
    KT_ = Ring(p, 4, [96, TB], BF16, "KT")
    V65 = Ring(p, 4, [128, 4, 65], BF16, "V65")
    V128 = Ring(p, 4, [128, 4, 128], BF16, "V128")

    def const(shape, dt, src, eng="sp"):
        t = p.sbuf(shape, dt, "k"); b = Buf(); c = p.chan()
        p.dma(eng, t[:], src, c, writes=[b])
        return t, b
    identb, b_idb = const([128, 128], BF16, cst["identb"])
    identf, b_idf = const([128, 128], F32, cst["identf"])
    Jb, b_J = const([128, 128], BF16, cst["Jb"])
    E, b_E = const([128, S], BF16, cst["E"])
    maskC, b_mC = const([128, 5, 512], BF16, cst["maskC"].rearrange("v p q -> p v q"))
    SA, b_SA = const([128, 256], F32, cst["SA"])
    SB, b_SB = const([128, 256], F32, cst["SB"])
    ovx, b_ov = const([128, 4, 129], BF16, cst["ovx"])
    Sel, b_Sel = const([12, 768], F32, cst["Sel"])
    tabs, b_tabs = const([33, NKIND], F32, bi["tabs"])
    cb, b_cb = const([128, NKIND], F32, bass.AP(bi["tabs"].tensor, 31 * NKIND, [[0, 128], [1, NKIND]]))
    sinke, b_sk = const([128, 4], F32, bass.AP(bi["sinks"].tensor, 0, [[0, 128], [1, 4]]))
    p.op("act", lambda e: e.activation(out=sinke[:], in_=sinke[:], func=AF.Exp), reads=[b_sk], writes=[b_sk])
    subln, b_sub = const([128, 1], F32, bi["subln"])
    p.op("dve", lambda e: e.tensor_scalar(out=subln[:], in0=subln[:], scalar1=1.0 - lam_init, scalar2=None, op0=ALU.mult),
         reads=[b_sub], writes=[b_sub])
    ones_f = p.sbuf([128, 128], F32, "ones"); b_ones = Buf()
    p.op("pool", lambda e: e.memset(ones_f[:], 1.0), writes=[b_ones])
    onesb = p.sbuf([128, 1], BF16, "onesb"); b_onesb = Buf()
    p.op("pool", lambda e: e.memset(onesb[:], 1.0), writes=[b_onesb])
    zb = p.sbuf([128, 1], F32, "zb"); b_zb = Buf()
    p.op("pool", lambda e: e.memset(zb[:], 0.0), writes=[b_zb])
    eps = p.sbuf([128, 1], F32, "eps"); b_eps = Buf()
    p.op("pool", lambda e: e.memset(eps[:], EPS), writes=[b_eps])
    for (t, b, _) in V65.items:
        p.op("pool", lambda e, t=t: e.memset(t[:], 1.0), writes=[b])
    lamp, b_lp = const([1, 256], F32, bi["lamp"])
    lt = p.sbuf([1, 128], F32, "lt"); b_lt = Buf()
    l2 = p.sbuf([1, 4], F32, "l2"); b_l2 = Buf()
    for j in range(2):
        p.op("dve", lambda e, j=j: e.tensor_tensor(out=lt[:, j * 64:(j + 1) * 64], in0=lamp[:, j * 128:j * 128 + 64],
                                                    in1=lamp[:, j * 128 + 64:j * 128 + 128], op=ALU.mult), reads=[b_lp], writes=[b_lt])
        p.op("act", lambda e, j=j: e.activation(out=lt[:, j * 64:(j + 1) * 64], in_=lt[:, j * 64:(j + 1) * 64], func=AF.Identity,
                                                accum_out=l2[:, j:j + 1]), reads=[b_lt], writes=[b_lt, b_l2])
    p.op("act", lambda e: e.activation(out=l2[:, 0:2], in_=l2[:, 0:2], func=AF.Exp), reads=[b_l2], writes=[b_l2])
    p.op("dve", lambda e: e.tensor_tensor(out=l2[:, 2:3], in0=l2[:, 1:2], in1=l2[:, 0:1], op=ALU.subtract), reads=[b_l2], writes=[b_l2])
    p.op("dve", lambda e: e.tensor_scalar(out=l2[:, 3:4], in0=l2[:, 2:3], scalar1=-lam_init, scalar2=None, op0=ALU.add),
         reads=[b_l2], writes=[b_l2])
    neglam = l2[0:1, 3:4]
    OHs = p.sbuf([33, TVL], F32, "OH"); b_OH = Buf(); c_OH = p.chan()
    tvs = p.sbuf([8, TVL], BF16, "tvs"); b_tvs = Buf(); c_tvs = p.chan()
    strips = p.sbuf([128, NKIND, SW], BF16, "strips"); b_st = Buf(); c_st = [p.chan() for _ in range(NKIND)]
    for v, (k0, k1) in enumerate(((0, 7), (7, 11), (11, 15))):
        p.dma("sp", OHs[:], cst["OH"][v], c_OH, writes=[b_OH])
        for cg in range(3):
            ps, bp, _ = X_.next()
            p.op("pe", lambda e, ps=ps, k0=k0, k1=k1, cg=cg: e.matmul(ps[0:k1 - k0, 0:384], lhsT=tabs[:, k0:k1], rhs=OHs[:, cg * 384:(cg + 1) * 384],
                                                                     start=True, stop=True), reads=[b_tabs, b_OH], writes=[bp])
            p.op("dve", lambda e, ps=ps, k0=k0, k1=k1, cg=cg: e.tensor_copy(out=tvs[0:k1 - k0, cg * 384:(cg + 1) * 384], in_=ps[0:k1 - k0, 0:384]),
                 reads=[bp], writes=[b_tvs])
        tk = p.dma("sp", sc.tvec.ap()[k0:k1, :], tvs[0:k1 - k0, :], c_tvs, reads=[b_tvs])
        for kd in range(k0, k1):
            p._wait("sp", tk)
            p.dma("sp", strips[:, kd, :], bass.AP(sc.tvec, kd * TVL, [[1, 128], [1, SW]]), c_st[kd], writes=[])
    st_toks = [("d", ch, 16) for ch in c_st]
    for tk in st_toks:
        p._wait("pe", tk)
    K_SEL, K_DIFF, K_CAUS, K_SWA, K_WIN = 0, 4, 6, 7, 11
    import os as _os
    if _os.environ.get('B2CUT') == '1':
        p.build()
        return

    kc2 = p.sbuf([128, S + 32], BF16, "kc2"); b_kc2 = Buf(); c_kc2 = [p.chan(), p.chan()]
    w1s = p.sbuf([128, 16, 128], BF16, "w1s"); b_w1 = Buf(); c_w1 = p.chan()
    w2s = p.sbuf([128, 64], BF16, "w2s"); b_w2 = Buf(); c_w2 = p.chan()
    posf = p.sbuf([128, 16], F32, "posf"); b_pf = Buf(); c_pf = p.chan()
    posb = p.sbuf([128, 16], BF16, "posb"); b_pb = Buf()
    b1 = p.sbuf([128, 1], F32, "b1"); b_b1 = Buf()
    xs = p.sbuf([128, 512], F32, "xs"); b_xs = Buf()
    x2 = p.sbuf([128, 512], F32, "x2"); b_x2 = Buf()
    hdn = p.sbuf([128, 512], BF16, "hdn"); b_hdn = Buf()
    kcmpT = p.sbuf([64, 512], BF16, "kcmpT"); b_kcm = Buf()
    vcx = p.sbuf([128, 4, 65], BF16, "vcx"); b_vcx = Buf()
    p.op("pool", lambda e: e.memset(vcx[:], 1.0), writes=[b_vcx])
    p.op("pool", lambda e: e.memset(hdn[:], 0.0), writes=[b_hdn])
    p.op("pool", lambda e: e.memset(kc2[:, S - 8:S + 32], 0.0), writes=[b_kc2])
    for kv, src in ((0, sc.nsa_kcT), (1, sc.nsa_vcT)):
        p.dma("sp", kc2[0:64, 0:S], src.ap()[:, 0:S], c_kc2[0], writes=[b_kc2])
        p.dma("sp", kc2[64:128, 0:S - 1], src.ap()[:, 1:S], c_kc2[1], writes=[b_kc2])
        p.dma("pool", w1s[:], bi["w1"][kv].rearrange("(j p) n -> p j n", p=128), c_w1, writes=[b_w1])
        p.dma("pool", w2s[:], bi["w2"][kv], c_w2, writes=[b_w2])
        p.dma("sp", posf[:], bi["pos"][kv], c_pf, writes=[b_pf])
        p.op("dve", lambda e: e.tensor_copy(out=posb[:], in_=posf[:]), reads=[b_pf], writes=[b_pb])
        ps, bp, _ = X_.next()
        for j in range(16):
            p.op("pe", lambda e, ps=ps, j=j: e.matmul(ps[:, 0:1], lhsT=w1s[:, j, :], rhs=posb[:, j:j + 1], start=(j == 0), stop=(j == 15)),
                 reads=[b_w1, b_pb], writes=[bp])
        p.op("dve", lambda e, ps=ps: e.tensor_copy(out=b1[:], in_=ps[:, 0:1]), reads=[bp], writes=[b_b1])
        ph, bph, _ = X_.next()
        for j in range(16):
            p.op("pe", lambda e, ph=ph, j=j: e.matmul(ph[:, 0:511], lhsT=w1s[:, j, :], rhs=kc2[:, 2 * j:2 * j + 16 * 511:16],
                                                      start=(j == 0), stop=(j == 15)), reads=[b_w1, b_kc2], writes=[bph])
        p.op("act", lambda e, ph=ph: e.activation(out=xs[:, 0:511], in_=ph[:, 0:511], func=AF.Identity, bias=b1[:], scale=1.0),
             reads=[bph, b_b1], writes=[b_xs])
        p.op("dve", lambda e: e.tensor_tensor(out=x2[:, 0:511], in0=xs[:, 0:511], in1=xs[:, 0:511], op=ALU.mult), reads=[b_xs], writes=[b_x2])
        p.op("dve", lambda e: e.tensor_scalar(out=x2[:, 0:511], in0=x2[:, 0:511], scalar1=0.044715, scalar2=1.0, op0=ALU.mult, op1=ALU.add),
             reads=[b_x2], writes=[b_x2])
        p.op("dve", lambda e: e.tensor_tensor(out=x2[:, 0:511], in0=x2[:, 0:511], in1=xs[:, 0:511], op=ALU.mult), reads=[b_x2, b_xs], writes=[b_x2])
        p.op("act", lambda e: e.activation(out=x2[:, 0:511], in_=x2[:, 0:511], func=AF.Sigmoid, scale=1.5957691216057308),
             reads=[b_x2], writes=[b_x2])
        p.op("dve", lambda e: e.tensor_tensor(out=hdn[:, 0:511], in0=x2[:, 0:511], in1=xs[:, 0:511], op=ALU.mult), reads=[b_x2, b_xs], writes=[b_hdn])
        if kv == 0:
            pk, bpk, _ = X_.next()
            p.op("pe", lambda e, pk=pk: e.matmul(pk[0:64, :], lhsT=w2s[:], rhs=hdn[:], start=True, stop=True), reads=[b_w2, b_hdn], writes=[bpk])
            p.op("act", lambda e, pk=pk: e.copy(out=kcmpT[:], in_=pk[0:64, :]), reads=[bpk], writes=[b_kcm])
        else:
            for ct in range(4):
                pk, bpk, _ = X_.next()
                p.op("pe", lambda e, pk=pk, ct=ct: e.matmul(pk[:, 0:64], lhsT=hdn[:, ct * 128:(ct + 1) * 128], rhs=w2s[:], start=True, stop=True),
                     reads=[b_w2, b_hdn], writes=[bpk])
                p.op("act", lambda e, pk=pk, ct=ct: e.copy(out=vcx[:, ct, 0:64], in_=pk[:, 0:64]), reads=[bpk], writes=[b_vcx])

    if _os.environ.get('B2CUT') == '2':
        p.build()
        return
    qm = p.sbuf([96, 4, TB], BF16, "qm"); b_qm = Buf(); c_qm = p.chan()
    qs = p.sbuf([64, 4, TB], BF16, "qs"); b_qs = Buf(); c_qs = p.chan()
    qn_ = p.sbuf([64, 4, TB], BF16, "qn"); b_qn = Buf(); c_qn = p.chan()
    qd = p.sbuf([64, 4, TB], BF16, "qd"); b_qd = Buf(); c_qd = p.chan()
    gsg = p.sbuf([12, TB], F32, "gsg"); b_gsg = Buf(); c_gsg = p.chan()
    rd = p.sbuf([65, TB], F32, "rd"); b_rd = Buf()
    osb = Ring(p, 2, [128, TB], F32, "osb", chan=False)
    ost = Ring(p, 3, [128, TB], BF16, "ost")
    p_acc = [(p.sbuf([64, TB], F32, "acc"), Buf()) for _ in range(4)]
    tmpn = p.sbuf([64, TB], F32, "tmpn"); b_tmpn = Buf()
    d1 = p.sbuf([128, TB], F32, "d1"); b_d1 = Buf()
    d2 = p.sbuf([128, TB], F32, "d2"); b_d2 = Buf()
    dsq = p.sbuf([128, TB], F32, "dsq"); b_dsq = Buf()
    drs = p.sbuf([128, TB], F32, "drs"); b_drs = Buf()
    impa = p.sbuf([128, 4, 128], F32, "impa"); b_impa = Buf()
    rdi = p.sbuf([128, 8], F32, "rdi"); b_rdi = Buf()
    top = p.sbuf([128, 16], F32, "top"); b_top = Buf()
    wk = p.sbuf([128, 128], F32, "wk"); b_wk = Buf()
    mq = p.sbuf([128, 128], F32, "mq"); b_mq = Buf()
    MT = p.sbuf([128, TB], BF16, "MT"); b_MT = Buf()
    SGC = True

    pend = []

    def defer(fn):
        pend.append(fn)

    def flush():
        while pend:
            pend.pop(0)()

    def mm(out, lhsT, rhs, start, stop):
        return lambda e: e.matmul(out, lhsT=lhsT, rhs=rhs, start=start, stop=stop, skip_group_check=SGC)

    def attn(qT, b_q, tiles, scale, ops, bo, M, den=None, after=None, n=None):
        if n is None:
            tiles = list(tiles)
            n = len(tiles)

        def stage1(tl):
            q0 = tl["q0"]
            sp_, bs, _ = S_.next()
            ex = tl.get("ex", [])
            p.op("pe", mm(sp_[:, q0:], tl["kT"], qT[:, q0:], True, not ex), reads=[b_q] + tl["kb"], writes=[bs])
            for j, (l_, r_, bufs) in enumerate(ex):
                p.op("pe", mm(sp_[:, q0:], l_, r_, False, j == len(ex) - 1), reads=bufs, writes=[bs])
            P, bP, _ = P_.next()
            bias = tl.get("bias", zb[:, 0:1])
            p.op("act", lambda e, P=P, sp_=sp_, q0=q0, bias=bias: e.activation(out=P[:, q0:], in_=sp_[:, q0:], func=AF.Exp, bias=bias, scale=scale),
                 reads=[bs, b_zb, b_cb], writes=[bP])
            return P, bP

        def stage2(i, tl, P, bP):
            q0 = tl["q0"]
            p.op("pe", mm(ops[0:M, q0:], tl["v"], P[:, q0:], i == 0, i == n - 1), reads=[bP] + tl["vb"], writes=[bo])
            if den is not None:
                p.op("pe", mm(den[0][0:1, q0:], onesb[:, 0:1], P[:, q0:], i == 0, i == n - 1), reads=[bP, b_onesb], writes=[den[1]])
            if after is not None:
                after(P, bP, tl, i, n)

        q_ = []
        for i, tl in enumerate(tiles):
            P, bP = stage1(tl)
            if i == 0:
                flush()
            q_.append((i, tl, P, bP))
            if len(q_) > 2:
                stage2(*q_.pop(0))
        while q_:
            stage2(*q_.pop(0))

    def load_kv(kT_src, dk, v_src, ring, dv):
        kt, bk, ck = KT_.next()
        p.dma("sp", kt[0:dk, :], kT_src, ck, writes=[bk])
        vt, bv, cv = ring.next()
        p.dma("sp", vt[:, :, 0:dv], v_src, cv, writes=[bv])
        return kt, bk, vt, bv

    def normalize(ops, bo, dv, den_ps, bden, dp, sink=None, mulneg=False, dest=None, bdest=None):
        p.op("dve", lambda e: e.tensor_scalar(out=rd[dp:dp + 1, :], in0=den_ps[dp:dp + 1, :], scalar1=(sink if sink is not None else 0.0),
                                              scalar2=1e-30, op0=ALU.add, op1=ALU.max), reads=[bden, b_sk], writes=[b_rd])
        p.op("dve", lambda e: e.reciprocal(out=rd[dp:dp + 1, :], in_=rd[dp:dp + 1, :]), reads=[b_rd], writes=[b_rd])
        if mulneg:
            p.op("dve", lambda e: e.tensor_scalar(out=rd[dp:dp + 1, :], in0=rd[dp:dp + 1, :], scalar1=neglam, scalar2=None, op0=ALU.mult),
                 reads=[b_rd, b_l2], writes=[b_rd])
        bc, bbc, _ = X_.next()
        p.op("pe", lambda e: e.matmul(bc[0:dv, :], lhsT=ones_f[dp:dp + 1, 0:dv], rhs=rd[dp:dp + 1, :], start=True, stop=True),
             reads=[b_ones, b_rd], writes=[bbc])
        o, bo_, _ = osb.next()
        p.op("act", lambda e: e.copy(out=o[0:dv, :], in_=ops[0:dv, :]), reads=[bo], writes=[bo_])
        p.op("dve", lambda e: e.tensor_tensor(out=dest[0:dv, :], in0=o[0:dv, :], in1=bc[0:dv, :], op=ALU.mult), reads=[bo_, bbc], writes=[bdest])

    import os as _os
    gsg_cur = (gsg, b_gsg)
    for qb in range(int(_os.environ.get('NQB', '16'))):
        flush()
        p.dma("sp", qm[:], sc.mla_qT.ap()[qb].rearrange("h d t -> d h t"), c_qm, writes=[b_qm])
        p.dma("sp", qs[:], sc.swa_qT.ap()[qb].rearrange("(h d) t -> d h t", d=64), c_qs, writes=[b_qs])
        p.dma("sp", qn_[:], sc.nsa_qT.ap()[qb].rearrange("(h d) t -> d h t", d=64), c_qn, writes=[b_qn])
        p.dma("sp", qd[:], sc.diff_qT.ap()[qb].rearrange("(h d) t -> d h t", d=64), c_qd, writes=[b_qd])
        p.dma("sp", gsg[:], sc.nsa_gsig.ap()[qb], c_gsg, writes=[b_gsg])

        def causal_tiles(kT_of, dk, v_of, ring, dv, strip_kind, far_bias, near_prev):
            nxt = load_kv(kT_of(0), dk, v_of(0), ring, dv)
            for kb in range(qb + 1):
                kt, bk, vt, bv = nxt
                if kb < qb:
                    nxt = load_kv(kT_of(kb + 1), dk, v_of(kb + 1), ring, dv)
                for t4 in range(4):
                    tl = dict(kT=kt[0:dk, t4 * 128:(t4 + 1) * 128], kb=[bk], v=vt[:, t4, :], vb=[bv], q0=0)
                    if kb == qb:
                        tl["q0"] = 128 * t4
                        tl["ex"] = [(Jb[:], strips[:, strip_kind, 0:512 - 128 * t4], [b_J])]
                    elif near_prev and kb == qb - 1 and t4 == 3:
                        tl["ex"] = [(Jb[:], strips[:, strip_kind, 128:640], [b_J])]
                    elif far_bias is not None:
                        tl["bias"] = far_bias
                    yield tl

        for h in range(4):
            ops, bo, _ = O_.next()
            tiles = causal_tiles(lambda kb: sc.mla_kT.ap()[kb, h], 96, lambda kb: sc.mla_v.ap()[kb][:, :, h * 64:(h + 1) * 64], V65, 64, K_CAUS, None, False)
            attn(qm[:, h, :], b_qm, tiles, 96 ** -0.5, ops, bo, 65, n=4 * (qb + 1))

            def epi(ops=ops, bo=bo, h=h, qb=qb):
                yt, byt, cyt = ost.next()
                normalize(ops, bo, 64, ops, bo, 64, dest=yt, bdest=byt)
                p.dma("sp", yT_out.ap()[qb, 0, h * 64:(h + 1) * 64, :], yt[0:64, :], cyt, reads=[byt], final=True)
            defer(epi)
        kbs = ([qb - 1] if qb > 0 else []) + [qb]
        kvl = {kb: load_kv(sc.swa_kT.ap()[kb], 64, sc.swa_v.ap()[kb], V65, 64) for kb in kbs}
        for g in range(4):
            tiles = []
            if qb > 0:
                kt, bk, vt, bv = kvl[qb - 1]
                tiles.append(dict(kT=kt[0:64, 384:512], kb=[bk], v=vt[:, 3, :], vb=[bv], q0=0, ex=[(Jb[:], strips[:, K_SWA + g, 128:640], [b_J])]))
            kt, bk, vt, bv = kvl[qb]
            for t4 in range(4):
                tiles.append(dict(kT=kt[0:64, t4 * 128:(t4 + 1) * 128], kb=[bk], v=vt[:, t4, :], vb=[bv], q0=128 * t4,
                                  ex=[(Jb[:], strips[:, K_SWA + g, 0:512 - 128 * t4], [b_J])]))
            if qb == 0:
                tiles[0]["q0"] = 0
            ops, bo, _ = O_.next()
            attn(qs[:, g, :], b_qs, tiles, 1.0, ops, bo, 65)

            def epi(ops=ops, bo=bo, g=g, qb=qb):
                yt, byt, cyt = ost.next()
                normalize(ops, bo, 64, ops, bo, 64, sink=sinke[64:65, g:g + 1], dest=yt, bdest=byt)
                p.dma("sp", yT_out.ap()[qb, 1, g * 64:(g + 1) * 64, :], yt[0:64, :], cyt, reads=[byt], final=True)
            defer(epi)
        for h in range(2):
            for m_ in range(2):
                ops, bo, _ = O_.next()
                dps, bd, _ = X_.next()
                tiles = causal_tiles(lambda kb: sc.diff_kT.ap()[kb, (2 * h + m_) * 64:(2 * h + m_ + 1) * 64, :], 64,
                                     lambda kb: sc.diff_v.ap()[kb][:, :, h * 128:(h + 1) * 128], V128, 128, K_DIFF + h, cb[:, K_DIFF + h:K_DIFF + h + 1], True)
                attn(qd[:, 2 * h + m_, :], b_qd, tiles, 1.0, ops, bo, 128, den=(dps, bd), n=4 * (qb + 1))

                def epi(ops=ops, bo=bo, dps=dps, bd=bd, m_=m_, h=h, qb=qb):
                    normalize(ops, bo, 128, dps, bd, 0, mulneg=(m_ == 1), dest=(d1 if m_ == 0 else d2), bdest=(b_d1 if m_ == 0 else b_d2))
                    if m_ == 0:
                        return
                    p.op("pool", lambda e: e.tensor_tensor(out=d1[:], in0=d1[:], in1=d2[:], op=ALU.add), reads=[b_d1, b_d2], writes=[b_d1])
                    p.op("act", lambda e: e.activation(out=dsq[:], in_=d1[:], func=AF.Square), reads=[b_d1], writes=[b_dsq])
                    ss, bss, _ = X_.next()
                    p.op("pe", lambda e, ss=ss: e.matmul(ss[:], lhsT=ones_f[:], rhs=dsq[:], start=True, stop=True), reads=[b_ones, b_dsq], writes=[bss])
                    p.op("act", lambda e, ss=ss: e.activation(out=drs[:], in_=ss[:], func=AF.Sqrt, bias=eps[:], scale=1.0 / 128), reads=[bss, b_eps], writes=[b_drs])
                    p.op("dve", lambda e: e.reciprocal(out=drs[:], in_=drs[:]), reads=[b_drs], writes=[b_drs])
                    yt, byt, cyt = ost.next()
                    p.op("dve", lambda e, yt=yt: e.scalar_tensor_tensor(out=yt[:], in0=d1[:], scalar=subln[:, 0:1], in1=drs[:], op0=ALU.mult, op1=ALU.mult),
                         reads=[b_d1, b_sub, b_drs], writes=[byt])
                    p.dma("sp", yT_out.ap()[qb, 3, h * 128:(h + 1) * 128, :], yt[:], cyt, reads=[byt], final=True)
                defer(epi)
        flush()
        p.op("pool", lambda e: e.memset(impa[:], 0.0), writes=[b_impa])
        cts = [ct for ct in range(4) if qb - 4 * ct >= 0]
        oc_keep = []
        for g in range(4):
            tiles = []
            for ct in cts:
                dl = qb - 4 * ct
                tl = dict(kT=kcmpT[:, ct * 128:(ct + 1) * 128], kb=[b_kcm], v=vcx[:, ct, :], vb=[b_vcx], q0=0, ct=ct)
                if dl <= 4:
                    tl["ex"] = [(identb[:], maskC[:, dl, :], [b_idb, b_mC])]
                tiles.append(tl)
            flush()
            ops, bo, _ = O_.next()
            ips = [X_.next(), X_.next()]
            for ip, bip, _ in ips:
                p.op("dve", lambda e, ip=ip: e.memset(ip[:], 0.0), writes=[bip])

            def after(P, bP, tl, i, n, ips=ips):
                for t4 in range(4):
                    ip, bip, _ = ips[t4 // 2]
                    p.op("pe", mm(ip[:, (t4 % 2) * 129:(t4 % 2) * 129 + 129], P[:, t4 * 128:(t4 + 1) * 128], ovx[:, tl["ct"], :], False, i == n - 1),
                         reads=[bP, b_ov], writes=[bip])
            attn(qn_[:, g, :], b_qn, tiles, 1.0, ops, bo, 65, after=after)
            for t4 in range(4):
                ip, bip, _ = ips[t4 // 2]
                o_ = (t4 % 2) * 129
                p.op("dve", lambda e, ip=ip, o_=o_, t4=t4: e.tensor_scalar(out=rdi[:, t4:t4 + 1], in0=ip[:, o_ + 128:o_ + 129], scalar1=1e-30, scalar2=None, op0=ALU.max),
                     reads=[bip], writes=[b_rdi])
                p.op("dve", lambda e, t4=t4: e.reciprocal(out=rdi[:, t4:t4 + 1], in_=rdi[:, t4:t4 + 1]), reads=[b_rdi], writes=[b_rdi])
                p.op("dve", lambda e, ip=ip, o_=o_, t4=t4: e.scalar_tensor_tensor(out=impa[:, t4, :], in0=ip[:, o_:o_ + 128], scalar=rdi[:, t4:t4 + 1],
                                                                                  in1=impa[:, t4, :], op0=ALU.mult, op1=ALU.add),
                     reads=[bip, b_rdi, b_impa], writes=[b_impa])
            oc_keep.append((ops, bo))
            gb_, bgb, _ = X_.next()
            p.op("pe", lambda e, gb_=gb_, g=g: e.matmul(gb_[0:64, :], lhsT=Sel[:, (g * 3) * 64:(g * 3 + 1) * 64], rhs=gsg[:], start=True, stop=True),
                 reads=[b_Sel, b_gsg], writes=[bgb])
            normalize(ops, bo, 64, ops, bo, 64, dest=tmpn, bdest=b_tmpn)
            accg = p_acc[g]
            p.op("dve", lambda e, gb_=gb_, accg=accg: e.tensor_tensor(out=accg[0][:], in0=tmpn[:], in1=gb_[0:64, :], op=ALU.mult),
                 reads=[b_tmpn, bgb], writes=[accg[1]])
        for t4 in range(4):
            T = 4 * qb + t4
            c0 = 126 - 2 * T
            p.op("dve", lambda e, t4=t4, c0=c0: e.tensor_tensor(out=impa[:, t4, :], in0=impa[:, t4, :], in1=SA[:, c0:c0 + 128], op=ALU.mult),
                 reads=[b_impa, b_SA], writes=[b_impa])
            p.op("dve", lambda e, t4=t4, c0=c0: e.tensor_tensor(out=impa[:, t4, :], in0=impa[:, t4, :], in1=SB[:, c0:c0 + 128], op=ALU.add),
                 reads=[b_impa, b_SB], writes=[b_impa])
            p.op("dve", lambda e, t4=t4: e.memset(impa[:, t4, 0:1], 1e4), reads=[], writes=[b_impa])
            p.op("dve", lambda e, t4=t4: e.max(out=top[:, 0:8], in_=impa[:, t4, :]), reads=[b_impa], writes=[b_top])
            p.op("dve", lambda e, t4=t4: e.match_replace(out=wk[:], in_to_replace=top[:, 0:8], in_values=impa[:, t4, :], imm_value=-1e30),
                 reads=[b_impa, b_top], writes=[b_wk])
            p.op("dve", lambda e: e.max(out=top[:, 8:16], in_=wk[:]), reads=[b_wk], writes=[b_top])
            p.op("dve", lambda e, t4=t4: e.tensor_scalar(out=mq[:], in0=impa[:, t4, :], scalar1=top[:, 15:16], scalar2=1.0, op0=ALU.is_ge, op1=ALU.subtract),
                 reads=[b_impa, b_top], writes=[b_mq])
            p.op("dve", lambda e: e.tensor_scalar(out=mq[:], in0=mq[:], scalar1=-NEGM, scalar2=None, op0=ALU.mult), reads=[b_mq], writes=[b_mq])
            tp, btp, _ = X_.next()
            p.op("pe", lambda e, tp=tp: e.transpose(tp[:, 0:128], mq[:], identf[:]), reads=[b_mq, b_idf], writes=[btp])
            p.op("act", lambda e, tp=tp, t4=t4: e.copy(out=MT[:, t4 * 128:(t4 + 1) * 128], in_=tp[:, 0:128]), reads=[btp], writes=[b_MT])
        kvs = [load_kv(sc.nsa_ksT.ap()[kb], 64, sc.nsa_vs.ap()[kb], V65, 64) for kb in range(0)]
        for g in range(4):
            accg = p_acc[g]
            for br in (1, 2):
                tiles = []
                nt = None
                if br == 1:
                    def sel_tiles(g=g):
                        nxt = load_kv(sc.nsa_ksT.ap()[0], 64, sc.nsa_vs.ap()[0], V65, 64)
                        for kb in range(qb + 1):
                            kt, bk, vt, bv = nxt
                            if kb < qb:
                                nxt = load_kv(sc.nsa_ksT.ap()[kb + 1], 64, sc.nsa_vs.ap()[kb + 1], V65, 64)
                            for t4 in range(4):
                                KT = kb * 4 + t4
                                tl = dict(kT=kt[0:64, t4 * 128:(t4 + 1) * 128], kb=[bk], v=vt[:, t4, :], vb=[bv], q0=0)
                                q0 = 128 * t4 if kb == qb else 0
                                tl["q0"] = q0
                                tl["ex"] = [(E[:, KT * 128:(KT + 1) * 128], MT[:, q0:], [b_E, b_MT])]
                                if kb == qb:
                                    tl["ex"].append((Jb[:], strips[:, K_SEL + g, 0:512 - q0], [b_J]))
                                elif kb == qb - 1 and t4 == 3:
                                    tl["ex"].append((Jb[:], strips[:, K_SEL + g, 128:640], [b_J]))
                                else:
                                    tl["bias"] = cb[:, K_SEL + g:K_SEL + g + 1]
                                yield tl
                    tiles = sel_tiles()
                    nt = 4 * (qb + 1)
                else:
                    if qb > 0:
                        kt, bk, vt, bv = load_kv(sc.nsa_kwT.ap()[qb - 1], 64, sc.nsa_vw.ap()[qb - 1], V65, 64)
                        for t4 in range(4):
                            rel = 512 - 128 * t4
                            tiles.append(dict(kT=kt[0:64, t4 * 128:(t4 + 1) * 128], kb=[bk], v=vt[:, t4, :], vb=[bv], q0=0,
                                              ex=[(Jb[:], strips[:, K_WIN + g, rel:rel + 512], [b_J])]))
                    kt, bk, vt, bv = load_kv(sc.nsa_kwT.ap()[qb], 64, sc.nsa_vw.ap()[qb], V65, 64)
                    for t4 in range(4):
                        tiles.append(dict(kT=kt[0:64, t4 * 128:(t4 + 1) * 128], kb=[bk], v=vt[:, t4, :], vb=[bv], q0=128 * t4,
                                          ex=[(Jb[:], strips[:, K_WIN + g, 0:512 - 128 * t4], [b_J])]))
                ops, bo, _ = O_.next()
                attn(qn_[:, g, :], b_qn, tiles, 1.0, ops, bo, 65, n=nt)

                def epi(ops=ops, bo=bo, g=g, br=br, accg=accg, qb=qb):
                    gb_, bgb, _ = X_.next()
                    p.op("pe", lambda e: e.matmul(gb_[0:64, :], lhsT=Sel[:, (g * 3 + br) * 64:(g * 3 + br + 1) * 64], rhs=gsg_cur[0][:], start=True, stop=True),
                         reads=[b_Sel, gsg_cur[1]], writes=[bgb])
                    normalize(ops, bo, 64, ops, bo, 64, dest=tmpn, bdest=b_tmpn)
                    p.op("dve", lambda e: e.tensor_tensor(out=tmpn[:], in0=tmpn[:], in1=gb_[0:64, :], op=ALU.mult), reads=[b_tmpn, bgb], writes=[b_tmpn])
                    p.op("pool", lambda e: e.tensor_tensor(out=accg[0][:], in0=accg[0][:], in1=tmpn[:], op=ALU.add), reads=[accg[1], b_tmpn], writes=[accg[1]])
                    if br == 2:
                        yt, byt, cyt = ost.next()
                        p.op("act", lambda e: e.copy(out=yt[0:64, :], in_=accg[0][:]), reads=[accg[1]], writes=[byt])
                        p.dma("sp", yT_out.ap()[qb, 2, g * 64:(g + 1) * 64, :], yt[0:64, :], cyt, reads=[byt], final=True)
                defer(epi)
    flush()
    p.build()


def build_launch_B(l):
    nc = bass.Bass("TRN2", target_bir_lowering=False)
    E_ = lambda n, s, dt=F32: nc.dram_tensor(n, list(s), dt, kind="ExternalInput")
    uT_in = E_("uT_in", [16, 128, 8, TB], BF16)
    shp = dict(win_fm=[D, FM_COLS], win_tm=[D, TM_COLS], wuq=[256, 384], wuqp=[256, 384], wukvk=[128, 256], wukvv=[128, 256],
               qnT=[128, 2], kvn=[128, 1])
    a = {k: E_(k, v).ap() for k, v in shp.items()}
    ropeC = E_("ropeC", [16, 96, TB]).ap()
    ropeS = E_("ropeS", [16, 96, TB]).ap()
    cshp = dict(identb=([128, 128], BF16), identf=([128, 128], F32), Jb=([128, 128], BF16), E=([128, S], BF16), maskC=([5, 128, 512], BF16),
                OH=([3, 33, TVL], F32), SA=([128, 256], F32), SB=([128, 256], F32), ovx=([128, 4, 129], BF16), Sel=([12, 768], F32))
    cst = {k: E_("k_" + k, v[0], v[1]).ap() for k, v in cshp.items()}
    bshp = dict(tabs=[33, NKIND], sinks=[4], lamp=[1, 256], subln=[128, 1], w1=[2, 2048, 128], w2=[2, 128, 64], pos=[2, 128, 16])
    bi = {k: E_("b_" + k, v).ap() for k, v in bshp.items()}
    yT_out = nc.dram_tensor("yT_out", [16, 4, 256, TB], BF16, kind="ExternalOutput")
    sc = Scratch(nc, "sc_")
    import os as _os
    if _os.environ.get('SKIPB1') != '1':
      phase_B1(nc, "B1", sc, uT_in, a["win_fm"], a["win_tm"], a["wuq"], a["wuqp"], a["wukvk"], a["wukvv"], a["qnT"], a["kvn"], ropeC, ropeS)
    lam_init = 0.8 - 0.6 * math.exp(-0.3 * l)
    phase_B2(nc, "B2", sc, yT_out, cst, bi, lam_init)
    return nc


_CACHE = {}


def _get(kind, arg=None):
    key = (kind, arg)
    if key not in _CACHE:
        if kind == "A":
            _CACHE[key] = build_launch_A()
        elif kind == "B":
            _CACHE[key] = build_launch_B(arg)
        else:
            _CACHE[key] = build_launch_C(arg)
    return _CACHE[key]


def _bf(x):
    return np.ascontiguousarray(x)


def kernel_unfused(**inp):
    x = np.asarray(inp["x"], np.float32)
    cores = list(range(8))
    h = [np.ascontiguousarray(x[c // 2, (c % 2) * 4096:(c % 2 + 1) * 4096]) for c in cores]
    C_, S_g = rope_tables()
    consts = b2_consts()
    for l in range(2):
        maps = []
        for c in cores:
            maps.append(dict(h_in=h[c], gT0=gT_layout(inp["norm_g"][l, 0]), gT1=gT_layout(inp["norm_g"][l, 1]),
                             wg=np.asarray(inp["ffn_w_gate"][l, 0], np.float32), wu=np.asarray(inp["ffn_w_up"][l, 0], np.float32),
                             wd=np.asarray(inp["ffn_w_down"][l, 0], np.float32)))
        res = run_bass_kernel_spmd(_get("A"), maps, core_ids=cores).results
        h = [np.asarray(r["h_out"]) for r in res]
        uT = [np.asarray(r["uT_out"]) for r in res]
        maps = []
        for c in cores:
            s_, r_ = c // 2, c % 2
            m = dict(uT_in=np.ascontiguousarray(np.concatenate([uT[2 * s_], uT[2 * s_ + 1]], 0)), ropeC=C_, ropeS=S_g)
            m.update(prep_B_weights(inp, l, r_))
            m.update({'k_' + k: v for k, v in consts.items()})
            m.update({'b_' + k: v for k, v in prep_B2_inputs(inp, l, r_).items()})
            maps.append(m)
        res = run_bass_kernel_spmd(_get("B", l), maps, core_ids=cores).results
        yT = [np.asarray(r["yT_out"]) for r in res]
        maps = []
        for c in cores:
            s_, r_ = c // 2, c % 2
            y_all = np.concatenate([yT[2 * s_][8 * r_:8 * r_ + 8], yT[2 * s_ + 1][8 * r_:8 * r_ + 8]], 2)
            y_all = y_all.reshape(8, 4, 4, 128, TB).transpose(0, 3, 1, 2, 4).reshape(8, 128, 16, TB)
            m = dict(h_in=h[c], uT_in=uT[c], yT_in=np.ascontiguousarray(y_all), gT2=gT_layout(inp["norm_g"][l, 2]),
                     wgate=np.asarray(inp["w_gate"][l], np.float32), wbr=np.asarray(inp["w_branch"][l], np.float32),
                     wo=np.asarray(inp["w_o"][l], np.float32), wg=np.asarray(inp["ffn_w_gate"][l, 1], np.float32),
                     wu=np.asarray(inp["ffn_w_up"][l, 1], np.float32), wd=np.asarray(inp["ffn_w_down"][l, 1], np.float32))
            if l == 1:
                m["gfin"] = np.asarray(inp["final_g"], np.float32)
            maps.append(m)
        res = run_bass_kernel_spmd(_get("C", l == 1), maps, core_ids=cores).results
        h = [np.asarray(r["h_out"]) for r in res]
    out = np.zeros((NB, S, D), np.float32)
    for c in cores:
        out[c // 2, (c % 2) * 4096:(c % 2 + 1) * 4096] = h[c]
    return out


PAIRS = [[0, 1], [2, 3], [4, 5], [6, 7]]
A_SHP = dict(gT0=[128, 8], gT1=[128, 8], gT2=[128, 8], wg1=[D, DFF], wu1=[D, DFF], wd1=[DFF, D], wg2=[D, DFF], wu2=[D, DFF], wd2=[DFF, D],
             wgate=[4, D, D], wbr=[4, 512, D], wo=[D, D])
B_SHP = dict(win_fm=[D, FM_COLS], win_tm=[D, TM_COLS], wuq=[256, 384], wuqp=[256, 384], wukvk=[128, 256], wukvv=[128, 256],
             qnT=[128, 2], kvn=[128, 1])
B2_SHP = dict(tabs=[33, NKIND], sinks=[4], lamp=[1, 256], subln=[128, 1], w1=[2, 2048, 128], w2=[2, 128, 64], pos=[2, 128, 16])
C_SHP = dict(identb=([128, 128], BF16), identf=([128, 128], F32), Jb=([128, 128], BF16), E=([128, S], BF16), maskC=([5, 128, 512], BF16),
             OH=([3, 33, TVL], F32), SA=([128, 256], F32), SB=([128, 256], F32), ovx=([128, 4, 129], BF16), Sel=([12, 768], F32))


def build_fused():
    nc = bass.Bass("TRN2", target_bir_lowering=False)
    E_ = lambda n, s, dt=F32: nc.dram_tensor(n, list(s), dt, kind="ExternalInput")
    x_in = E_("x_in", [4096, D]).ap()
    par = E_("par", [128, TB]).ap()
    gfin = E_("gfin", [D]).ap()
    ropeC = E_("ropeC", [16, 96, TB]).ap()
    ropeS = E_("ropeS", [16, 96, TB]).ap()
    cst = {k: E_("k_" + k, v[0], v[1]).ap() for k, v in C_SHP.items()}
    out = nc.dram_tensor("out", [4096, D], F32, kind="ExternalOutput").ap()
    hA = dram(nc, "hA", [4096, D], F32).ap()
    hC = dram(nc, "hC", [4096, D], F32).ap()
    uT_mine = dram(nc, "uT_mine", [8, 128, 8, TB], BF16)
    uT_full = dram(nc, "uT_full", [4, 2, 2, 128, 8, TB], BF16)
    yT_mine = dram(nc, "yT_mine", [16, 4, 256, TB], BF16)
    yT_full = dram(nc, "yT_full", [8, 2, 2, 4, 256, TB], BF16)
    sc = Scratch(nc, "sc_")
    h_src = x_in
    import os as _os
    fcut = int(_os.environ.get("FCUT", "99"))
    nph = [0]

    def go():
        nph[0] += 1
        return nph[0] <= fcut
    for l in range(2):
        a = {k: E_("l%d_%s" % (l, k), v).ap() for k, v in A_SHP.items()}
        b = {k: E_("l%d_%s" % (l, k), v).ap() for k, v in B_SHP.items()}
        bi = {k: E_("l%d_b_%s" % (l, k), v).ap() for k, v in B2_SHP.items()}
        t = "L%d" % l
        if go():
            phase_A(nc, t + "A", h_src, hA, uT_mine, a["gT0"], a["gT1"], a["wg1"], a["wu1"], a["wd1"], 8)
        p = Prog(nc, t + "X1")
        if _os.environ.get("NOCC") != "1" and go():
          for j in range(4):
            p.collective("AllGather", PAIRS, uT_mine.ap()[2 * j:2 * j + 2].rearrange("b p c t -> (b p) (c t)"),
                         uT_full.ap()[j].rearrange("r b p c t -> (r b p) (c t)"))
        p.build()
        if go():
          phase_B1(nc, t + "B1", sc, (lambda bb: uT_full.ap()[(bb % 8) // 2, bb // 8, (bb % 8) % 2]), b["win_fm"], b["win_tm"], b["wuq"], b["wuqp"], b["wukvk"], b["wukvv"], b["qnT"], b["kvn"], ropeC, ropeS)
        if go():
            phase_B2(nc, t + "B2", sc, yT_mine, cst, bi, 0.8 - 0.6 * math.exp(-0.3 * l))
        p = Prog(nc, t + "X2")
        if _os.environ.get("NOCC") != "1" and go():
          for j in range(8):
            p.collective("AllGather", PAIRS, yT_mine.ap()[2 * j:2 * j + 2].rearrange("b i f t -> (b i f) t"),
                         yT_full.ap()[j].rearrange("r b i f t -> (r b i f) t"))
        p.build()
        if go():
          phase_C(nc, t + "C", hA, out if l == 1 else hC, uT_mine, None, a["gT2"], a["wgate"], a["wbr"], a["wo"], a["wg2"], a["wu2"], a["wd2"], 8,
                gfin_d=gfin if l == 1 else None, ygath=yT_full, par_d=par)
        h_src = hC
    return nc


def kernel(**inp):
    x = np.asarray(inp["x"], np.float32)
    cores = list(range(8))
    C_, S_g = rope_tables()
    consts = b2_consts()
    f32 = lambda a: np.ascontiguousarray(np.asarray(a, np.float32))
    shared = dict(gfin=f32(inp["final_g"]), ropeC=C_, ropeS=S_g)
    shared.update({"k_" + k: v for k, v in consts.items()})
    for l in range(2):
        shared.update({"l%d_gT%d" % (l, i): gT_layout(inp["norm_g"][l, i]) for i in range(3)})
        for j, nm in ((0, "1"), (1, "2")):
            shared["l%d_wg%s" % (l, nm)] = f32(inp["ffn_w_gate"][l, j])
            shared["l%d_wu%s" % (l, nm)] = f32(inp["ffn_w_up"][l, j])
            shared["l%d_wd%s" % (l, nm)] = f32(inp["ffn_w_down"][l, j])
        shared["l%d_wgate" % l] = f32(inp["w_gate"][l])
        shared["l%d_wbr" % l] = f32(inp["w_branch"][l])
        shared["l%d_wo" % l] = f32(inp["w_o"][l])
    perpar = []
    for r_ in range(2):
        d = {}
        for l in range(2):
            d.update({"l%d_%s" % (l, k): v for k, v in prep_B_weights(inp, l, r_).items()})
            d.update({"l%d_b_%s" % (l, k): v for k, v in prep_B2_inputs(inp, l, r_).items()})
        d["par"] = np.full((128, TB), float(r_), np.float32)
        perpar.append(d)
    maps = []
    for c in cores:
        m = dict(shared)
        m.update(perpar[c % 2])
        m["x_in"] = np.ascontiguousarray(x[c // 2, (c % 2) * 4096:(c % 2 + 1) * 4096])
        maps.append(m)
    if "F" not in _CACHE:
        _CACHE["F"] = build_fused()
    res = run_bass_kernel_spmd(_CACHE["F"], maps, core_ids=cores).results
    out = np.zeros((NB, S, D), np.float32)
    for c in cores:
        out[c // 2, (c % 2) * 4096:(c % 2 + 1) * 4096] = np.asarray(res[c]["out"])
    return out
```

```python
import contextlib
import math
import numpy as np
import ml_dtypes
import concourse.bass as bass
import concourse.mybir as mybir
from concourse.bass_utils import run_bass_kernel_spmd

F32 = mybir.dt.float32
BF16 = mybir.dt.bfloat16
AF = mybir.ActivationFunctionType
ALU = mybir.AluOpType

D = 1024
S = 8192
NB = 4
DFF = 2816
NF = 22
EPS = 1e-6
TB = 512
NEGM = -30000.0


class Buf:
    __slots__ = ("lw", "rd", "excl")

    def __init__(self, excl=False):
        self.lw = None
        self.rd = []
        self.excl = excl


class Chan:
    __slots__ = ("sem", "n")

    def __init__(self, sem):
        self.sem = sem
        self.n = 0


class Prog:
    ENGS = ("pe", "act", "dve", "pool", "sp")

    def __init__(self, nc, tag):
        self.nc = nc
        self.tag = tag
        self.stack = contextlib.ExitStack()
        self.lists = {e: [] for e in self.ENGS}
        self.nops = {e: 0 for e in self.ENGS}
        self.waited = {e: {} for e in self.ENGS}
        self.needed = {e: set() for e in self.ENGS}
        self.sems = []
        self.esem = {e: self._sem(tag + "s_" + e) for e in self.ENGS}
        self.nchan = 0
        self.out_toks = []
        self.all_chans = []
        self.nt = 0

    def _sem(self, name):
        h = self.nc.alloc_semaphore(name=name)
        self.sems.append(h)
        return h

    def chan(self):
        self.nchan += 1
        ch = Chan(self._sem("%sc%d" % (self.tag, self.nchan)))
        self.all_chans.append(ch)
        return ch

    def sbuf(self, shape, dt, name=None):
        self.nt += 1
        t = self.stack.enter_context(self.nc.sbuf_tensor("%s_%s%d" % (self.tag, name or "t", self.nt), list(shape), dt))
        return t

    def psum(self, shape, dt=F32, name=None):
        self.nt += 1
        return self.stack.enter_context(self.nc.psum_tensor("%s_%s%d" % (self.tag, name or "p", self.nt), list(shape), dt))

    def _wait(self, eng, tok, same_ok=False):
        if tok is None:
            return
        kind, key, val = tok
        if kind == "e" and key == eng and (same_ok or eng in ("pe", "sp")):
            return
        k = (kind, id(key) if kind == "d" else key)
        w = self.waited[eng]
        if w.get(k, -1) >= val:
            return
        w[k] = val
        if kind == "e":
            self.needed[key].add(val)
        self.lists[eng].append(("w", kind, key, val))

    def _deps(self, eng, reads, writes):
        for b in reads:
            self._wait(eng, b.lw)
        for b in writes:
            self._wait(eng, b.lw)
            for t in b.rd:
                self._wait(eng, t, same_ok=True)

    def _mark(self, tok, reads, writes):
        for b in reads:
            b.rd.append(tok)
            if len(b.rd) > 64:
                b.rd = b.rd[-48:]
        for b in writes:
            b.lw = tok
            b.rd = []

    def op(self, eng, fn, reads=(), writes=()):
        ex = [b for b in reads if b.excl]
        if ex:
            reads = [b for b in reads if not b.excl]
            writes = list(writes) + ex
        self._deps(eng, reads, writes)
        self.nops[eng] += 1
        o = self.nops[eng]
        self.lists[eng].append(("o", fn, o))
        tok = ("e", eng, o)
        self._mark(tok, reads, writes)
        return tok

    def dma(self, eng, out, in_, chan, reads=(), writes=(), final=False):
        self._deps(eng, reads, writes)
        if chan.n > 0:
            self._wait(eng, ("d", chan, 16 * chan.n))
        chan.n += 1
        tok = ("d", chan, 16 * chan.n)
        self.lists[eng].append(("d", out, in_, chan))
        self._mark(tok, reads, writes)
        if final:
            self.out_toks.append(tok)
        return tok

    def collective(self, kind, groups, in_ap, out_ap):
        ch = Chan(self._sem("%scc%d" % (self.tag, self.nchan)))
        self.nchan += 1
        self.lists["pool"].append(("c", lambda e: e.collective_compute(kind, ALU.bypass, replica_groups=groups, ins=[in_ap], outs=[out_ap]), ch))
        self.lists["pool"].append(("w", "d", ch, 1))

    def build(self):
        nc = self.nc
        for t in self.out_toks:
            self._wait("sp", t)
        for ch in self.all_chans:
            if ch.n > 0:
                self._wait("sp", ("d", ch, 16 * ch.n))
        last = {e: self.nops[e] for e in self.ENGS}
        for e in ("pe", "act", "dve", "pool"):
            for f in ("pe", "act", "dve", "pool"):
                if f != e and last[f] > 0:
                    self._wait(e, ("e", f, last[f]))
        rankd = {e: {o: i + 1 for i, o in enumerate(sorted(self.needed[e]))} for e in self.ENGS}

        def run(ename, e):
            need = self.needed[ename]
            for it in self.lists[ename]:
                if it[0] == "w":
                    _, kind, key, val = it
                    if kind == "e":
                        e.wait_ge(self.esem[key], rankd[key][val])
                    else:
                        e.wait_ge(key.sem, val)
                elif it[0] == "o":
                    ins = it[1](e)
                    if it[2] in need:
                        ins.then_inc(self.esem[ename], 1)
                elif it[0] == "c":
                    it[1](e).then_inc(it[2].sem, 1)
                else:
                    _, out, in_, chan = it
                    e.dma_start(out=out, in_=in_).then_inc(chan.sem, 16)

        with nc.Block() as block:
            block.tensor(lambda e: run("pe", e))
            block.scalar(lambda e: run("act", e))
            block.vector(lambda e: run("dve", e))
            block.gpsimd(lambda e: run("pool", e))
            block.sync(lambda e: run("sp", e))
        self.stack.close()
        nc.all_engine_barrier()
        nc.clear_and_free_semaphores(self.sems)
        nc.all_engine_barrier()


class Ring:
    def __init__(self, p, n, shape, dt, name, psum=False, chan=True):
        self.items = []
        for i in range(n):
            t = p.psum(shape, dt, name) if psum else p.sbuf(shape, dt, name)
            self.items.append((t, Buf(excl=psum), p.chan() if chan else None))
        self.i = 0

    def next(self):
        it = self.items[self.i % len(self.items)]
        self.i += 1
        return it


def dram(nc, name, shape, dt, kind="Internal"):
    return nc.dram_tensor(name, list(shape), dt, kind=kind)


def precast(p, src, K, N, dst, chans, nsplit=None):
    C = K // 128
    b = Buf()
    for c in range(C):
        ch = chans[c % len(chans)]
        p.dma("pool", dst.ap()[:, c, :], src[c * 128:(c + 1) * 128, :], ch, writes=[])
    return b


def precast_chunked(p, src, K, chunks, dst, chans):
    for i, (c0, w) in enumerate(chunks):
        ch = chans[i % len(chans)]
        p.dma("pool", dst.ap()[i, :, :, 0:w], src[:, c0:c0 + w].rearrange("(c p) j -> p c j", p=128), ch, writes=[])


class TokCtx:
    def __init__(self, p, nc):
        self.p = p
        self.nc = nc
        self.identb = p.sbuf([128, 128], BF16, "identb")
        self.b_id = Buf()
        p.op("pool", lambda e: e.memset(self.identb[:], 1.0), writes=[self.b_id])
        p.op("pool", lambda e: e.affine_select(out=self.identb[:], in_=self.identb[:], pattern=[[-1, 128]],
                                                compare_op=ALU.is_equal, fill=0.0, base=0, channel_multiplier=1),
             reads=[self.b_id], writes=[self.b_id])
        self.eps = p.sbuf([128, 1], F32, "eps")
        self.b_eps = Buf()
        p.op("pool", lambda e: e.memset(self.eps[:], EPS), writes=[self.b_eps])
        self.h = [(p.sbuf([128, D], F32, "h"), Buf(), p.chan(), p.chan()) for _ in range(4)]
        self.junk = p.sbuf([128, D], BF16, "junk")
        self.b_junk = Buf()
        self.ss = p.sbuf([128, 4], F32, "ss")
        self.b_ss = Buf()
        self.rs = p.sbuf([128, 4], F32, "rs")
        self.b_rs = Buf()
        self.xnb = Ring(p, 2, [128, D], BF16, "xnb", chan=False)
        self.pst = Ring(p, 2, [128, D], BF16, "pst", psum=True, chan=False)
        self.psf = Ring(p, 6, [128, 512], F32, "psf", psum=True, chan=False)
        self.wst = Ring(p, 8, [128, 8, 128], BF16, "wst")
        self.sg = Ring(p, 2, [128, 512], F32, "sg", chan=False)
        self.hidT = p.sbuf([128, NF, TB], BF16, "hidT")
        self.b_hid = Buf()
        self.wd = p.sbuf([128, NF, D], BF16, "wd")
        self.b_wd = Buf()
        self.c_wd = [p.chan() for _ in range(2)]

    def load_h(self, src, row0):
        p = self.p
        for t in range(4):
            ht, hb, hc, _ = self.h[t]
            p.dma("sp", ht[:], src[row0 + t * 128: row0 + (t + 1) * 128, :], hc, writes=[hb])

    def store_h(self, dst, row0, final=False):
        p = self.p
        for t in range(4):
            ht, hb, _, hc2 = self.h[t]
            p.dma("sp", dst[row0 + t * 128: row0 + (t + 1) * 128, :], ht[:], hc2, reads=[hb], final=final)

    def norm_T(self, gT, b_g, outT, b_out):
        p = self.p
        for t in range(4):
            ht, hb, _, _ = self.h[t]
            ss, rs = self.ss, self.rs
            p.op("act", lambda e, ht=ht, t=t: e.activation(out=self.junk[:], in_=ht[:], func=AF.Square,
                                                           accum_out=ss[:, t:t + 1]),
                 reads=[hb], writes=[self.b_junk, self.b_ss])
            p.op("act", lambda e, t=t: e.activation(out=rs[:, t:t + 1], in_=ss[:, t:t + 1], func=AF.Sqrt,
                                                    bias=self.eps[:], scale=1.0 / D),
                 reads=[self.b_ss, self.b_eps], writes=[self.b_rs])
            p.op("dve", lambda e, t=t: e.reciprocal(out=rs[:, t:t + 1], in_=rs[:, t:t + 1]),
                 reads=[self.b_rs], writes=[self.b_rs])
            xn, bxn, _ = self.xnb.next()
            p.op("dve", lambda e, xn=xn, ht=ht, t=t: e.tensor_scalar(out=xn[:], in0=ht[:], scalar1=rs[:, t:t + 1],
                                                                     scalar2=None, op0=ALU.mult),
                 reads=[hb, self.b_rs], writes=[bxn])
            ps, bps, _ = self.pst.next()
            for c in range(8):
                p.op("pe", lambda e, ps=ps, xn=xn, c=c: e.transpose(ps[:, c * 128:(c + 1) * 128],
                                                                    xn[:, c * 128:(c + 1) * 128], self.identb[:]),
                     reads=[bxn, self.b_id], writes=[bps])
            p.op("dve", lambda e, ps=ps, t=t: e.tensor_tensor(
                out=outT[:, :, t * 128:(t + 1) * 128], in0=ps[:].rearrange("p (c j) -> p c j", c=8),
                in1=gT[:].unsqueeze(2).to_broadcast([128, 8, 128]), op=ALU.mult),
                 reads=[bps, b_g], writes=[b_out])

    def load_wd(self, wd_c):
        p = self.p
        half = NF // 2
        p.dma("sp", self.wd[:, 0:half, :], wd_c.ap()[:, 0:half, :], self.c_wd[0], writes=[self.b_wd])
        p.dma("sp", self.wd[:, half:NF, :], wd_c.ap()[:, half:NF, :], self.c_wd[1], writes=[])
        self.b_wd.lw = None
        self.wd_toks = [("d", self.c_wd[0], 16 * self.c_wd[0].n), ("d", self.c_wd[1], 16 * self.c_wd[1].n)]

    def ffn(self, xnT, b_xn, wg_c, wu_c):
        p = self.p
        for f in range(NF):
            wg, bwg, cwg = self.wst.next()
            p.dma("sp", wg[:], wg_c.ap()[f], cwg, writes=[bwg])
            wu, bwu, cwu = self.wst.next()
            p.dma("sp", wu[:], wu_c.ap()[f], cwu, writes=[bwu])
            gps, bg, _ = self.psf.next()
            for c in range(8):
                p.op("pe", lambda e, gps=gps, wg=wg, c=c: e.matmul(gps[:], lhsT=wg[:, c, :], rhs=xnT[:, c, :],
                                                                   start=(c == 0), stop=(c == 7)),
                     reads=[bwg, b_xn], writes=[bg])
            ups, bu, _ = self.psf.next()
            for c in range(8):
                p.op("pe", lambda e, ups=ups, wu=wu, c=c: e.matmul(ups[:], lhsT=wu[:, c, :], rhs=xnT[:, c, :],
                                                                   start=(c == 0), stop=(c == 7)),
                     reads=[bwu, b_xn], writes=[bu])
            sg, bsg, _ = self.sg.next()
            p.op("act", lambda e, sg=sg, gps=gps: e.activation(out=sg[:], in_=gps[:], func=AF.Silu),
                 reads=[bg], writes=[bsg])
            p.op("dve", lambda e, sg=sg, ups=ups, f=f: e.tensor_tensor(out=self.hidT[:, f, :], in0=sg[:], in1=ups[:],
                                                                       op=ALU.mult),
                 reads=[bsg, bu], writes=[self.b_hid])
        for tk in self.wd_toks:
            p._wait("pe", tk)
        for t in range(4):
            ht, hb, _, _ = self.h[t]
            for hf in range(2):
                ops, bo, _ = self.psf.next()
                for f in range(NF):
                    p.op("pe", lambda e, ops=ops, t=t, hf=hf, f=f: e.matmul(
                        ops[:], lhsT=self.hidT[:, f, t * 128:(t + 1) * 128], rhs=self.wd[:, f, hf * 512:(hf + 1) * 512],
                        start=(f == 0), stop=(f == NF - 1)),
                         reads=[self.b_hid], writes=[bo])
                p.op("dve", lambda e, ops=ops, ht=ht, hf=hf: e.scalar_tensor_tensor(
                    out=ht[:, hf * 512:(hf + 1) * 512], in0=ops[:], scalar=0.5, in1=ht[:, hf * 512:(hf + 1) * 512],
                    op0=ALU.mult, op1=ALU.add),
                     reads=[bo, hb], writes=[hb])


def phase_A(nc, tag, h_in, h_out, uT_out, gT0_d, gT1_d, wg_d, wu_d, wd_d, nblk):
    wg_c = dram(nc, tag + "wg_c", [NF, 128, 8, 128], BF16)
    wu_c = dram(nc, tag + "wu_c", [NF, 128, 8, 128], BF16)
    wd_c = dram(nc, tag + "wd_c", [128, NF, D], BF16)
    p = Prog(nc, tag + "pc")
    chans = [p.chan() for _ in range(8)]
    precast_chunked(p, wg_d, D, [(f * 128, 128) for f in range(NF)], wg_c, chans)
    precast_chunked(p, wu_d, D, [(f * 128, 128) for f in range(NF)], wu_c, chans)
    precast(p, wd_d, DFF, D, wd_c, chans)
    for ch in chans:
        p._wait("pool", ("d", ch, 16 * ch.n))
    p.build()
    p = Prog(nc, tag)
    cx = TokCtx(p, nc)
    g0 = p.sbuf([128, 8], F32, "g0"); b_g0 = Buf(); c_g0 = p.chan()
    g1 = p.sbuf([128, 8], F32, "g1"); b_g1 = Buf(); c_g1 = p.chan()
    p.dma("sp", g0[:], gT0_d, c_g0, writes=[b_g0])
    p.dma("sp", g1[:], gT1_d, c_g1, writes=[b_g1])
    cx.load_wd(wd_c)
    xnT = p.sbuf([128, 8, TB], BF16, "xnT"); b_xn = Buf()
    uT = p.sbuf([128, 8, TB], BF16, "uT"); b_u = Buf(); c_u = p.chan()
    for lb in range(nblk):
        cx.load_h(h_in, lb * TB)
        cx.norm_T(g0, b_g0, xnT, b_xn)
        cx.ffn(xnT, b_xn, wg_c, wu_c)
        cx.store_h(h_out, lb * TB, final=True)
        cx.norm_T(g1, b_g1, uT, b_u)
        p.dma("sp", uT_out.ap()[lb], uT[:], c_u, reads=[b_u], final=True)
    p.build()


def build_launch_A():
    nc = bass.Bass("TRN2", target_bir_lowering=False)
    h_in = nc.dram_tensor("h_in", [4096, D], F32, kind="ExternalInput").ap()
    gT0 = nc.dram_tensor("gT0", [128, 8], F32, kind="ExternalInput").ap()
    gT1 = nc.dram_tensor("gT1", [128, 8], F32, kind="ExternalInput").ap()
    wg = nc.dram_tensor("wg", [D, DFF], F32, kind="ExternalInput").ap()
    wu = nc.dram_tensor("wu", [D, DFF], F32, kind="ExternalInput").ap()
    wd = nc.dram_tensor("wd", [DFF, D], F32, kind="ExternalInput").ap()
    h_out = nc.dram_tensor("h_out", [4096, D], F32, kind="ExternalOutput").ap()
    uT_out = nc.dram_tensor("uT_out", [8, 128, 8, TB], BF16, kind="ExternalOutput")
    phase_A(nc, "A", h_in, h_out, uT_out, gT0, gT1, wg, wu, wd, 8)
    return nc


def gT_layout(g):
    return np.ascontiguousarray(np.asarray(g, np.float32).reshape(8, 128).T)


def phase_C(nc, tag, h_in, h_out, uT_in, yT_in, gT2_d, wgate_d, wbr_d, wo_d, wg_d, wu_d, wd_d, nblk, gfin_d=None, ygath=None, par_d=None):
    wg_c = dram(nc, tag + "wg_c", [NF, 128, 8, 128], BF16)
    wu_c = dram(nc, tag + "wu_c", [NF, 128, 8, 128], BF16)
    wd_c = dram(nc, tag + "wd_c", [128, NF, D], BF16)
    wgate_c = [dram(nc, tag + "wgate_c%d" % i, [8, 128, 8, 128], BF16) for i in range(4)]
    wbr_c = [dram(nc, tag + "wbr_c%d" % i, [8, 128, 4, 128], BF16) for i in range(4)]
    wo_c = dram(nc, tag + "wo_c", [128, 8, D], BF16)
    p = Prog(nc, tag + "pc")
    chans = [p.chan() for _ in range(8)]
    precast_chunked(p, wg_d, D, [(f * 128, 128) for f in range(NF)], wg_c, chans)
    precast_chunked(p, wu_d, D, [(f * 128, 128) for f in range(NF)], wu_c, chans)
    precast(p, wd_d, DFF, D, wd_c, chans)
    for i in range(4):
        precast_chunked(p, wgate_d[i], D, [(j * 128, 128) for j in range(8)], wgate_c[i], chans)
        precast_chunked(p, wbr_d[i], 512, [(j * 128, 128) for j in range(8)], wbr_c[i], chans)
    precast(p, wo_d, D, D, wo_c, chans)
    for ch in chans:
        p._wait("pool", ("d", ch, 16 * ch.n))
    p.build()

    p = Prog(nc, tag)
    cx = TokCtx(p, nc)
    g2 = p.sbuf([128, 8], F32, "g2"); b_g2 = Buf(); c_g2 = p.chan()
    p.dma("sp", g2[:], gT2_d, c_g2, writes=[b_g2])
    cx.load_wd(wd_c)
    wo = p.sbuf([128, 8, D], BF16, "wo"); b_wo = Buf(); c_wo = p.chan()
    p.dma("sp", wo[:], wo_c.ap(), c_wo, writes=[b_wo])
    if gfin_d is not None:
        gfin = p.sbuf([128, D], F32, "gfin"); b_gf = Buf(); c_gf = p.chan()
        p.dma("sp", gfin[:], bass.AP(gfin_d.tensor, 0, [[0, 128], [1, D]]), c_gf, writes=[b_gf])
    xnT = p.sbuf([128, 8, TB], BF16, "xnT"); b_xn = Buf()
    uT = p.sbuf([128, 8, TB], BF16, "uT"); b_u = Buf(); c_u = p.chan()
    yT = p.sbuf([128, 16, TB], BF16, "yT"); b_y = Buf(); c_y = p.chan()
    mT = p.sbuf([128, 8, TB], BF16, "mT"); b_m = Buf()
    macc = Ring(p, 2, [128, TB], F32, "macc", chan=False)
    tmpr = Ring(p, 2, [128, TB], F32, "tmpr", chan=False)
    wbst = Ring(p, 6, [128, 4, 128], BF16, "wbst")
    if ygath is not None:
        yT2 = p.sbuf([128, 16, TB], BF16, "yT2"); b_y2 = Buf(); c_y2 = [p.chan() for _ in range(8)]
        c_y1 = [p.chan() for _ in range(8)]
        parf = p.sbuf([128, TB], F32, "parf"); b_pf = Buf(); c_pf = p.chan()
        parm = p.sbuf([128, TB], mybir.dt.uint32, "parm"); b_pm = Buf()
        p.dma("sp", parf[:], par_d, c_pf, writes=[b_pf])
        p.op("dve", lambda e: e.tensor_scalar(out=parm[:], in0=parf[:], scalar1=0.5, scalar2=None, op0=ALU.is_gt), reads=[b_pf], writes=[b_pm])
    for lb in range(nblk):
        cx.load_h(h_in, lb * TB)
        p.dma("sp", uT[:], uT_in.ap()[lb], c_u, writes=[b_u])
        if ygath is None:
            p.dma("sp", yT[:], yT_in.ap()[lb], c_y, writes=[b_y])
        else:
            n_ = 0
            for rr in range(2):
                for i in range(4):
                    p.dma("sp", yT[:, i * 4 + 2 * rr:i * 4 + 2 * rr + 2, :],
                          ygath.ap()[lb // 2, rr, lb % 2, i].rearrange("(k p) t -> p k t", p=128), c_y1[n_], writes=[b_y] if n_ == 0 else [])
                    p.dma("sp", yT2[:, i * 4 + 2 * rr:i * 4 + 2 * rr + 2, :],
                          ygath.ap()[4 + lb // 2, rr, lb % 2, i].rearrange("(k p) t -> p k t", p=128), c_y2[n_], writes=[b_y2] if n_ == 0 else [])
                    n_ += 1
            for ch in c_y1 + c_y2:
                p._wait("dve", ("d", ch, 16 * ch.n))
            for k in range(16):
                p.op("dve", lambda e, k=k: e.copy_predicated(yT[:, k, :], parm[:], yT2[:, k, :]), reads=[b_pm, b_y2], writes=[b_y])
        for j in range(8):
            ma, bma, _ = macc.next()
            for i in range(4):
                wgt, bwg, cwg = cx.wst.next()
                p.dma("sp", wgt[:], wgate_c[i].ap()[j], cwg, writes=[bwg])
                wb, bwb, cwb = wbst.next()
                p.dma("sp", wb[:], wbr_c[i].ap()[j], cwb, writes=[bwb])
                gps, bg, _ = cx.psf.next()
                for c in range(8):
                    p.op("pe", lambda e, gps=gps, wgt=wgt, c=c: e.matmul(gps[:], lhsT=wgt[:, c, :], rhs=uT[:, c, :],
                                                                         start=(c == 0), stop=(c == 7)),
                         reads=[bwg, b_u], writes=[bg])
                bps, bb, _ = cx.psf.next()
                for k in range(4):
                    p.op("pe", lambda e, bps=bps, wb=wb, k=k, i=i: e.matmul(bps[:], lhsT=wb[:, k, :], rhs=yT[:, i * 4 + k, :],
                                                                             start=(k == 0), stop=(k == 3)),
                         reads=[bwb, b_y], writes=[bb])
                sg, bsg, _ = cx.sg.next()
                p.op("act", lambda e, sg=sg, gps=gps: e.activation(out=sg[:], in_=gps[:], func=AF.Sigmoid),
                     reads=[bg], writes=[bsg])
                if i == 0:
                    p.op("dve", lambda e, ma=ma, sg=sg, bps=bps: e.tensor_tensor(out=ma[:], in0=sg[:], in1=bps[:], op=ALU.mult),
                         reads=[bsg, bb], writes=[bma])
                else:
                    tm, btm, _ = tmpr.next()
                    p.op("dve", lambda e, tm=tm, sg=sg, bps=bps: e.tensor_tensor(out=tm[:], in0=sg[:], in1=bps[:], op=ALU.mult),
                         reads=[bsg, bb], writes=[btm])
                    p.op("pool", lambda e, ma=ma, tm=tm: e.tensor_tensor(out=ma[:], in0=ma[:], in1=tm[:], op=ALU.add),
                         reads=[bma, btm], writes=[bma])
            p.op("act", lambda e, ma=ma, j=j: e.copy(out=mT[:, j, :], in_=ma[:]), reads=[bma], writes=[b_m])
        for t in range(4):
            ht, hb, _, _ = cx.h[t]
            for hf in range(2):
                ops, bo, _ = cx.psf.next()
                for j in range(8):
                    p.op("pe", lambda e, ops=ops, t=t, hf=hf, j=j: e.matmul(
                        ops[:], lhsT=mT[:, j, t * 128:(t + 1) * 128], rhs=wo[:, j, hf * 512:(hf + 1) * 512],
                        start=(j == 0), stop=(j == 7)), reads=[b_m, b_wo], writes=[bo])
                p.op("dve", lambda e, ops=ops, ht=ht, hf=hf: e.tensor_tensor(
                    out=ht[:, hf * 512:(hf + 1) * 512], in0=ht[:, hf * 512:(hf + 1) * 512], in1=ops[:], op=ALU.add),
                     reads=[bo, hb], writes=[hb])
        cx.norm_T(g2, b_g2, xnT, b_xn)
        cx.ffn(xnT, b_xn, wg_c, wu_c)
        if gfin_d is not None:
            for t in range(4):
                ht, hb, _, _ = cx.h[t]
                ss, rs = cx.ss, cx.rs
                p.op("act", lambda e, ht=ht, t=t: e.activation(out=cx.junk[:], in_=ht[:], func=AF.Square,
                                                               accum_out=ss[:, t:t + 1]),
                     reads=[hb], writes=[cx.b_junk, cx.b_ss])
                p.op("act", lambda e, t=t: e.activation(out=rs[:, t:t + 1], in_=ss[:, t:t + 1], func=AF.Sqrt,
                                                        bias=cx.eps[:], scale=1.0 / D),
                     reads=[cx.b_ss, cx.b_eps], writes=[cx.b_rs])
                p.op("dve", lambda e, t=t: e.reciprocal(out=rs[:, t:t + 1], in_=rs[:, t:t + 1]),
                     reads=[cx.b_rs], writes=[cx.b_rs])
                p.op("dve", lambda e, ht=ht, t=t: e.scalar_tensor_tensor(out=ht[:], in0=ht[:], scalar=rs[:, t:t + 1],
                                                                         in1=gfin[:], op0=ALU.mult, op1=ALU.mult),
                     reads=[hb, cx.b_rs, b_gf], writes=[hb])
        cx.store_h(h_out, lb * TB, final=True)
    p.build()


def build_launch_C(final):
    nc = bass.Bass("TRN2", target_bir_lowering=False)
    h_in = nc.dram_tensor("h_in", [4096, D], F32, kind="ExternalInput").ap()
    uT_in = nc.dram_tensor("uT_in", [8, 128, 8, TB], BF16, kind="ExternalInput")
    yT_in = nc.dram_tensor("yT_in", [8, 128, 16, TB], BF16, kind="ExternalInput")
    gT2 = nc.dram_tensor("gT2", [128, 8], F32, kind="ExternalInput").ap()
    wgate = nc.dram_tensor("wgate", [4, D, D], F32, kind="ExternalInput").ap()
    wbr = nc.dram_tensor("wbr", [4, 512, D], F32, kind="ExternalInput").ap()
    wo = nc.dram_tensor("wo", [D, D], F32, kind="ExternalInput").ap()
    wg = nc.dram_tensor("wg", [D, DFF], F32, kind="ExternalInput").ap()
    wu = nc.dram_tensor("wu", [D, DFF], F32, kind="ExternalInput").ap()
    wd = nc.dram_tensor("wd", [DFF, D], F32, kind="ExternalInput").ap()
    gfin = nc.dram_tensor("gfin", [D], F32, kind="ExternalInput").ap() if final else None
    h_out = nc.dram_tensor("h_out", [4096, D], F32, kind="ExternalOutput").ap()
    phase_C(nc, "C", h_in, h_out, uT_in, yT_in, gT2, wgate, wbr, wo, wg, wu, wd, 8, gfin)
    return nc


FM_COLS = 1804
TM_COLS = 448
OFF = dict(cq=0, ckv=256, kr=384, krp=416, sq=448, sk=704, nq=768, kcvc=1024, kskw=1152, ng=1280, dq=1292, dk=1548)
NKIND = 15
FM_CHUNKS = [(0, 128), (128, 128), (256, 128), (384, 32), (416, 32), (448, 128), (576, 128), (704, 64), (768, 128), (896, 128),
             (1024, 128), (1152, 128), (1280, 12), (1292, 128), (1420, 128), (1548, 128), (1676, 128)]
SW = 1024
TVL = 1152


class Scratch:
    def __init__(self, nc, tag):
        d = lambda n, s, dt=BF16: dram(nc, tag + n, s, dt)
        self.mla_qT = d("mla_qT", [16, 4, 96, TB])
        self.mla_kT = d("mla_kT", [16, 4, 96, TB])
        self.mla_v = d("mla_v", [16, 128, 4, 256])
        self.swa_qT = d("swa_qT", [16, 256, TB])
        self.swa_kT = d("swa_kT", [16, 64, TB])
        self.swa_v = d("swa_v", [16, 128, 4, 64])
        self.nsa_qT = d("nsa_qT", [16, 256, TB])
        self.nsa_kcT = d("nsa_kcT", [64, S + 64])
        self.nsa_vcT = d("nsa_vcT", [64, S + 64])
        self.nsa_ksT = d("nsa_ksT", [16, 64, TB])
        self.nsa_kwT = d("nsa_kwT", [16, 64, TB])
        self.nsa_vs = d("nsa_vs", [16, 128, 4, 64])
        self.nsa_vw = d("nsa_vw", [16, 128, 4, 64])
        self.nsa_gsig = d("nsa_gsig", [16, 12, TB], F32)
        self.diff_qT = d("diff_qT", [16, 256, TB])
        self.diff_kT = d("diff_kT", [16, 256, TB])
        self.diff_v = d("diff_v", [16, 128, 4, 256])
        self.tvec = d("tvec", [NKIND, TVL])


def phase_B1(nc, tag, sc, uT_in, win_fm_d, win_tm_d, wuq_d, wuqp_d, wukvk_d, wukvv_d, qnT_d, kvn_d, ropeC_d, ropeS_d, nblk=16, stage=99):
    fm_c = dram(nc, tag + "fm_c", [len(FM_CHUNKS), 128, 8, 128], BF16)
    tm_c = dram(nc, tag + "tm_c", [128, 8, TM_COLS], BF16)
    p = Prog(nc, tag + "pc")
    chans = [p.chan() for _ in range(8)]
    precast_chunked(p, win_fm_d, D, FM_CHUNKS, fm_c, chans)
    precast(p, win_tm_d, D, TM_COLS, tm_c, chans)
    for ch in chans:
        p._wait("pool", ("d", ch, 16 * ch.n))
    p.build()

    p = Prog(nc, tag)
    psf = Ring(p, 7, [128, 512], F32, "psf", psum=True, chan=False)
    wst = Ring(p, 8, [128, 8, 128], BF16, "wst")
    uTr = Ring(p, 2, [128, 8, TB], BF16, "uT")
    ev = Ring(p, 6, [128, TB], BF16, "ev")
    eps = p.sbuf([128, 1], F32, "eps"); b_eps = Buf()
    p.op("pool", lambda e: e.memset(eps[:], EPS), writes=[b_eps])
    ones_f = p.sbuf([128, 128], F32, "ones"); b_ones = Buf()
    p.op("pool", lambda e: e.memset(ones_f[:], 1.0), writes=[b_ones])
    def res(shape, src):
        t = p.sbuf(shape, BF16, "res"); b = Buf(); c = p.chan()
        p.dma("pool", t[:], src, c, writes=[b])
        return t, b
    wuq, b_wuq = res([128, 2, 384], wuq_d.rearrange("(c p) n -> p c n", p=128))
    wuqp, b_wuqp = res([128, 2, 384], wuqp_d.rearrange("(c p) n -> p c n", p=128))
    wkk, b_wkk = res([128, 256], wukvk_d)
    wkv, b_wkv = res([128, 256], wukvv_d)
    wtm = p.sbuf([128, 8, TM_COLS], BF16, "wtm"); b_wtm = Buf(); c_wtm = p.chan()
    p.dma("sp", wtm[:], tm_c.ap(), c_wtm, writes=[b_wtm])
    qn = p.sbuf([128, 2], F32, "qn"); b_qn = Buf(); c_qn = p.chan()
    p.dma("sp", qn[:], qnT_d, c_qn, writes=[b_qn])
    kvn = p.sbuf([128, 1], F32, "kvn"); b_kvn = Buf(); c_kvn = p.chan()
    p.dma("sp", kvn[:], kvn_d, c_kvn, writes=[b_kvn])
    rC = p.sbuf([96, TB], F32, "rC"); b_rC = Buf(); c_rC = p.chan()
    rS = p.sbuf([96, TB], F32, "rS"); b_rS = Buf(); c_rS = p.chan()
    rC32 = p.sbuf([32, TB], F32, "rC32"); b_rC32 = Buf(); c_rC32 = p.chan()
    rS32 = p.sbuf([32, TB], F32, "rS32"); b_rS32 = Buf(); c_rS32 = p.chan()
    sq = [(p.sbuf([128, TB], F32, "sq"), Buf()) for _ in range(2)]
    cqf = [(p.sbuf([128, TB], F32, "cqf"), Buf()) for _ in range(2)]
    rs = p.sbuf([128, TB], F32, "rs"); b_rs = Buf()
    cqn = p.sbuf([128, 2, TB], BF16, "cqn"); b_cqn = Buf()
    ckvn = p.sbuf([128, TB], BF16, "ckvn"); b_ckvn = Buf()
    ta = Ring(p, 2, [96, TB], F32, "ta", chan=False)
    tb = Ring(p, 2, [96, TB], F32, "tb", chan=False)
    vt = p.sbuf([128, 4, 256], BF16, "vt"); b_vt = Buf(); c_vt = p.chan()
    vtm = p.sbuf([128, 4, TM_COLS], BF16, "vtm"); b_vtm = Buf(); c_vtm = [p.chan() for _ in range(4)]
    gs = p.sbuf([12, TB], F32, "gs"); b_gs = Buf(); c_gs = p.chan()
    flip = [0]

    def fm(uT, b_u, col0, w):
        wt, bw, cw = wst.next()
        p.dma("sp", wt[:], fm_c.ap()[FM_CHUNKS.index((col0, w))], cw, writes=[bw])
        ps, bp, _ = psf.next()
        for c in range(8):
            p.op("pe", lambda e, ps=ps, wt=wt, c=c, w=w: e.matmul(ps[0:w, :], lhsT=wt[:, c, 0:w], rhs=uT[:, c, :],
                                                                   start=(c == 0), stop=(c == 7)),
                 reads=[bw, b_u], writes=[bp])
        return ps, bp

    def evac(ps, bp, w, scale, dsts):
        t, bt, ct = ev.next()
        flip[0] ^= 1
        if flip[0]:
            p.op("act", lambda e: e.activation(out=t[0:w, :], in_=ps[0:w, :], func=AF.Copy, scale=scale),
                 reads=[bp], writes=[bt])
        else:
            p.op("dve", lambda e: e.tensor_scalar(out=t[0:w, :], in0=ps[0:w, :], scalar1=scale, scalar2=None, op0=ALU.mult),
                 reads=[bp], writes=[bt])
        for i, (dst, r0, nr) in enumerate(dsts):
            ch = ct if i == 0 else extra_ch[i - 1]
            p.dma("sp", dst, t[r0:r0 + nr, :], ch, reads=[bt])

    extra_ch = [p.chan() for _ in range(3)]

    def rms_fm(chunks, nfeat, gn, b_gn, outs, b_out):
        for i, (ps, bp) in enumerate(chunks):
            p.op("act", lambda e, i=i, ps=ps: e.activation(out=sq[i][0][:], in_=ps[:], func=AF.Square),
                 reads=[bp], writes=[sq[i][1]])
            p.op("dve", lambda e, i=i, ps=ps: e.tensor_copy(out=cqf[i][0][:], in_=ps[:]), reads=[bp], writes=[cqf[i][1]])
        ss, bss, _ = psf.next()
        n = len(chunks)
        for i in range(n):
            p.op("pe", lambda e, i=i, ss=ss: e.matmul(ss[:], lhsT=ones_f[:], rhs=sq[i][0][:], start=(i == 0), stop=(i == n - 1)),
                 reads=[b_ones, sq[i][1]], writes=[bss])
        p.op("act", lambda e, ss=ss: e.activation(out=rs[:], in_=ss[:], func=AF.Sqrt, bias=eps[:], scale=1.0 / nfeat),
             reads=[bss, b_eps], writes=[b_rs])
        p.op("dve", lambda e: e.reciprocal(out=rs[:], in_=rs[:]), reads=[b_rs], writes=[b_rs])
        for i in range(n):
            p.op("dve", lambda e, i=i: e.scalar_tensor_tensor(out=outs[i], in0=cqf[i][0][:], scalar=gn[:, i:i + 1], in1=rs[:],
                                                              op0=ALU.mult, op1=ALU.mult),
                 reads=[cqf[i][1], b_gn, b_rs], writes=[b_out])

    for blk in range(nblk):
        uT, b_u, c_u = uTr.next()
        p.dma("sp", uT[:], (uT_in(blk) if callable(uT_in) else uT_in.ap()[blk]), c_u, writes=[b_u])
        p.dma("sp", rC[:], ropeC_d[blk], c_rC, writes=[b_rC])
        p.dma("sp", rS[:], ropeS_d[blk], c_rS, writes=[b_rS])
        p.dma("sp", rC32[:], ropeC_d[blk, 64:96, :], c_rC32, writes=[b_rC32])
        p.dma("sp", rS32[:], ropeS_d[blk, 64:96, :], c_rS32, writes=[b_rS32])
        ch = [fm(uT, b_u, OFF["cq"], 128), fm(uT, b_u, OFF["cq"] + 128, 128)]
        rms_fm(ch, 256, qn, b_qn, [cqn[:, 0, :], cqn[:, 1, :]], b_cqn)
        ch = [fm(uT, b_u, OFF["ckv"], 128)]
        rms_fm(ch, 128, kvn, b_kvn, [ckvn[:]], b_ckvn)
        if stage < 1:
            continue
        pa, bpa = fm(uT, b_u, OFF["kr"], 32)
        pb, bpb = fm(uT, b_u, OFF["krp"], 32)
        t1, bt1, _ = ta.next()
        t2, bt2, _ = tb.next()
        p.op("dve", lambda e, t1=t1, pa=pa: e.tensor_tensor(out=t1[0:32, :], in0=pa[0:32, :], in1=rC32[:], op=ALU.mult),
             reads=[bpa, b_rC32], writes=[bt1])
        p.op("dve", lambda e, t2=t2, pb=pb: e.tensor_tensor(out=t2[0:32, :], in0=pb[0:32, :], in1=rS32[:], op=ALU.mult),
             reads=[bpb, b_rS32], writes=[bt2])
        kr, bkr, ckr = ev.next()
        p.op("pool", lambda e, kr=kr, t1=t1, t2=t2: e.tensor_tensor(out=kr[0:32, :], in0=t1[0:32, :], in1=t2[0:32, :], op=ALU.add),
             reads=[bt1, bt2], writes=[bkr])
        for h in range(4):
            p.dma("sp", sc.mla_kT.ap()[blk, h, 64:96, :], kr[0:32, :], ckr if h == 0 else extra_ch[h - 1], reads=[bkr])
        if stage < 2:
            continue
        for h in range(4):
            pq, bpq, _ = psf.next()
            pp, bpp, _ = psf.next()
            for c in range(2):
                p.op("pe", lambda e, pq=pq, c=c, h=h: e.matmul(pq[0:96, :], lhsT=wuq[:, c, h * 96:(h + 1) * 96], rhs=cqn[:, c, :],
                                                               start=(c == 0), stop=(c == 1)), reads=[b_wuq, b_cqn], writes=[bpq])
            for c in range(2):
                p.op("pe", lambda e, pp=pp, c=c, h=h: e.matmul(pp[0:96, :], lhsT=wuqp[:, c, h * 96:(h + 1) * 96], rhs=cqn[:, c, :],
                                                               start=(c == 0), stop=(c == 1)), reads=[b_wuqp, b_cqn], writes=[bpp])
            t1, bt1, _ = ta.next()
            t2, bt2, _ = tb.next()
            p.op("dve", lambda e, t1=t1, pq=pq: e.tensor_tensor(out=t1[:], in0=pq[0:96, :], in1=rC[:], op=ALU.mult),
                 reads=[bpq, b_rC], writes=[bt1])
            p.op("dve", lambda e, t2=t2, pp=pp: e.tensor_tensor(out=t2[:], in0=pp[0:96, :], in1=rS[:], op=ALU.mult),
                 reads=[bpp, b_rS], writes=[bt2])
            qq, bqq, cqq = ev.next()
            p.op("pool", lambda e, qq=qq, t1=t1, t2=t2: e.tensor_tensor(out=qq[0:96, :], in0=t1[:], in1=t2[:], op=ALU.add),
                 reads=[bt1, bt2], writes=[bqq])
            p.dma("sp", sc.mla_qT.ap()[blk, h], qq[0:96, :], cqq, reads=[bqq])
        if stage < 3:
            continue
        for hp in range(2):
            pk, bpk, _ = psf.next()
            p.op("pe", lambda e, pk=pk, hp=hp: e.matmul(pk[:], lhsT=wkk[:, hp * 128:(hp + 1) * 128], rhs=ckvn[:], start=True, stop=True),
                 reads=[b_wkk, b_ckvn], writes=[bpk])
            evac(pk, bpk, 128, 1.0, [(sc.mla_kT.ap()[blk, 2 * hp, 0:64, :], 0, 64), (sc.mla_kT.ap()[blk, 2 * hp + 1, 0:64, :], 64, 64)])
        for t in range(4):
            pv, bpv, _ = psf.next()
            p.op("pe", lambda e, pv=pv, t=t: e.matmul(pv[:, 0:256], lhsT=ckvn[:, t * 128:(t + 1) * 128], rhs=wkv[:], start=True, stop=True),
                 reads=[b_ckvn, b_wkv], writes=[bpv])
            p.op("act", lambda e, pv=pv, t=t: e.copy(out=vt[:, t, :], in_=pv[:, 0:256]), reads=[bpv], writes=[b_vt])
        p.dma("sp", sc.mla_v.ap()[blk], vt[:], c_vt, reads=[b_vt])
        if stage < 4:
            continue
        for i in range(2):
            ps, bp = fm(uT, b_u, OFF["sq"] + i * 128, 128)
            evac(ps, bp, 128, 0.125, [(sc.swa_qT.ap()[blk, i * 128:(i + 1) * 128, :], 0, 128)])
        ps, bp = fm(uT, b_u, OFF["sk"], 64)
        evac(ps, bp, 64, 1.0, [(sc.swa_kT.ap()[blk], 0, 64)])
        for i in range(2):
            ps, bp = fm(uT, b_u, OFF["nq"] + i * 128, 128)
            evac(ps, bp, 128, 0.125, [(sc.nsa_qT.ap()[blk, i * 128:(i + 1) * 128, :], 0, 128)])
        ps, bp = fm(uT, b_u, OFF["kcvc"], 128)
        evac(ps, bp, 128, 1.0, [(sc.nsa_kcT.ap()[:, blk * TB:(blk + 1) * TB], 0, 64), (sc.nsa_vcT.ap()[:, blk * TB:(blk + 1) * TB], 64, 64)])
        ps, bp = fm(uT, b_u, OFF["kskw"], 128)
        evac(ps, bp, 128, 1.0, [(sc.nsa_ksT.ap()[blk], 0, 64), (sc.nsa_kwT.ap()[blk], 64, 64)])
        ps, bp = fm(uT, b_u, OFF["ng"], 12)
        p.op("act", lambda e, ps=ps: e.activation(out=gs[:], in_=ps[0:12, :], func=AF.Sigmoid), reads=[bp], writes=[b_gs])
        p.dma("sp", sc.nsa_gsig.ap()[blk], gs[:], c_gs, reads=[b_gs])
        for i in range(2):
            ps, bp = fm(uT, b_u, OFF["dq"] + i * 128, 128)
            evac(ps, bp, 128, 0.125, [(sc.diff_qT.ap()[blk, i * 128:(i + 1) * 128, :], 0, 128)])
        for i in range(2):
            ps, bp = fm(uT, b_u, OFF["dk"] + i * 128, 128)
            evac(ps, bp, 128, 1.0, [(sc.diff_kT.ap()[blk, i * 128:(i + 1) * 128, :], 0, 128)])
        if stage < 5:
            continue
        for t in range(4):
            pv, bpv, _ = psf.next()
            for c in range(8):
                p.op("pe", lambda e, pv=pv, t=t, c=c, uT=uT: e.matmul(pv[:, 0:TM_COLS], lhsT=uT[:, c, t * 128:(t + 1) * 128], rhs=wtm[:, c, :],
                                                               start=(c == 0), stop=(c == 7)), reads=[b_u, b_wtm], writes=[bpv])
            p.op("act", lambda e, pv=pv, t=t: e.copy(out=vtm[:, t, :], in_=pv[:, 0:TM_COLS]), reads=[bpv], writes=[b_vtm])
        p.dma("sp", sc.swa_v.ap()[blk], vtm[:, :, 0:64], c_vtm[0], reads=[b_vtm])
        p.dma("sp", sc.nsa_vs.ap()[blk], vtm[:, :, 64:128], c_vtm[1], reads=[b_vtm])
        p.dma("sp", sc.nsa_vw.ap()[blk], vtm[:, :, 128:192], c_vtm[2], reads=[b_vtm])
        p.dma("sp", sc.diff_v.ap()[blk], vtm[:, :, 192:448], c_vtm[3], reads=[b_vtm])
    for it in p.lists["sp"]:
        pass
    p.final_all_dma = True
    p.build()


CO = np.cumsum([0, 256, 128, 32, 512, 128, 128, 512, 128, 128, 128, 128, 128, 128, 24, 512, 512, 512])
(O_CQ, O_CKV, O_KR, O_SQ, O_SK, O_SV, O_NQ, O_NKC, O_NVC, O_NKS, O_NVS, O_NKW, O_NVW, O_NG, O_DQ, O_DK, O_DV) = [int(v) for v in CO[:-1]]
PERM32 = np.concatenate([np.arange(16, 32), np.arange(0, 16)])


def rope_tables():
    half = 16
    freqs = (10000.0 ** (-np.arange(half, dtype=np.float32) / half)).astype(np.float32)
    pos = np.arange(S, dtype=np.float32)
    ang = pos[:, None] * freqs[None, :]
    cos = np.cos(ang).astype(np.float32).T
    sin = np.sin(ang).astype(np.float32).T
    C = np.ones((96, S), np.float32)
    Sg = np.zeros((96, S), np.float32)
    C[64:80] = cos
    C[80:96] = cos
    Sg[64:80] = -sin
    Sg[80:96] = sin
    C = np.ascontiguousarray(C.reshape(96, 16, TB).transpose(1, 0, 2))
    Sg = np.ascontiguousarray(Sg.reshape(96, 16, TB).transpose(1, 0, 2))
    return C, Sg


def prep_B_weights(inp, l, r):
    w = np.asarray(inp["w_in"][l], np.float32)
    sl = lambda o, n: w[:, o:o + n]
    kr = sl(O_KR, 32)
    fmc = [sl(O_CQ, 256), sl(O_CKV, 128), kr, kr[:, PERM32],
           sl(O_SQ + 256 * r, 256), sl(O_SK + 64 * r, 64),
           sl(O_NQ + 256 * r, 256), sl(O_NKC + 64 * r, 64), sl(O_NVC + 64 * r, 64), sl(O_NKS + 64 * r, 64), sl(O_NKW + 64 * r, 64),
           sl(O_NG + 12 * r, 12), sl(O_DQ + 256 * r, 256), sl(O_DK + 256 * r, 256)]
    win_fm = np.ascontiguousarray(np.concatenate(fmc, 1))
    assert win_fm.shape[1] == FM_COLS
    win_tm = np.ascontiguousarray(np.concatenate([sl(O_SV + 64 * r, 64), sl(O_NVS + 64 * r, 64), sl(O_NVW + 64 * r, 64), sl(O_DV + 256 * r, 256)], 1))
    wuq = np.asarray(inp["mla_w_uq"][l], np.float32).reshape(256, 8, 96)[:, 4 * r:4 * r + 4]
    wuqp = wuq.copy()
    wuqp[:, :, 64:96] = wuq[:, :, 64:96][:, :, PERM32]
    wukv = np.asarray(inp["mla_w_ukv"][l], np.float32).reshape(128, 8, 128)[:, 4 * r:4 * r + 4]
    return dict(win_fm=win_fm, win_tm=win_tm, wuq=np.ascontiguousarray(wuq.reshape(256, 384)),
                wuqp=np.ascontiguousarray(wuqp.reshape(256, 384)),
                wukvk=np.ascontiguousarray(wukv[:, :, 0:64].reshape(128, 256)),
                wukvv=np.ascontiguousarray(wukv[:, :, 64:128].reshape(128, 256)),
                qnT=np.ascontiguousarray(np.asarray(inp["mla_q_norm"][l], np.float32).reshape(2, 128).T),
                kvn=np.ascontiguousarray(np.asarray(inp["mla_kv_norm"][l], np.float32).reshape(128, 1)))


def t5_bucket_np(n):
    n = np.maximum(n, 0)
    nf = np.maximum(n, 1).astype(np.float32)
    large = 16 + (np.log(nf / 16) / math.log(128 / 16) * 16).astype(np.int32)
    large = np.minimum(large, 31)
    return np.where(n < 16, n, large)


def b2_consts():
    c = {}
    c["identb"] = np.eye(128, dtype=np.float32).astype(ml_dtypes.bfloat16)
    c["identf"] = np.eye(128, dtype=np.float32)
    c["Jb"] = np.eye(128, dtype=np.float32)[::-1].copy().astype(ml_dtypes.bfloat16)
    k = np.arange(S)
    c["E"] = (k[None, :] // 64 == np.arange(128)[:, None]).astype(np.float32).astype(ml_dtypes.bfloat16)
    cl = np.arange(128)[:, None]
    ql = np.arange(512)[None, :]
    c["maskC"] = np.stack([np.where(16 * cl + 31 - ql <= 512 * dl, 0.0, NEGM) for dl in range(5)]).astype(np.float32).astype(ml_dtypes.bfloat16)
    d = np.arange(TVL) - 127
    bk = t5_bucket_np(d)
    oh = np.zeros((3, 33, TVL), np.float32)
    for v, hi in enumerate((10 ** 9, 128, 512)):
        ok = (d >= 0) & (d < hi)
        for b in range(32):
            oh[v, b] = ((bk == b) & ok)
        oh[v, 32] = ~ok
    c["OH"] = oh
    qq = np.arange(128)[:, None]
    m = np.arange(256)[None, :] - 126
    cc = (qq >= 64).astype(np.int32)
    c["SA"] = (m < cc - 1).astype(np.float32)
    c["SB"] = np.where((m == cc) | (m == cc - 1), 1e4, np.where(m > cc, -1.0, 0.0)).astype(np.float32)
    cs = np.arange(512) * 16
    ss = np.arange(128) * 64
    ov = ((cs[:, None] < ss[None, :] + 64) & (ss[None, :] < cs[:, None] + 32)).astype(np.float32)
    ov[511] = 0
    ovx = np.concatenate([ov, np.ones((512, 1), np.float32)], 1).reshape(4, 128, 129).transpose(1, 0, 2)
    c["ovx"] = np.ascontiguousarray(ovx).astype(ml_dtypes.bfloat16)
    sel = np.zeros((12, 12 * 64), np.float32)
    for r_ in range(12):
        sel[r_, r_ * 64:(r_ + 1) * 64] = 1
    c["Sel"] = sel
    return c


def prep_B2_inputs(inp, l, r):
    tab = np.asarray(inp["rel_bias_table"], np.float32)
    tabs = np.zeros((33, NKIND), np.float32)
    tabs[:32, 0:4] = tab[:, 8 + 4 * r:8 + 4 * r + 4]
    tabs[:32, 4:6] = tab[:, 16 + 2 * r:16 + 2 * r + 2]
    tabs[:32, 7:11] = tab[:, 4 * r:4 * r + 4]
    tabs[:32, 11:15] = tab[:, 8 + 4 * r:8 + 4 * r + 4]
    tabs[32, :] = NEGM
    d = dict(tabs=tabs,
             sinks=np.ascontiguousarray(np.asarray(inp["swa_sinks"][l], np.float32)[4 * r:4 * r + 4]),
             lamp=np.ascontiguousarray(np.asarray(inp["diff_lambda"][l], np.float32).reshape(1, 256)),
             subln=np.ascontiguousarray(np.asarray(inp["diff_subln"][l], np.float32).reshape(128, 1)),
             w1=np.ascontiguousarray(np.asarray(inp["nsa_cmp_w1"][l], np.float32)),
             w2=np.ascontiguousarray(np.asarray(inp["nsa_cmp_w2"][l], np.float32)),
             pos=np.ascontiguousarray(np.asarray(inp["nsa_cmp_pos"][l], np.float32).reshape(2, 16, 2, 64).transpose(0, 2, 3, 1).reshape(2, 128, 16)))
    return d


def phase_B2(nc, tag, sc, yT_out, cst, bi, lam_init):
    p = Prog(nc, tag)
    S_ = Ring(p, 3, [128, 512], F32, "S", psum=True, chan=False)
    O_ = Ring(p, 2, [128, 512], F32, "O", psum=True, chan=False)
    X_ = Ring(p, 3, [128, 512], F32, "X", psum=True, chan=False)
    P_ = Ring(p, 4, [128, 512], BF16, "P", chan=False)
    KT_ = Ring(p, 4, [96, TB], BF16, "KT")
    V65 = Ring(p, 4, [128, 4, 65], BF16, "V65")
    V128 = Ring(p, 4, [128, 4, 128], BF16, "V128")

    def const(shape, dt, src, eng="sp"):
        t = p.sbuf(shape, dt, "k"); b = Buf(); c = p.chan()
        p.dma(eng, t[:], src, c, writes=[b])
        return t, b
    identb, b_idb = const([128, 128], BF16, cst["identb"])
    identf, b_idf = const([128, 128], F32, cst["identf"])
    Jb, b_J = const([128, 128], BF16, cst["Jb"])
    E, b_E = const([128, S], BF16, cst["E"])
    maskC, b_mC = const([128, 5, 512], BF16, cst["maskC"].rearrange("v p q -> p v q"))
    SA, b_SA = const([128, 256], F32, cst["SA"])
    SB, b_SB = const([128, 256], F32, cst["SB"])
    ovx, b_ov = const([128, 4, 129], BF16, cst["ovx"])
    Sel, b_Sel = const([12, 768], F32, cst["Sel"])
    tabs, b_tabs = const([33, NKIND], F32, bi["tabs"])
    cb, b_cb = const([128, NKIND], F32, bass.AP(bi["tabs"].tensor, 31 * NKIND, [[0, 128], [1, NKIND]]))
    sinke, b_sk = const([128, 4], F32, bass.AP(bi["sinks"].tensor, 0, [[0, 128], [1, 4]]))
    p.op("act", lambda e: e.activation(out=sinke[:], in_=sinke[:], func=AF.Exp), reads=[b_sk], writes=[b_sk])
    subln, b_sub = const([128, 1], F32, bi["subln"])
    p.op("dve", lambda e: e.tensor_scalar(out=subln[:], in0=subln[:], scalar1=1.0 - lam_init, scalar2=None, op0=ALU.mult),
         reads=[b_sub], writes=[b_sub])
    ones_f = p.sbuf([128, 128], F32, "ones"); b_ones = Buf()
    p.op("pool", lambda e: e.memset(ones_f[:], 1.0), writes=[b_ones])
    onesb = p.sbuf([128, 1], BF16, "onesb"); b_onesb = Buf()
    p.op("pool", lambda e: e.memset(onesb[:], 1.0), writes=[b_onesb])
    zb = p.sbuf([128, 1], F32, "zb"); b_zb = Buf()
    p.op("pool", lambda e: e.memset(zb[:], 0.0), writes=[b_zb])
    eps = p.sbuf([128, 1], F32, "eps"); b_eps = Buf()
    p.op("pool", lambda e: e.memset(eps[:], EPS), writes=[b_eps])
    for (t, b, _) in V65.items:
        p.op("pool", lambda e, t=t: e.memset(t[:], 1.0), writes=[b])
    lamp, b_lp = const([1, 256], F32, bi["lamp"])
    lt = p.sbuf([1, 128], F32, "lt"); b_lt = Buf()
    l2 = p.sbuf([1, 4], F32, "l2"); b_l2 = Buf()
    for j in range(2):
        p.op("dve", lambda e, j=j: e.tensor_tensor(out=lt[:, j * 64:(j + 1) * 64], in0=lamp[:, j * 128:j * 128 + 64],
                                                    in1=lamp[:, j * 128 + 64:j * 128 + 128], op=ALU.mult), reads=[b_lp], writes=[b_lt])
        p.op("act", lambda e, j=j: e.activation(out=lt[:, j * 64:(j + 1) * 64], in_=lt[:, j * 64:(j + 1) * 64], func=AF.Identity,
                                                accum_out=l2[:, j:j + 1]), reads=[b_lt], writes=[b_lt, b_l2])
    p.op("act", lambda e: e.activation(out=l2[:, 0:2], in_=l2[:, 0:2], func=AF.Exp), reads=[b_l2], writes=[b_l2])
    p.op("dve", lambda e: e.tensor_tensor(out=l2[:, 2:3], in0=l2[:, 1:2], in1=l2[:, 0:1], op=ALU.subtract), reads=[b_l2], writes=[b_l2])
    p.op("dve", lambda e: e.tensor_scalar(out=l2[:, 3:4], in0=l2[:, 2:3], scalar1=-lam_init, scalar2=None, op0=ALU.add),
         reads=[b_l2], writes=[b_l2])
    neglam = l2[0:1, 3:4]
    OHs = p.sbuf([33, TVL], F32, "OH"); b_OH = Buf(); c_OH = p.chan()
    tvs = p.sbuf([8, TVL], BF16, "tvs"); b_tvs = Buf(); c_tvs = p.chan()
    strips = p.sbuf([128, NKIND, SW], BF16, "strips"); b_st = Buf(); c_st = [p.chan() for _ in range(NKIND)]
    for v, (k0, k1) in enumerate(((0, 7), (7, 11), (11, 15))):
        p.dma("sp", OHs[:], cst["OH"][v], c_OH, writes=[b_OH])
        for cg in range(3):
            ps, bp, _ = X_.next()
            p.op("pe", lambda e, ps=ps, k0=k0, k1=k1, cg=cg: e.matmul(ps[0:k1 - k0, 0:384], lhsT=tabs[:, k0:k1], rhs=OHs[:, cg * 384:(cg + 1) * 384],
                                                                     start=True, stop=True), reads=[b_tabs, b_OH], writes=[bp])
            p.op("dve", lambda e, ps=ps, k0=k0, k1=k1, cg=cg: e.tensor_copy(out=tvs[0:k1 - k0, cg * 384:(cg + 1) * 384], in_=ps[0:k1 - k0, 0:384]),
                 reads=[bp], writes=[b_tvs])
        tk = p.dma("sp", sc.tvec.ap()[k0:k1, :], tvs[0:k1 - k0, :], c_tvs, reads=[b_tvs])
        for kd in range(k0, k1):
            p._wait("sp", tk)
            p.dma("sp", strips[:, kd, :], bass.AP(sc.tvec, kd * TVL, [[1, 128], [1, SW]]), c_st[kd], writes=[])
    st_toks = [("d", ch, 16) for ch in c_st]
    for tk in st_toks:
        p._wait("pe", tk)
    K_SEL, K_DIFF, K_CAUS, K_SWA, K_WIN = 0, 4, 6, 7, 11
    import os as _os
    if _os.environ.get('B2CUT') == '1':
        p.build()
        return

    kc2 = p.sbuf([128, S + 32], BF16, "kc2"); b_kc2 = Buf(); c_kc2 = [p.chan(), p.chan()]
    w1s = p.sbuf([128, 16, 128], BF16, "w1s"); b_w1 = Buf(); c_w1 = p.chan()
    w2s = p.sbuf([128, 64], BF16, "w2s"); b_w2 = Buf(); c_w2 = p.chan()
    posf = p.sbuf([128, 16], F32, "posf"); b_pf = Buf(); c_pf = p.chan()
    posb = p.sbuf([128, 16], BF16, "posb"); b_pb = Buf()
    b1 = p.sbuf([128, 1], F32, "b1"); b_b1 = Buf()
    xs = p.sbuf([128, 512], F32, "xs"); b_xs = Buf()
    x2 = p.sbuf([128, 512], F32, "x2"); b_x2 = Buf()
    hdn = p.sbuf([128, 512], BF16, "hdn"); b_hdn = Buf()
    kcmpT = p.sbuf([64, 512], BF16, "kcmpT"); b_kcm = Buf()
    vcx = p.sbuf([128, 4, 65], BF16, "vcx"); b_vcx = Buf()
    p.op("pool", lambda e: e.memset(vcx[:], 1.0), writes=[b_vcx])
    p.op("pool", lambda e: e.memset(hdn[:], 0.0), writes=[b_hdn])
    p.op("pool", lambda e: e.memset(kc2[:, S - 8:S + 32], 0.0), writes=[b_kc2])
    for kv, src in ((0, sc.nsa_kcT), (1, sc.nsa_vcT)):
        p.dma("sp", kc2[0:64, 0:S], src.ap()[:, 0:S], c_kc2[0], writes=[b_kc2])
        p.dma("sp", kc2[64:128, 0:S - 1], src.ap()[:, 1:S], c_kc2[1], writes=[b_kc2])
        p.dma("pool", w1s[:], bi["w1"][kv].rearrange("(j p) n -> p j n", p=128), c_w1, writes=[b_w1])
        p.dma("pool", w2s[:], bi["w2"][kv], c_w2, writes=[b_w2])
        p.dma("sp", posf[:], bi["pos"][kv], c_pf, writes=[b_pf])
        p.op("dve", lambda e: e.tensor_copy(out=posb[:], in_=posf[:]), reads=[b_pf], writes=[b_pb])
        ps, bp, _ = X_.next()
        for j in range(16):
            p.op("pe", lambda e, ps=ps, j=j: e.matmul(ps[:, 0:1], lhsT=w1s[:, j, :], rhs=posb[:, j:j + 1], start=(j == 0), stop=(j == 15)),
                 reads=[b_w1, b_pb], writes=[bp])
        p.op("dve", lambda e, ps=ps: e.tensor_copy(out=b1[:], in_=ps[:, 0:1]), reads=[bp], writes=[b_b1])
        ph, bph, _ = X_.next()
        for j in range(16):
            p.op("pe", lambda e, ph=ph, j=j: e.matmul(ph[:, 0:511], lhsT=w1s[:, j, :], rhs=kc2[:, 2 * j:2 * j + 16 * 511:16],
                                                      start=(j == 0), stop=(j == 15)), reads=[b_w1, b_kc2], writes=[bph])
        p.op("act", lambda e, ph=ph: e.activation(out=xs[:, 0:511], in_=ph[:, 0:511], func=AF.Identity, bias=b1[:], scale=1.0),
             reads=[bph, b_b1], writes=[b_xs])
        p.op("dve", lambda e: e.tensor_tensor(out=x2[:, 0:511], in0=xs[:, 0:511], in1=xs[:, 0:511], op=ALU.mult), reads=[b_xs], writes=[b_x2])
        p.op("dve", lambda e: e.tensor_scalar(out=x2[:, 0:511], in0=x2[:, 0:511], scalar1=0.044715, scalar2=1.0, op0=ALU.mult, op1=ALU.add),
             reads=[b_x2], writes=[b_x2])
        p.op("dve", lambda e: e.tensor_tensor(out=x2[:, 0:511], in0=x2[:, 0:511], in1=xs[:, 0:511], op=ALU.mult), reads=[b_x2, b_xs], writes=[b_x2])
        p.op("act", lambda e: e.activation(out=x2[:, 0:511], in_=x2[:, 0:511], func=AF.Sigmoid, scale=1.5957691216057308),
             reads=[b_x2], writes=[b_x2])
        p.op("dve", lambda e: e.tensor_tensor(out=hdn[:, 0:511], in0=x2[:, 0:511], in1=xs[:, 0:511], op=ALU.mult), reads=[b_x2, b_xs], writes=[b_hdn])
        if kv == 0:
            pk, bpk, _ = X_.next()
            p.op("pe", lambda e, pk=pk: e.matmul(pk[0:64, :], lhsT=w2s[:], rhs=hdn[:], start=True, stop=True), reads=[b_w2, b_hdn], writes=[bpk])
            p.op("act", lambda e, pk=pk: e.copy(out=kcmpT[:], in_=pk[0:64, :]), reads=[bpk], writes=[b_kcm])
        else:
            for ct in range(4):
                pk, bpk, _ = X_.next()
                p.op("pe", lambda e, pk=pk, ct=ct: e.matmul(pk[:, 0:64], lhsT=hdn[:, ct * 128:(ct + 1) * 128], rhs=w2s[:], start=True, stop=True),
                     reads=[b_w2, b_hdn], writes=[bpk])
                p.op("act", lambda e, pk=pk, ct=ct: e.copy(out=vcx[:, ct, 0:64], in_=pk[:, 0:64]), reads=[bpk], writes=[b_vcx])

    if _os.environ.get('B2CUT') == '2':
        p.build()
        return
    qm = p.sbuf([96, 4, TB], BF16, "qm"); b_qm = Buf(); c_qm = p.chan()
    qs = p.sbuf([64, 4, TB], BF16, "qs"); b_qs = Buf(); c_qs = p.chan()
    qn_ = p.sbuf([64, 4, TB], BF16, "qn"); b_qn = Buf(); c_qn = p.chan()
    qd = p.sbuf([64, 4, TB], BF16, "qd"); b_qd = Buf(); c_qd = p.chan()
    gsg = p.sbuf([12, TB], F32, "gsg"); b_gsg = Buf(); c_gsg = p.chan()
    rd = p.sbuf([65, TB], F32, "rd"); b_rd = Buf()
    osb = Ring(p, 2, [128, TB], F32, "osb", chan=False)
    accd = Ring(p, 2, [128, TB], F32, "accd", chan=False)
    ost = Ring(p, 3, [128, TB], BF16, "ost")
    p_acc = [(p.sbuf([64, TB], F32, "acc"), Buf()) for _ in range(4)]
    tmpn = p.sbuf([64, TB], F32, "tmpn"); b_tmpn = Buf()
    d1 = p.sbuf([128, TB], F32, "d1"); b_d1 = Buf()
    d2 = p.sbuf([128, TB], F32, "d2"); b_d2 = Buf()
    dsq = p.sbuf([128, TB], F32, "dsq"); b_dsq = Buf()
    drs = p.sbuf([128, TB], F32, "drs"); b_drs = Buf()
    impa = p.sbuf([128, 4, 128], F32, "impa"); b_impa = Buf()
    rdi = p.sbuf([128, 8], F32, "rdi"); b_rdi = Buf()
    top = p.sbuf([128, 16], F32, "top"); b_top = Buf()
    wk = p.sbuf([128, 128], F32, "wk"); b_wk = Buf()
    mq = p.sbuf([128, 128], F32, "mq"); b_mq = Buf()
    MT = p.sbuf([128, TB], BF16, "MT"); b_MT = Buf()
    SGC = True

    pend = []

    def defer(fn):
        pend.append(fn)

    def flush():
        while pend:
            pend.pop(0)()

    def mm(out, lhsT, rhs, start, stop):
        return lambda e: e.matmul(out, lhsT=lhsT, rhs=rhs, start=start, stop=stop, skip_group_check=SGC)

    def attn(qT, b_q, tiles, scale, ops, bo, M, den=None, after=None, n=None):
        if n is None:
            tiles = list(tiles)
            n = len(tiles)

        def stage1(tl):
            q0 = tl["q0"]
            sp_, bs, _ = S_.next()
            ex = tl.get("ex", [])
            p.op("pe", mm(sp_[:, q0:], tl["kT"], qT[:, q0:], True, not ex), reads=[b_q] + tl["kb"], writes=[bs])
            for j, (l_, r_, bufs) in enumerate(ex):
                p.op("pe", mm(sp_[:, q0:], l_, r_, False, j == len(ex) - 1), reads=bufs, writes=[bs])
            P, bP, _ = P_.next()
            bias = tl.get("bias", zb[:, 0:1])
            p.op("act", lambda e, P=P, sp_=sp_, q0=q0, bias=bias: e.activation(out=P[:, q0:], in_=sp_[:, q0:], func=AF.Exp, bias=bias, scale=scale),
                 reads=[bs, b_zb, b_cb], writes=[bP])
            return P, bP

        def stage2(i, tl, P, bP):
            q0 = tl["q0"]
            p.op("pe", mm(ops[0:M, q0:], tl["v"], P[:, q0:], i == 0, i == n - 1), reads=[bP] + tl["vb"], writes=[bo])
            if den is not None:
                acc_, bacc_ = den
                if i == 0:
                    p.op("pool", lambda e: e.tensor_copy(out=acc_[:], in_=P[:]), reads=[bP], writes=[bacc_])
                else:
                    p.op("pool", lambda e: e.tensor_tensor(out=acc_[:, q0:], in0=acc_[:, q0:], in1=P[:, q0:], op=ALU.add), reads=[bP, bacc_], writes=[bacc_])
            if after is not None:
                after(P, bP, tl, i, n)

        q_ = []
        for i, tl in enumerate(tiles):
            P, bP = stage1(tl)
            if i == 0:
                flush()
            q_.append((i, tl, P, bP))
            if len(q_) > 2:
                stage2(*q_.pop(0))
        while q_:
            stage2(*q_.pop(0))

    def load_kv(kT_src, dk, v_src, ring, dv):
        kt, bk, ck = KT_.next()
        p.dma("sp", kt[0:dk, :], kT_src, ck, writes=[bk])
        vt, bv, cv = ring.next()
        p.dma("sp", vt[:, :, 0:dv], v_src, cv, writes=[bv])
        return kt, bk, vt, bv

    def normalize(ops, bo, dv, den_ps, bden, dp, sink=None, mulneg=False, dest=None, bdest=None):
        p.op("dve", lambda e: e.tensor_scalar(out=rd[dp:dp + 1, :], in0=den_ps[dp:dp + 1, :], scalar1=(sink if sink is not None else 0.0),
                                              scalar2=1e-30, op0=ALU.add, op1=ALU.max), reads=[bden, b_sk], writes=[b_rd])
        p.op("dve", lambda e: e.reciprocal(out=rd[dp:dp + 1, :], in_=rd[dp:dp + 1, :]), reads=[b_rd], writes=[b_rd])
        if mulneg:
            p.op("dve", lambda e: e.tensor_scalar(out=rd[dp:dp + 1, :], in0=rd[dp:dp + 1, :], scalar1=neglam, scalar2=None, op0=ALU.mult),
                 reads=[b_rd, b_l2], writes=[b_rd])
        bc, bbc, _ = X_.next()
        p.op("pe", lambda e: e.matmul(bc[0:dv, :], lhsT=ones_f[dp:dp + 1, 0:dv], rhs=rd[dp:dp + 1, :], start=True, stop=True),
             reads=[b_ones, b_rd], writes=[bbc])
        o, bo_, _ = osb.next()
        p.op("act", lambda e: e.copy(out=o[0:dv, :], in_=ops[0:dv, :]), reads=[bo], writes=[bo_])
        p.op("dve", lambda e: e.tensor_tensor(out=dest[0:dv, :], in0=o[0:dv, :], in1=bc[0:dv, :], op=ALU.mult), reads=[bo_, bbc], writes=[bdest])

    import os as _os
    gsg_cur = (gsg, b_gsg)
    for qb in range(int(_os.environ.get('NQB', '16'))):
        flush()
        p.dma("sp", qm[:], sc.mla_qT.ap()[qb].rearrange("h d t -> d h t"), c_qm, writes=[b_qm])
        p.dma("sp", qs[:], sc.swa_qT.ap()[qb].rearrange("(h d) t -> d h t", d=64), c_qs, writes=[b_qs])
        p.dma("sp", qn_[:], sc.nsa_qT.ap()[qb].rearrange("(h d) t -> d h t", d=64), c_qn, writes=[b_qn])
        p.dma("sp", qd[:], sc.diff_qT.ap()[qb].rearrange("(h d) t -> d h t", d=64), c_qd, writes=[b_qd])
        p.dma("sp", gsg[:], sc.nsa_gsig.ap()[qb], c_gsg, writes=[b_gsg])

        def causal_tiles(kT_of, dk, v_of, ring, dv, strip_kind, far_bias, near_prev):
            nxt = load_kv(kT_of(0), dk, v_of(0), ring, dv)
            for kb in range(qb + 1):
                kt, bk, vt, bv = nxt
                if kb < qb:
                    nxt = load_kv(kT_of(kb + 1), dk, v_of(kb + 1), ring, dv)
                for t4 in range(4):
                    tl = dict(kT=kt[0:dk, t4 * 128:(t4 + 1) * 128], kb=[bk], v=vt[:, t4, :], vb=[bv], q0=0)
                    if kb == qb:
                        tl["q0"] = 128 * t4
                        tl["ex"] = [(Jb[:], strips[:, strip_kind, 0:512 - 128 * t4], [b_J])]
                    elif near_prev and kb == qb - 1 and t4 == 3:
                        tl["ex"] = [(Jb[:], strips[:, strip_kind, 128:640], [b_J])]
                    elif far_bias is not None:
                        tl["bias"] = far_bias
                    yield tl

        for h in range(4):
            ops, bo, _ = O_.next()
            tiles = causal_tiles(lambda kb: sc.mla_kT.ap()[kb, h], 96, lambda kb: sc.mla_v.ap()[kb][:, :, h * 64:(h + 1) * 64], V65, 64, K_CAUS, None, False)
            attn(qm[:, h, :], b_qm, tiles, 96 ** -0.5, ops, bo, 65, n=4 * (qb + 1))

            def epi(ops=ops, bo=bo, h=h, qb=qb):
                yt, byt, cyt = ost.next()
                normalize(ops, bo, 64, ops, bo, 64, dest=yt, bdest=byt)
                p.dma("sp", yT_out.ap()[qb, 0, h * 64:(h + 1) * 64, :], yt[0:64, :], cyt, reads=[byt], final=True)
            defer(epi)
        kbs = ([qb - 1] if qb > 0 else []) + [qb]
        kvl = {kb: load_kv(sc.swa_kT.ap()[kb], 64, sc.swa_v.ap()[kb], V65, 64) for kb in kbs}
        for g in range(4):
            tiles = []
            if qb > 0:
                kt, bk, vt, bv = kvl[qb - 1]
                tiles.append(dict(kT=kt[0:64, 384:512], kb=[bk], v=vt[:, 3, :], vb=[bv], q0=0, ex=[(Jb[:], strips[:, K_SWA + g, 128:640], [b_J])]))
            kt, bk, vt, bv = kvl[qb]
            for t4 in range(4):
                tiles.append(dict(kT=kt[0:64, t4 * 128:(t4 + 1) * 128], kb=[bk], v=vt[:, t4, :], vb=[bv], q0=128 * t4,
                                  ex=[(Jb[:], strips[:, K_SWA + g, 0:512 - 128 * t4], [b_J])]))
            if qb == 0:
                tiles[0]["q0"] = 0
            ops, bo, _ = O_.next()
            attn(qs[:, g, :], b_qs, tiles, 1.0, ops, bo, 65)

            def epi(ops=ops, bo=bo, g=g, qb=qb):
                yt, byt, cyt = ost.next()
                normalize(ops, bo, 64, ops, bo, 64, sink=sinke[64:65, g:g + 1], dest=yt, bdest=byt)
                p.dma("sp", yT_out.ap()[qb, 1, g * 64:(g + 1) * 64, :], yt[0:64, :], cyt, reads=[byt], final=True)
            defer(epi)
        for h in range(2):
            for m_ in range(2):
                ops, bo, _ = O_.next()
                dacc, bdacc, _ = accd.next()
                tiles = causal_tiles(lambda kb: sc.diff_kT.ap()[kb, (2 * h + m_) * 64:(2 * h + m_ + 1) * 64, :], 64,
                                     lambda kb: sc.diff_v.ap()[kb][:, :, h * 128:(h + 1) * 128], V128, 128, K_DIFF + h, cb[:, K_DIFF + h:K_DIFF + h + 1], True)
                attn(qd[:, 2 * h + m_, :], b_qd, tiles, 1.0, ops, bo, 128, den=(dacc, bdacc), n=4 * (qb + 1))

                def epi(ops=ops, bo=bo, dacc=dacc, bdacc=bdacc, m_=m_, h=h, qb=qb):
                    dps, bd, _ = X_.next()
                    p.op("pe", lambda e: e.matmul(dps[0:1, :], lhsT=ones_f[:, 0:1], rhs=dacc[:], start=True, stop=True), reads=[b_ones, bdacc], writes=[bd])
                    normalize(ops, bo, 128, dps, bd, 0, mulneg=(m_ == 1), dest=(d1 if m_ == 0 else d2), bdest=(b_d1 if m_ == 0 else b_d2))
                    if m_ == 0:
                        return
                    p.op("pool", lambda e: e.tensor_tensor(out=d1[:], in0=d1[:], in1=d2[:], op=ALU.add), reads=[b_d1, b_d2], writes=[b_d1])
                    p.op("act", lambda e: e.activation(out=dsq[:], in_=d1[:], func=AF.Square), reads=[b_d1], writes=[b_dsq])
                    ss, bss, _ = X_.next()
                    p.op("pe", lambda e, ss=ss: e.matmul(ss[:], lhsT=ones_f[:], rhs=dsq[:], start=True, stop=True), reads=[b_ones, b_dsq], writes=[bss])
                    p.op("act", lambda e, ss=ss: e.activation(out=drs[:], in_=ss[:], func=AF.Sqrt, bias=eps[:], scale=1.0 / 128), reads=[bss, b_eps], writes=[b_drs])
                    p.op("dve", lambda e: e.reciprocal(out=drs[:], in_=drs[:]), reads=[b_drs], writes=[b_drs])
                    yt, byt, cyt = ost.next()
                    p.op("dve", lambda e, yt=yt: e.scalar_tensor_tensor(out=yt[:], in0=d1[:], scalar=subln[:, 0:1], in1=drs[:], op0=ALU.mult, op1=ALU.mult),
                         reads=[b_d1, b_sub, b_drs], writes=[byt])
                    p.dma("sp", yT_out.ap()[qb, 3, h * 128:(h + 1) * 128, :], yt[:], cyt, reads=[byt], final=True)
                defer(epi)
        flush()
        p.op("pool", lambda e: e.memset(impa[:], 0.0), writes=[b_impa])
        cts = [ct for ct in range(4) if qb - 4 * ct >= 0]
        oc_keep = []
        for g in range(4):
            tiles = []
            for ct in cts:
                dl = qb - 4 * ct
                tl = dict(kT=kcmpT[:, ct * 128:(ct + 1) * 128], kb=[b_kcm], v=vcx[:, ct, :], vb=[b_vcx], q0=0, ct=ct)
                if dl <= 4:
                    tl["ex"] = [(identb[:], maskC[:, dl, :], [b_idb, b_mC])]
                tiles.append(tl)
            flush()
            ops, bo, _ = O_.next()
            ips = [X_.next(), X_.next()]
            for ip, bip, _ in ips:
                p.op("dve", lambda e, ip=ip: e.memset(ip[:], 0.0), writes=[bip])

            def after(P, bP, tl, i, n, ips=ips):
                for t4 in range(4):
                    ip, bip, _ = ips[t4 // 2]
                    p.op("pe", mm(ip[:, (t4 % 2) * 129:(t4 % 2) * 129 + 129], P[:, t4 * 128:(t4 + 1) * 128], ovx[:, tl["ct"], :], False, i == n - 1),
                         reads=[bP, b_ov], writes=[bip])
            attn(qn_[:, g, :], b_qn, tiles, 1.0, ops, bo, 65, after=after)
            for t4 in range(4):
                ip, bip, _ = ips[t4 // 2]
                o_ = (t4 % 2) * 129
                p.op("dve", lambda e, ip=ip, o_=o_, t4=t4: e.tensor_scalar(out=rdi[:, t4:t4 + 1], in0=ip[:, o_ + 128:o_ + 129], scalar1=1e-30, scalar2=None, op0=ALU.max),
                     reads=[bip], writes=[b_rdi])
                p.op("dve", lambda e, t4=t4: e.reciprocal(out=rdi[:, t4:t4 + 1], in_=rdi[:, t4:t4 + 1]), reads=[b_rdi], writes=[b_rdi])
                p.op("dve", lambda e, ip=ip, o_=o_, t4=t4: e.scalar_tensor_tensor(out=impa[:, t4, :], in0=ip[:, o_:o_ + 128], scalar=rdi[:, t4:t4 + 1],
                                                                                  in1=impa[:, t4, :], op0=ALU.mult, op1=ALU.add),
                     reads=[bip, b_rdi, b_impa], writes=[b_impa])
            oc_keep.append((ops, bo))
            gb_, bgb, _ = X_.next()
            p.op("pe", lambda e, gb_=gb_, g=g: e.matmul(gb_[0:64, :], lhsT=Sel[:, (g * 3) * 64:(g * 3 + 1) * 64], rhs=gsg[:], start=True, stop=True),
                 reads=[b_Sel, b_gsg], writes=[bgb])
            normalize(ops, bo, 64, ops, bo, 64, dest=tmpn, bdest=b_tmpn)
            accg = p_acc[g]
            p.op("dve", lambda e, gb_=gb_, accg=accg: e.tensor_tensor(out=accg[0][:], in0=tmpn[:], in1=gb_[0:64, :], op=ALU.mult),
                 reads=[b_tmpn, bgb], writes=[accg[1]])
        for t4 in range(4):
            T = 4 * qb + t4
            c0 = 126 - 2 * T
            p.op("dve", lambda e, t4=t4, c0=c0: e.tensor_tensor(out=impa[:, t4, :], in0=impa[:, t4, :], in1=SA[:, c0:c0 + 128], op=ALU.mult),
                 reads=[b_impa, b_SA], writes=[b_impa])
            p.op("dve", lambda e, t4=t4, c0=c0: e.tensor_tensor(out=impa[:, t4, :], in0=impa[:, t4, :], in1=SB[:, c0:c0 + 128], op=ALU.add),
                 reads=[b_impa, b_SB], writes=[b_impa])
            p.op("dve", lambda e, t4=t4: e.memset(impa[:, t4, 0:1], 1e4), reads=[], writes=[b_impa])
            p.op("dve", lambda e, t4=t4: e.max(out=top[:, 0:8], in_=impa[:, t4, :]), reads=[b_impa], writes=[b_top])
            p.op("dve", lambda e, t4=t4: e.match_replace(out=wk[:], in_to_replace=top[:, 0:8], in_values=impa[:, t4, :], imm_value=-1e30),
                 reads=[b_impa, b_top], writes=[b_wk])
            p.op("dve", lambda e: e.max(out=top[:, 8:16], in_=wk[:]), reads=[b_wk], writes=[b_top])
            p.op("dve", lambda e, t4=t4: e.tensor_scalar(out=mq[:], in0=impa[:, t4, :], scalar1=top[:, 15:16], scalar2=1.0, op0=ALU.is_ge, op1=ALU.subtract),
                 reads=[b_impa, b_top], writes=[b_mq])
            p.op("dve", lambda e: e.tensor_scalar(out=mq[:], in0=mq[:], scalar1=-NEGM, scalar2=None, op0=ALU.mult), reads=[b_mq], writes=[b_mq])
            tp, btp, _ = X_.next()
            p.op("pe", lambda e, tp=tp: e.transpose(tp[:, 0:128], mq[:], identf[:]), reads=[b_mq, b_idf], writes=[btp])
            p.op("act", lambda e, tp=tp, t4=t4: e.copy(out=MT[:, t4 * 128:(t4 + 1) * 128], in_=tp[:, 0:128]), reads=[btp], writes=[b_MT])
        kvs = [load_kv(sc.nsa_ksT.ap()[kb], 64, sc.nsa_vs.ap()[kb], V65, 64) for kb in range(0)]
        for g in range(4):
            accg = p_acc[g]
            for br in (1, 2):
                tiles = []
                nt = None
                if br == 1:
                    def sel_tiles(g=g):
                        nxt = load_kv(sc.nsa_ksT.ap()[0], 64, sc.nsa_vs.ap()[0], V65, 64)
                        for kb in range(qb + 1):
                            kt, bk, vt, bv = nxt
                            if kb < qb:
                                nxt = load_kv(sc.nsa_ksT.ap()[kb + 1], 64, sc.nsa_vs.ap()[kb + 1], V65, 64)
                            for t4 in range(4):
                                KT = kb * 4 + t4
                                tl = dict(kT=kt[0:64, t4 * 128:(t4 + 1) * 128], kb=[bk], v=vt[:, t4, :], vb=[bv], q0=0)
                                q0 = 128 * t4 if kb == qb else 0
                                tl["q0"] = q0
                                tl["ex"] = [(E[:, KT * 128:(KT + 1) * 128], MT[:, q0:], [b_E, b_MT])]
                                if kb == qb:
                                    tl["ex"].append((Jb[:], strips[:, K_SEL + g, 0:512 - q0], [b_J]))
                                elif kb == qb - 1 and t4 == 3:
                                    tl["ex"].append((Jb[:], strips[:, K_SEL + g, 128:640], [b_J]))
                                else:
                                    tl["bias"] = cb[:, K_SEL + g:K_SEL + g + 1]
                                yield tl
                    tiles = sel_tiles()
                    nt = 4 * (qb + 1)
                else:
                    if qb > 0:
                        kt, bk, vt, bv = load_kv(sc.nsa_kwT.ap()[qb - 1], 64, sc.nsa_vw.ap()[qb - 1], V65, 64)
                        for t4 in range(4):
                            rel = 512 - 128 * t4
                            tiles.append(dict(kT=kt[0:64, t4 * 128:(t4 + 1) * 128], kb=[bk], v=vt[:, t4, :], vb=[bv], q0=0,
                                              ex=[(Jb[:], strips[:, K_WIN + g, rel:rel + 512], [b_J])]))
                    kt, bk, vt, bv = load_kv(sc.nsa_kwT.ap()[qb], 64, sc.nsa_vw.ap()[qb], V65, 64)
                    for t4 in range(4):
                        tiles.append(dict(kT=kt[0:64, t4 * 128:(t4 + 1) * 128], kb=[bk], v=vt[:, t4, :], vb=[bv], q0=128 * t4,
                                          ex=[(Jb[:], strips[:, K_WIN + g, 0:512 - 128 * t4], [b_J])]))
                ops, bo, _ = O_.next()
                attn(qn_[:, g, :], b_qn, tiles, 1.0, ops, bo, 65, n=nt)

                def epi(ops=ops, bo=bo, g=g, br=br, accg=accg, qb=qb):
                    gb_, bgb, _ = X_.next()
                    p.op("pe", lambda e: e.matmul(gb_[0:64, :], lhsT=Sel[:, (g * 3 + br) * 64:(g * 3 + br + 1) * 64], rhs=gsg_cur[0][:], start=True, stop=True),
                         reads=[b_Sel, gsg_cur[1]], writes=[bgb])
                    normalize(ops, bo, 64, ops, bo, 64, dest=tmpn, bdest=b_tmpn)
                    p.op("dve", lambda e: e.tensor_tensor(out=tmpn[:], in0=tmpn[:], in1=gb_[0:64, :], op=ALU.mult), reads=[b_tmpn, bgb], writes=[b_tmpn])
                    p.op("pool", lambda e: e.tensor_tensor(out=accg[0][:], in0=accg[0][:], in1=tmpn[:], op=ALU.add), reads=[accg[1], b_tmpn], writes=[accg[1]])
                    if br == 2:
                        yt, byt, cyt = ost.next()
                        p.op("act", lambda e: e.copy(out=yt[0:64, :], in_=accg[0][:]), reads=[accg[1]], writes=[byt])
                        p.dma("sp", yT_out.ap()[qb, 2, g * 64:(g + 1) * 64, :], yt[0:64, :], cyt, reads=[byt], final=True)
                defer(epi)
    flush()
    p.build()


def build_launch_B(l):
    nc = bass.Bass("TRN2", target_bir_lowering=False)
    E_ = lambda n, s, dt=F32: nc.dram_tensor(n, list(s), dt, kind="ExternalInput")
    uT_in = E_("uT_in", [16, 128, 8, TB], BF16)
    shp = dict(win_fm=[D, FM_COLS], win_tm=[D, TM_COLS], wuq=[256, 384], wuqp=[256, 384], wukvk=[128, 256], wukvv=[128, 256],
               qnT=[128, 2], kvn=[128, 1])
    a = {k: E_(k, v).ap() for k, v in shp.items()}
    ropeC = E_("ropeC", [16, 96, TB]).ap()
    ropeS = E_("ropeS", [16, 96, TB]).ap()
    cshp = dict(identb=([128, 128], BF16), identf=([128, 128], F32), Jb=([128, 128], BF16), E=([128, S], BF16), maskC=([5, 128, 512], BF16),
                OH=([3, 33, TVL], F32), SA=([128, 256], F32), SB=([128, 256], F32), ovx=([128, 4, 129], BF16), Sel=([12, 768], F32))
    cst = {k: E_("k_" + k, v[0], v[1]).ap() for k, v in cshp.items()}
    bshp = dict(tabs=[33, NKIND], sinks=[4], lamp=[1, 256], subln=[128, 1], w1=[2, 2048, 128], w2=[2, 128, 64], pos=[2, 128, 16])
    bi = {k: E_("b_" + k, v).ap() for k, v in bshp.items()}
    yT_out = nc.dram_tensor("yT_out", [16, 4, 256, TB], BF16, kind="ExternalOutput")
    sc = Scratch(nc, "sc_")
    import os as _os
    if _os.environ.get('SKIPB1') != '1':
      phase_B1(nc, "B1", sc, uT_in, a["win_fm"], a["win_tm"], a["wuq"], a["wuqp"], a["wukvk"], a["wukvv"], a["qnT"], a["kvn"], ropeC, ropeS)
    lam_init = 0.8 - 0.6 * math.exp(-0.3 * l)
    phase_B2(nc, "B2", sc, yT_out, cst, bi, lam_init)
    return nc


_CACHE = {}


def _get(kind, arg=None):
    key = (kind, arg)
    if key not in _CACHE:
        if kind == "A":
            _CACHE[key] = build_launch_A()
        elif kind == "B":
            _CACHE[key] = build_launch_B(arg)
        else:
            _CACHE[key] = build_launch_C(arg)
    return _CACHE[key]


def _bf(x):
    return np.ascontiguousarray(x)


def kernel_unfused(**inp):
    x = np.asarray(inp["x"], np.float32)
    cores = list(range(8))
    h = [np.ascontiguousarray(x[c // 2, (c % 2) * 4096:(c % 2 + 1) * 4096]) for c in cores]
    C_, S_g = rope_tables()
    consts = b2_consts()
    for l in range(2):
        maps = []
        for c in cores:
            maps.append(dict(h_in=h[c], gT0=gT_layout(inp["norm_g"][l, 0]), gT1=gT_layout(inp["norm_g"][l, 1]),
                             wg=np.asarray(inp["ffn_w_gate"][l, 0], np.float32), wu=np.asarray(inp["ffn_w_up"][l, 0], np.float32),
                             wd=np.asarray(inp["ffn_w_down"][l, 0], np.float32)))
        res = run_bass_kernel_spmd(_get("A"), maps, core_ids=cores).results
        h = [np.asarray(r["h_out"]) for r in res]
        uT = [np.asarray(r["uT_out"]) for r in res]
        maps = []
        for c in cores:
            s_, r_ = c // 2, c % 2
            m = dict(uT_in=np.ascontiguousarray(np.concatenate([uT[2 * s_], uT[2 * s_ + 1]], 0)), ropeC=C_, ropeS=S_g)
            m.update(prep_B_weights(inp, l, r_))
            m.update({'k_' + k: v for k, v in consts.items()})
            m.update({'b_' + k: v for k, v in prep_B2_inputs(inp, l, r_).items()})
            maps.append(m)
        res = run_bass_kernel_spmd(_get("B", l), maps, core_ids=cores).results
        yT = [np.asarray(r["yT_out"]) for r in res]
        maps = []
        for c in cores:
            s_, r_ = c // 2, c % 2
            y_all = np.concatenate([yT[2 * s_][8 * r_:8 * r_ + 8], yT[2 * s_ + 1][8 * r_:8 * r_ + 8]], 2)
            y_all = y_all.reshape(8, 4, 4, 128, TB).transpose(0, 3, 1, 2, 4).reshape(8, 128, 16, TB)
            m = dict(h_in=h[c], uT_in=uT[c], yT_in=np.ascontiguousarray(y_all), gT2=gT_layout(inp["norm_g"][l, 2]),
                     wgate=np.asarray(inp["w_gate"][l], np.float32), wbr=np.asarray(inp["w_branch"][l], np.float32),
                     wo=np.asarray(inp["w_o"][l], np.float32), wg=np.asarray(inp["ffn_w_gate"][l, 1], np.float32),
                     wu=np.asarray(inp["ffn_w_up"][l, 1], np.float32), wd=np.asarray(inp["ffn_w_down"][l, 1], np.float32))
            if l == 1:
                m["gfin"] = np.asarray(inp["final_g"], np.float32)
            maps.append(m)
        res = run_bass_kernel_spmd(_get("C", l == 1), maps, core_ids=cores).results
        h = [np.asarray(r["h_out"]) for r in res]
    out = np.zeros((NB, S, D), np.float32)
    for c in cores:
        out[c // 2, (c % 2) * 4096:(c % 2 + 1) * 4096] = h[c]
    return out


PAIRS = [[0, 1], [2, 3], [4, 5], [6, 7]]
A_SHP = dict(gT0=[128, 8], gT1=[128, 8], gT2=[128, 8], wg1=[D, DFF], wu1=[D, DFF], wd1=[DFF, D], wg2=[D, DFF], wu2=[D, DFF], wd2=[DFF, D],
             wgate=[4, D, D], wbr=[4, 512, D], wo=[D, D])
B_SHP = dict(win_fm=[D, FM_COLS], win_tm=[D, TM_COLS], wuq=[256, 384], wuqp=[256, 384], wukvk=[128, 256], wukvv=[128, 256],
             qnT=[128, 2], kvn=[128, 1])
B2_SHP = dict(tabs=[33, NKIND], sinks=[4], lamp=[1, 256], subln=[128, 1], w1=[2, 2048, 128], w2=[2, 128, 64], pos=[2, 128, 16])
C_SHP = dict(identb=([128, 128], BF16), identf=([128, 128], F32), Jb=([128, 128], BF16), E=([128, S], BF16), maskC=([5, 128, 512], BF16),
             OH=([3, 33, TVL], F32), SA=([128, 256], F32), SB=([128, 256], F32), ovx=([128, 4, 129], BF16), Sel=([12, 768], F32))


def build_fused():
    nc = bass.Bass("TRN2", target_bir_lowering=False)
    E_ = lambda n, s, dt=F32: nc.dram_tensor(n, list(s), dt, kind="ExternalInput")
    x_in = E_("x_in", [4096, D]).ap()
    par = E_("par", [128, TB]).ap()
    gfin = E_("gfin", [D]).ap()
    ropeC = E_("ropeC", [16, 96, TB]).ap()
    ropeS = E_("ropeS", [16, 96, TB]).ap()
    cst = {k: E_("k_" + k, v[0], v[1]).ap() for k, v in C_SHP.items()}
    out = nc.dram_tensor("out", [4096, D], F32, kind="ExternalOutput").ap()
    hA = dram(nc, "hA", [4096, D], F32).ap()
    hC = dram(nc, "hC", [4096, D], F32).ap()
    uT_mine = dram(nc, "uT_mine", [8, 128, 8, TB], BF16)
    uT_full = dram(nc, "uT_full", [4, 2, 2, 128, 8, TB], BF16)
    yT_mine = dram(nc, "yT_mine", [16, 4, 256, TB], BF16)
    yT_full = dram(nc, "yT_full", [8, 2, 2, 4, 256, TB], BF16)
    sc = Scratch(nc, "sc_")
    h_src = x_in
    import os as _os
    fcut = int(_os.environ.get("FCUT", "99"))
    nph = [0]

    def go():
        nph[0] += 1
        return nph[0] <= fcut
    for l in range(2):
        a = {k: E_("l%d_%s" % (l, k), v).ap() for k, v in A_SHP.items()}
        b = {k: E_("l%d_%s" % (l, k), v).ap() for k, v in B_SHP.items()}
        bi = {k: E_("l%d_b_%s" % (l, k), v).ap() for k, v in B2_SHP.items()}
        t = "L%d" % l
        if go():
            phase_A(nc, t + "A", h_src, hA, uT_mine, a["gT0"], a["gT1"], a["wg1"], a["wu1"], a["wd1"], 8)
        p = Prog(nc, t + "X1")
        if _os.environ.get("NOCC") != "1" and go():
          for j in range(4):
            p.collective("AllGather", PAIRS, uT_mine.ap()[2 * j:2 * j + 2].rearrange("b p c t -> (b p) (c t)"),
                         uT_full.ap()[j].rearrange("r b p c t -> (r b p) (c t)"))
        p.build()
        if go():
          phase_B1(nc, t + "B1", sc, (lambda bb: uT_full.ap()[(bb % 8) // 2, bb // 8, (bb % 8) % 2]), b["win_fm"], b["win_tm"], b["wuq"], b["wuqp"], b["wukvk"], b["wukvv"], b["qnT"], b["kvn"], ropeC, ropeS)
        if go():
            phase_B2(nc, t + "B2", sc, yT_mine, cst, bi, 0.8 - 0.6 * math.exp(-0.3 * l))
        p = Prog(nc, t + "X2")
        if _os.environ.get("NOCC") != "1" and go():
          for j in range(8):
            p.collective("AllGather", PAIRS, yT_mine.ap()[2 * j:2 * j + 2].rearrange("b i f t -> (b i f) t"),
                         yT_full.ap()[j].rearrange("r b i f t -> (r b i f) t"))
        p.build()
        if go():
          phase_C(nc, t + "C", hA, out if l == 1 else hC, uT_mine, None, a["gT2"], a["wgate"], a["wbr"], a["wo"], a["wg2"], a["wu2"], a["wd2"], 8,
                gfin_d=gfin if l == 1 else None, ygath=yT_full, par_d=par)
        h_src = hC
    return nc


def kernel(**inp):
    x = np.asarray(inp["x"], np.float32)
    cores = list(range(8))
    C_, S_g = rope_tables()
    consts = b2_consts()
    f32 = lambda a: np.ascontiguousarray(np.asarray(a, np.float32))
    shared = dict(gfin=f32(inp["final_g"]), ropeC=C_, ropeS=S_g)
    shared.update({"k_" + k: v for k, v in consts.items()})
    for l in range(2):
        shared.update({"l%d_gT%d" % (l, i): gT_layout(inp["norm_g"][l, i]) for i in range(3)})
        for j, nm in ((0, "1"), (1, "2")):
            shared["l%d_wg%s" % (l, nm)] = f32(inp["ffn_w_gate"][l, j])
            shared["l%d_wu%s" % (l, nm)] = f32(inp["ffn_w_up"][l, j])
            shared["l%d_wd%s" % (l, nm)] = f32(inp["ffn_w_down"][l, j])
        shared["l%d_wgate" % l] = f32(inp["w_gate"][l])
        shared["l%d_wbr" % l] = f32(inp["w_branch"][l])
        shared["l%d_wo" % l] = f32(inp["w_o"][l])
    perpar = []
    for r_ in range(2):
        d = {}
        for l in range(2):
            d.update({"l%d_%s" % (l, k): v for k, v in prep_B_weights(inp, l, r_).items()})
            d.update({"l%d_b_%s" % (l, k): v for k, v in prep_B2_inputs(inp, l, r_).items()})
        d["par"] = np.full((128, TB), float(r_), np.float32)
        perpar.append(d)
    maps = []
    for c in cores:
        m = dict(shared)
        m.update(perpar[c % 2])
        m["x_in"] = np.ascontiguousarray(x[c // 2, (c % 2) * 4096:(c % 2 + 1) * 4096])
        maps.append(m)
    if "F" not in _CACHE:
        _CACHE["F"] = build_fused()
    res = run_bass_kernel_spmd(_CACHE["F"], maps, core_ids=cores).results
    out = np.zeros((NB, S, D), np.float32)
    for c in cores:
        out[c // 2, (c % 2) * 4096:(c % 2 + 1) * 4096] = np.asarray(res[c]["out"])
    return out
```
